# Optimizing a Trainium2 kernel written in Bass

```python
import jax, jax.numpy as jnp
from jax import lax
import numpy as np

D_MODEL = 2048
BATCH = 4
SEQ = 2048
DEPTH = 2

GRID_W = 64
CTX_LEN = 256
CONV_W = 1024
CONV_K = 3
MLA_HEADS = 8
MLA_NOPE = 128
MLA_ROPE = 64
MLA_V = 128
Q_LORA = 512
KV_LORA = 256
MLA_W = MLA_HEADS * MLA_V
MLA_SCALE = (MLA_NOPE + MLA_ROPE) ** -0.5
NA_HEADS = 16
NA_HD = 64
NA_W = NA_HEADS * NA_HD
NA_KH = 8
NA_KW = 16
NA_SCALE = NA_HD ** -0.5
FNET_GROUPS = 4
FNET_GW = 256
FNET_W = FNET_GROUPS * FNET_GW

N_BRANCH = 4
Q_BLOCK = 128
ROPE_THETA = 10000.0
EPS = 1e-6
IN_WIDTHS = (KV_LORA + MLA_ROPE, NA_W, NA_W, Q_LORA, NA_W, CONV_W, CONV_W, CONV_W, FNET_W,
             CONV_W, MLA_W, NA_W, FNET_W, N_BRANCH * D_MODEL)
KV_COLS = KV_LORA + MLA_ROPE + 2 * NA_W
N_IN = sum(IN_WIDTHS)

kernel_name = 'hybrid_parallel_mixer_diffusion_block'


def _split(h, widths):
    offs = np.cumsum(widths)[:-1].tolist()
    return jnp.split(h, offs, axis=-1)


def _rmsnorm(x, g):
    xf = x.astype(jnp.float32)
    y = xf * lax.rsqrt(jnp.mean(xf * xf, axis=-1, keepdims=True) + EPS)
    return (y * g.astype(jnp.float32)).astype(x.dtype)


def _heads(t, h, d):
    return t.reshape(t.shape[0], t.shape[1], h, d)


def _axial_rope(x):
    n = x.shape[1]
    nf = MLA_ROPE // 4
    t = jnp.arange(n, dtype=jnp.int32)
    pos = jnp.stack([t // GRID_W, t % GRID_W], axis=-1).astype(jnp.float32)
    inv = ROPE_THETA ** (-jnp.arange(nf, dtype=jnp.float32) / nf)
    ang = pos[:, :, None] * inv
    cos = jnp.cos(ang)[None, :, None]
    sin = jnp.sin(ang)[None, :, None]
    xr = x.astype(jnp.float32).reshape(x.shape[:-1] + (2, 2, nf))
    a, b = xr[..., 0, :], xr[..., 1, :]
    out = jnp.stack([a * cos - b * sin, a * sin + b * cos], axis=-2)
    return out.reshape(x.shape).astype(x.dtype)


def _dense_attention(q, k, v, scale):
    b, s, h, d = q.shape
    nb = s // Q_BLOCK
    qb = q.reshape(b, nb, Q_BLOCK, h, d).swapaxes(0, 1)

    def one(qi):
        sc = jnp.einsum('bqhd,bkhd->bhqk', qi, k, preferred_element_type=jnp.float32) * scale
        p = jax.nn.softmax(sc, axis=-1).astype(v.dtype)
        return jnp.einsum('bhqk,bkhv->bqhv', p, v)

    out = lax.map(one, qb)
    return out.swapaxes(0, 1).reshape(b, s, h, v.shape[-1])


def _neighbourhood_attention(q, k, v, kc, vc, rpb):
    b, s, h, d = q.shape
    rows = s // GRID_W
    kh = min(NA_KH, rows)
    kw = NA_KW
    qg = q.reshape(b, rows, GRID_W, h, d).swapaxes(0, 1)
    kg = k.reshape(b, rows, GRID_W, h, d)
    vg = v.reshape(b, rows, GRID_W, h, d)
    row_start = jnp.clip(jnp.arange(rows, dtype=jnp.int32) - kh // 2, 0, rows - kh)
    cols = jnp.arange(GRID_W, dtype=jnp.int32)
    col_idx = jnp.clip(cols - kw // 2, 0, GRID_W - kw)[:, None] + jnp.arange(kw, dtype=jnp.int32)
    col_off = col_idx - cols[:, None] + (NA_KW - 1)

    def one(args):
        r, qr = args
        rs = row_start[r]
        kn = lax.dynamic_slice_in_dim(kg, rs, kh, axis=1)[:, :, col_idx]
        vn = lax.dynamic_slice_in_dim(vg, rs, kh, axis=1)[:, :, col_idx]
        row_off = rs + jnp.arange(kh, dtype=jnp.int32) - r + (NA_KH - 1)
        bias = rpb[:, row_off[None, :, None], col_off[:, None, :]].astype(jnp.float32)
        s_nb = jnp.einsum('bchd,bicjhd->bhcij', qr, kn, preferred_element_type=jnp.float32) * NA_SCALE + bias
        s_cx = jnp.einsum('bchd,bkhd->bhck', qr, kc, preferred_element_type=jnp.float32) * NA_SCALE
        logits = jnp.concatenate([s_nb.reshape(b, h, GRID_W, kh * kw), s_cx], axis=-1)
        p = jax.nn.softmax(logits, axis=-1).astype(v.dtype)
        p_nb = p[..., :kh * kw].reshape(b, h, GRID_W, kh, kw)
        p_cx = p[..., kh * kw:]
        return (jnp.einsum('bhcij,bicjhd->bchd', p_nb, vn)
                + jnp.einsum('bhck,bkhd->bchd', p_cx, vc))

    out = lax.map(one, (jnp.arange(rows, dtype=jnp.int32), qg))
    return out.swapaxes(0, 1).reshape(b, s, h * d)


def _short_conv(z, w):
    return lax.conv_general_dilated(z, w.astype(z.dtype)[:, None, :], (1,),
                                    [(CONV_K // 2, CONV_K // 2)],
                                    dimension_numbers=('NWC', 'WIO', 'NWC'),
                                    feature_group_count=z.shape[-1])


def _fourier(v):
    b, n, _ = v.shape
    f = jnp.fft.fft2(v.astype(jnp.float32).reshape(b, n, FNET_GROUPS, FNET_GW), axes=(1, 3), norm='ortho')
    return f.real.reshape(b, n, FNET_W).astype(v.dtype)


def _mla_kv(h, g_kv, w_ukv, rotate):
    b, n, _ = h.shape
    c_kv, k_r = _split(h, (KV_LORA, MLA_ROPE))
    kv = (_rmsnorm(c_kv, g_kv) @ w_ukv).reshape(b, n, MLA_HEADS, MLA_NOPE + MLA_V)
    k_nope, v = _split(kv, (MLA_NOPE, MLA_V))
    k_r = k_r[:, :, None, :]
    if rotate:
        k_r = _axial_rope(k_r)
    k = jnp.concatenate([k_nope, jnp.broadcast_to(k_r, (b, n, MLA_HEADS, MLA_ROPE))], axis=-1)
    return k, v


def _mla_q(h, g_q, w_uq, rotate):
    b, n, _ = h.shape
    q = (_rmsnorm(h, g_q) @ w_uq).reshape(b, n, MLA_HEADS, MLA_NOPE + MLA_ROPE)
    q_nope, q_r = _split(q, (MLA_NOPE, MLA_ROPE))
    if rotate:
        q_r = _axial_rope(q_r)
    return jnp.concatenate([q_nope, q_r], axis=-1)


def _combine(parts, mla_o, na_o, conv_w, w_p_conv, w_p_mla, w_p_na, w_p_fnet, w_out):
    cb, cc, cx, fv, g_cv, g_ml, g_na, g_fn, mg = parts[5:]
    conv_o = cb * _short_conv(cc * cx, conv_w)
    fn_o = _fourier(fv)
    b, n, _ = mg.shape
    gates = jax.nn.sigmoid(mg).reshape(b, n, N_BRANCH, D_MODEL)
    branches = ((conv_o, g_cv, w_p_conv), (mla_o, g_ml, w_p_mla), (na_o, g_na, w_p_na), (fn_o, g_fn, w_p_fnet))
    m = gates[:, :, 0] * ((conv_o * jax.nn.silu(g_cv)) @ w_p_conv)
    for i in range(1, N_BRANCH):
        o, g, w = branches[i]
        m = m + gates[:, :, i] * ((o * jax.nn.silu(g)) @ w)
    return m @ w_out


def _layer(xc, xl, c, c_ctx, g_pre, g_post, w_ada, b_ada, w_in, g_q, g_kv, w_uq, w_ukv,
           conv_w, na_rpb, w_p_conv, w_p_mla, w_p_na, w_p_fnet, w_out, ctx_out):
    shl, scl, gtl = jnp.split(jax.nn.silu(c) @ w_ada + b_ada, 3, axis=-1)
    shc, scc, gtc = jnp.split(jax.nn.silu(c_ctx) @ w_ada + b_ada, 3, axis=-1)
    ul = _rmsnorm(xl, g_pre) * (1 + scl[:, None]) + shl[:, None]
    uc = _rmsnorm(xc, g_pre) * (1 + scc) + shc
    b, s, _ = xl.shape

    pl = _split(ul @ w_in, IN_WIDTHS)
    if ctx_out:
        pc = _split(uc @ w_in, IN_WIDTHS)
    else:
        pc = _split(uc @ w_in[:, :KV_COLS], IN_WIDTHS[:3])

    kc_m, vc_m = _mla_kv(pc[0], g_kv, w_ukv, False)
    kc_n = _heads(pc[1], NA_HEADS, NA_HD)
    vc_n = _heads(pc[2], NA_HEADS, NA_HD)

    kl_m, vl_m = _mla_kv(pl[0], g_kv, w_ukv, True)
    ql_m = _mla_q(pl[3], g_q, w_uq, True)
    mla_l = _dense_attention(ql_m, jnp.concatenate([kc_m, kl_m], axis=1),
                             jnp.concatenate([vc_m, vl_m], axis=1), MLA_SCALE).reshape(b, s, MLA_W)
    na_l = _neighbourhood_attention(_heads(pl[4], NA_HEADS, NA_HD), _heads(pl[1], NA_HEADS, NA_HD),
                                    _heads(pl[2], NA_HEADS, NA_HD), kc_n, vc_n, na_rpb)
    yl = _combine(pl, mla_l, na_l, conv_w, w_p_conv, w_p_mla, w_p_na, w_p_fnet, w_out)
    xl_new = xl + gtl[:, None] * _rmsnorm(yl, g_post)

    if not ctx_out:
        return None, xl_new
    qc_m = _mla_q(pc[3], g_q, w_uq, False)
    mla_c = _dense_attention(qc_m, kc_m, vc_m, MLA_SCALE).reshape(xc.shape[0], xc.shape[1], MLA_W)
    na_c = _dense_attention(_heads(pc[4], NA_HEADS, NA_HD), kc_n, vc_n, NA_SCALE).reshape(xc.shape[0], xc.shape[1], NA_W)
    yc = _combine(pc, mla_c, na_c, conv_w, w_p_conv, w_p_mla, w_p_na, w_p_fnet, w_out)
    xc_new = xc + gtc * _rmsnorm(yc, g_post)
    return xc_new, xl_new


def setup_inputs(seed: int = 0) -> dict:
    key = jax.random.key(seed)
    ks = jax.random.split(key, 20)

    def nrm(k, shape, scale):
        return jax.random.normal(k, shape, jnp.float32) * scale

    return {
        'x': nrm(ks[0], (BATCH, SEQ, D_MODEL), 1.0),
        'c': nrm(ks[1], (BATCH, D_MODEL), 1.0),
        'ctx': nrm(ks[2], (BATCH, CTX_LEN, D_MODEL), 1.0),
        'c_ctx': nrm(ks[3], (D_MODEL,), 1.0),
        'g_pre': 1.0 + nrm(ks[4], (DEPTH, D_MODEL), 0.05),
        'g_post': 1.0 + nrm(ks[5], (DEPTH, D_MODEL), 0.05),
        'w_ada': nrm(ks[6], (DEPTH, D_MODEL, 3 * D_MODEL), 0.5 * D_MODEL ** -0.5),
        'b_ada': nrm(ks[7], (DEPTH, 3 * D_MODEL), 0.02),
        'w_in': nrm(ks[8], (DEPTH, D_MODEL, N_IN), D_MODEL ** -0.5),
        'g_q': 1.0 + nrm(ks[9], (DEPTH, Q_LORA), 0.05),
        'g_kv': 1.0 + nrm(ks[10], (DEPTH, KV_LORA), 0.05),
        'w_uq': nrm(ks[11], (DEPTH, Q_LORA, MLA_HEADS * (MLA_NOPE + MLA_ROPE)), Q_LORA ** -0.5),
        'w_ukv': nrm(ks[12], (DEPTH, KV_LORA, MLA_HEADS * (MLA_NOPE + MLA_V)), KV_LORA ** -0.5),
        'conv_w': nrm(ks[13], (DEPTH, CONV_K, CONV_W), CONV_K ** -0.5),
        'na_rpb': nrm(ks[14], (DEPTH, NA_HEADS, 2 * NA_KH - 1, 2 * NA_KW - 1), 0.1),
        'w_p_conv': nrm(ks[15], (DEPTH, CONV_W, D_MODEL), CONV_W ** -0.5),
        'w_p_mla': nrm(ks[16], (DEPTH, MLA_W, D_MODEL), MLA_W ** -0.5),
        'w_p_na': nrm(ks[17], (DEPTH, NA_W, D_MODEL), NA_W ** -0.5),
        'w_p_fnet': nrm(ks[18], (DEPTH, FNET_W, D_MODEL), FNET_W ** -0.5),
        'w_out': nrm(ks[19], (DEPTH, D_MODEL, D_MODEL), D_MODEL ** -0.5),
    }


def reference(x, c, ctx, c_ctx, g_pre, g_post, w_ada, b_ada, w_in, g_q, g_kv, w_uq, w_ukv,
              conv_w, na_rpb, w_p_conv, w_p_mla, w_p_na, w_p_fnet, w_out):
    xl, xc = x, ctx
    for i in range(DEPTH):
        xc, xl = _layer(xc, xl, c, c_ctx, g_pre[i], g_post[i], w_ada[i], b_ada[i], w_in[i],
                        g_q[i], g_kv[i], w_uq[i], w_ukv[i], conv_w[i], na_rpb[i],
                        w_p_conv[i], w_p_mla[i], w_p_na[i], w_p_fnet[i], w_out[i],
                        i < DEPTH - 1)
    return xl
```

```python
from contextlib import ExitStack
import numpy as np
import ml_dtypes
import concourse.bass as bass
import concourse.mybir as mybir
from concourse.bass_utils import run_bass_kernel_spmd

F32 = mybir.dt.float32
BF16 = mybir.dt.bfloat16
AF = mybir.ActivationFunctionType
ALU = mybir.AluOpType

D = 2048
NLAT = 2048
NCTX = 256
NT = NLAT + NCTX
KC = D // 128
N_IN = 20288
EPS = 1e-6
GRID_W = 64
MLA_SCALE = 192.0 ** -0.5
NA_SCALE = 0.125
NEG = -30000.0
NWT = 40

C_KV0, C_NAK, C_NAV, C_CQ, C_NAQ, C_CB, C_CC, C_CX, C_FV = 0, 320, 1344, 2368, 2880, 3904, 4928, 5952, 6976
C_GCV, C_GML, C_GNA, C_GFN, C_MG = 8000, 9024, 10048, 11072, 12096


class Eng:
    def __init__(self, nc, name, h, is_pe=False):
        self.nc, self.name, self.h, self.is_pe = nc, name, h, is_pe
        self.sem = nc.alloc_semaphore("pg_" + name)
        self.n = 0
        self.seen = {}

    def wait(self, ev):
        sem, val = ev
        if self.is_pe and sem is self.sem:
            return
        if self.seen.get(sem, 0) >= val:
            return
        self.h.wait_ge(sem, val)
        self.seen[sem] = val

    def mark(self, inst):
        self.n += 1
        inst.then_inc(self.sem, 1)
        return (self.sem, self.n)


class Buf:
    def __init__(self, t, name="", part=False):
        self.t = t
        self.name = name
        self.w = {}
        self.r = {}
        self.part = part
        self.dsem = None
        self.dcnt = 0

    def __getitem__(self, k):
        return self.t[k]


class K:
    def __init__(self, nc):
        self.nc = nc
        self.pe = Eng(nc, "pe", nc.tensor, True)
        self.act = Eng(nc, "act", nc.scalar)
        self.dve = Eng(nc, "dve", nc.vector)
        self.pool = Eng(nc, "pool", nc.gpsimd)
        self.sp = Eng(nc, "sp", nc.sync)
        self.nsem = 0
        self.psum = []
        self.psi = 0
        self.free_sems = []
        self.scopes = []

    def _pre(self, eng, reads, writes):
        for b in reads:
            for ev in b.w.values():
                eng.wait(ev)
        for b in writes:
            if not b.part:
                for ev in b.w.values():
                    eng.wait(ev)
            for ev in b.r.values():
                eng.wait(ev)

    def _post(self, ev, reads, writes):
        for b in reads:
            b.r[ev[0]] = ev
        for b in writes:
            if b.part:
                b.w[ev[0]] = ev
            else:
                b.w = {ev[0]: ev}
                b.r = {}

    def op(self, eng, fn, reads=(), writes=()):
        self._pre(eng, reads, writes)
        inst = fn(eng.h)
        ev = eng.mark(inst)
        self._post(ev, reads, writes)

    def mm(self, mms, reads=(), writes=()):
        eng = self.pe
        self._pre(eng, reads, writes)
        inst = None
        for (o, l, r, st, sp) in mms:
            inst = self.nc.tensor.matmul(o, l, r, start=st, stop=sp)
        ev = eng.mark(inst)
        self._post(ev, reads, writes)

    def dma(self, q, out, in_, owner, reads=(), writes=(), slow=False):
        self._pre(q, reads, writes)
        if owner.dsem is None:
            if self.free_sems:
                owner.dsem, owner.dcnt = self.free_sems.pop()
                q.wait((owner.dsem, owner.dcnt))
            else:
                owner.dsem = self.nc.alloc_semaphore("d%d" % self.nsem)
                self.nsem += 1
        if slow:
            inst = q.h.dma_start(out=out, in_=in_, allow_slow_non_contiguous=True)
        else:
            inst = q.h.dma_start(out=out, in_=in_)
        owner.dcnt += 16
        inst.then_inc(owner.dsem, 16)
        ev = (owner.dsem, owner.dcnt)
        self._post(ev, reads, writes)

    def sb(self, es, name, shape, dt, part=False):
        self.nsb = getattr(self, "nsb", 0) + 1
        t = es.enter_context(self.nc.sbuf_tensor("s%d_%s" % (self.nsb, name), list(shape), dt))
        b = Buf(t, name, part)
        if self.scopes:
            self.scopes[-1].append(b)
        return b

    def scope(self):
        return _Scope(self)

    def pool_of(self, es, name, shape, dt, n, part=False):
        return Ring([self.sb(es, "%s%d" % (name, i), shape, dt, part) for i in range(n)])

    def ps(self):
        b = self.psum[self.psi % len(self.psum)]
        self.psi += 1
        return b


class _Scope:
    def __init__(self, k):
        self.k = k
        self.es = ExitStack()

    def __enter__(self):
        self.k.scopes.append([])
        self.es.__enter__()
        return self.es

    def __exit__(self, *a):
        k = self.k
        bufs = k.scopes.pop()
        evs = {}
        for b in bufs:
            for d in (b.w, b.r):
                for (sem, val) in d.values():
                    if evs.get(sem, (None, 0))[1] < val:
                        evs[sem] = (sem, val)
        for eng in (k.pe, k.act, k.dve, k.pool, k.sp):
            for ev in evs.values():
                if not (ev[0] is eng.sem):
                    eng.wait(ev)
                elif eng.is_pe:
                    pass
        for b in bufs:
            if b.dsem is not None:
                k.free_sems.append((b.dsem, b.dcnt))
        return self.es.__exit__(*a)


class Ring:
    def __init__(self, bufs):
        self.bufs = bufs
        self.i = 0

    def next(self):
        b = self.bufs[self.i % len(self.bufs)]
        self.i += 1
        return b


def split_groups(ranges, n=512):
    out = []
    for (a, b) in ranges:
        t = a
        while t < b:
            e = min(b, (t // n + 1) * n)
            out.append((t, e - t))
            t = e
    return out


from contextlib import ExitStack


class Cst:
    pass


def rstd_op(k, out_buf, out_ap, ps_buf, ps_ap, nfeat):
    k.op(k.dve, lambda e: e.tensor_scalar(out=out_ap, in0=ps_ap, scalar1=1.0 / nfeat, scalar2=EPS,
                                          op0=ALU.mult, op1=ALU.add), reads=[ps_buf], writes=[out_buf])
    k.op(k.act, lambda e: e.activation(out=out_ap, in_=out_ap, func=AF.Sqrt), reads=[out_buf], writes=[out_buf])
    k.op(k.dve, lambda e: e.reciprocal(out=out_ap, in_=out_ap), reads=[out_buf], writes=[out_buf])


def na_slots(j):
    s = [-2, -1, 0, 1, 2]
    if j in (0, 8):
        s.append(3)
    if j in (7, 15):
        s.append(-3)
    return s


def na_mask_index():
    idx = {}
    n = 0
    for j in range(16):
        for d in na_slots(j):
            idx[(j, d)] = n
            n += 1
    return idx, n


NA_MIDX, NA_NM = na_mask_index()


def build_program(layers=(0, 1), taps=(), stop_after=None, x1_in=False):
    nc = bass.Bass("TRN2", target_bir_lowering=False)
    k = K(nc)

    def din(name, shape, dt=F32):
        return Buf(nc.dram_tensor(name, list(shape), dt, kind="ExternalInput").ap(), name, part=True)

    def dscr(name, shape, dt):
        kind = "ExternalOutput" if name in taps else "Internal"
        return Buf(nc.dram_tensor(name, list(shape), dt, kind=kind).ap(), name, part=True)

    T = Cst()
    T.xT = din("xT", [D, NT])
    T.cvec = din("cvec", [128, 16, 2])
    T.w_ada = din("w_ada", [2, D, 3 * D])
    T.b_ada = din("b_ada", [2, 128, 48])
    T.w_in = din("w_in", [2, D, N_IN])
    T.gpre = din("gpre", [128, 2, 16])
    T.gpost = din("gpost", [128, 2, 16])
    T.gq = din("gq", [128, 2, 4])
    T.gkv = din("gkv", [128, 2, 2])
    T.w_uq = din("w_uq", [2, 512, 1536])
    T.w_ukv = din("w_ukv", [2, 256, 2048])
    T.convw = din("convw", [128, 2, 8, 3])
    T.brel = din("brel", [2, 16, 128, 7, 128])
    T.w_p = din("w_p", [2, 4, 1024, D])
    T.w_out = din("w_out", [2, D, D])
    T.ident = din("ident", [128, 128])
    T.perm = din("perm", [64, 64])
    T.rope = din("rope", [64, 2, NLAT])
    T.namask = din("namask", [128, NA_NM, 128], BF16)
    T.cmask = din("cmask", [128, 2, NLAT])
    T.dftN = din("dftN", [2, NLAT, NLAT], BF16)
    T.dftC = din("dftC", [2, NCTX, NCTX], BF16)
    T.dftM = din("dftM", [2, 256, 256], BF16)
    T.outT = Buf(nc.dram_tensor("outT", [D, 1024], F32, kind="ExternalOutput").ap(), "outT", part=True)
    T.HT = dscr("HT", [N_IN, NT], BF16)
    T.VNA = dscr("VNA", [NT, 1024], BF16)
    T.VF = dscr("VF", [NT, 1024], BF16)
    T.OG = dscr("OG", [4, 1024, NT], BF16)
    T.X1 = dscr("X1", [D, NT], F32)
    T.UTd = dscr("UTd", [D, NT], BF16) if "UTd" in taps else None
    T.MODd = dscr("MODd", [128, 2, 48, 2], F32) if "MODd" in taps else None

    owners = []
    _dma = k.dma

    def dma_reg(q, out, in_, owner, reads=(), writes=(), slow=False):
        if owner not in owners:
            owners.append(owner)
        _dma(q, out, in_, owner, reads, writes, slow)
    k.dma = dma_reg

    with ExitStack() as top:
        for i in range(8):
            k.psum.append(Buf(top.enter_context(nc.psum_tensor("psb%d" % i, [128, 512], F32)), "ps%d" % i))
        C = Cst()
        C.ones_f = k.sb(top, "ones_f", [128, 128], F32)
        C.ones_b = k.sb(top, "ones_b", [128, 128], BF16)
        C.ident_b = k.sb(top, "ident_b", [128, 128], BF16)
        C.perm_b = k.sb(top, "perm_b", [64, 64], BF16)
        C.perm_f = k.sb(top, "perm_f", [64, 64], F32)
        C.dftM = k.sb(top, "dftM", [128, 2, 2, 256], BF16)
        C.mod = k.sb(top, "mod", [128, 2, 48, 2], F32, part=True)
        C.A = k.sb(top, "modA", [128, 2, 2, 16], F32, part=True)
        C.G = k.sb(top, "modG", [128, 2, 2, 16], F32, part=True)
        C.gpre = k.sb(top, "gpre", [128, 2, 16], F32)
        C.gpost = k.sb(top, "gpost", [128, 2, 16], F32)
        C.gq = k.sb(top, "gq", [128, 2, 4], F32)
        C.gkv = k.sb(top, "gkv", [128, 2, 2], F32)
        C.convw = k.sb(top, "convw", [128, 2, 8, 3], F32)

        k.op(k.dve, lambda e: e.memset(C.ones_f[:], 1.0), writes=[C.ones_f])
        k.op(k.dve, lambda e: e.memset(C.ones_b[:], 1.0), writes=[C.ones_b])
        k.dma(k.pool, C.ident_b[:], T.ident[:], C.ident_b, writes=[C.ident_b])
        k.dma(k.pool, C.perm_b[:], T.perm[:], C.perm_b, writes=[C.perm_b])
        k.dma(k.sp, C.perm_f[:], T.perm[:], C.perm_f, writes=[C.perm_f])
        for cs in range(2):
            k.dma(k.sp, C.dftM[:, cs, :, :], T.dftM[cs].rearrange("(k p) c -> p k c", p=128), C.dftM,
                  writes=[C.dftM])
        for (dst, src) in ((C.gpre, T.gpre), (C.gpost, T.gpost), (C.gq, T.gq), (C.gkv, T.gkv), (C.convw, T.convw)):
            k.dma(k.sp, dst[:], src[:], dst, writes=[dst])

        stages = []

        def run(name, fn, *a):
            if stages and stages[-1] == "__stop__":
                return
            globals()[fn](k, T, C, *a)
            stages.append(name)
            if stop_after == name:
                stages.append("__stop__")

        run("ada", "stage_ada")
        if T.MODd is not None and "__stop__" in stages[-1:]:
            pass
        for l in layers:
            lat_q = NLAT if l == 0 else 1024
            ctxq = (l == 0)
            src = T.xT if (l == 0 or x1_in) else T.X1
            with k.scope() as es_u:
                UT = k.sb(es_u, "UT", [128, KC, NT], BF16, part=True)
                run("prenorm%d" % l, "stage_prenorm", l, UT, src)
                run("inproj%d" % l, "stage_inproj", l, UT, lat_q, ctxq)
            run("conv%d" % l, "stage_conv", l, lat_q, ctxq)
            run("mla%d" % l, "stage_mla", l, lat_q, ctxq)
            run("na%d" % l, "stage_na", l, lat_q, ctxq)
            run("four%d" % l, "stage_fourier", l, lat_q, ctxq)
            dst = T.X1 if l == 0 else T.outT
            run("epi%d" % l, "stage_epilogue", l, lat_q, ctxq, src, dst)

        if T.MODd is not None:
            k.dma(k.sp, T.MODd[:], C.mod[:], C.mod, reads=[C.mod], writes=[T.MODd])
        for ob in owners:
            k.sp.wait((ob.dsem, ob.dcnt))
    return nc


def stage_ada(k, T, C):
    with k.scope() as es:
        cv = k.sb(es, "cv", [128, 16, 2], F32)
        scv = k.sb(es, "scv", [128, 16, 2], F32)
        k.dma(k.sp, cv[:], T.cvec[:], cv, writes=[cv])
        k.op(k.act, lambda e: e.activation(out=scv[:], in_=cv[:], func=AF.Silu), reads=[cv], writes=[scv])
        wpool = k.pool_of(es, "wada", [128, 16, 512], F32, 2)
        bada = k.sb(es, "bada", [128, 2, 48], F32)
        k.dma(k.sp, bada[:], T.b_ada[:].rearrange("l p m -> p l m"), bada, writes=[bada])
        for l in range(2):
            pm = k.psum[l]
            pm.part = True
            for cg in range(12):
                wt = wpool.next()
                k.dma(k.sp, wt[:], T.w_ada[l][:, cg * 512:(cg + 1) * 512].rearrange("(k p) c -> p k c", p=128),
                      wt, writes=[wt])
                mms = []
                for s in range(4):
                    m = cg * 4 + s
                    for kk in range(16):
                        mms.append((pm.t[:, 2 * m:2 * m + 2], wt[:, kk, s * 128:(s + 1) * 128], scv[:, kk, :],
                                    kk == 0, kk == 15))
                k.mm(mms, reads=[wt, scv], writes=[pm])
            pmv = pm.t[:, 0:96].rearrange("p (m v) -> p m v", v=2)
            for v in range(2):
                k.op(k.dve, lambda e: e.tensor_tensor(out=C.mod[:, l, :, v], in0=pmv[:, :, v], in1=bada[:, l, :],
                                                      op=ALU.add), reads=[pm, bada], writes=[C.mod])
            pm.part = False
            for v in range(2):
                k.op(k.dve, lambda e: e.scalar_tensor_tensor(out=C.A[:, l, v, :], in0=C.mod[:, l, 16:32, v],
                                                             scalar=1.0, in1=C.gpre[:, l, :],
                                                             op0=ALU.add, op1=ALU.mult),
                     reads=[C.mod, C.gpre], writes=[C.A])
                k.op(k.dve, lambda e: e.tensor_tensor(out=C.G[:, l, v, :], in0=C.mod[:, l, 32:48, v],
                                                      in1=C.gpost[:, l, :], op=ALU.mult),
                     reads=[C.mod, C.gpost], writes=[C.G])


def stage_prenorm(k, T, C, l, UT, src):
    TB = 256
    with k.scope() as es:
        xp = k.pool_of(es, "xn", [128, KC, TB], F32, 2)
        sqp = k.pool_of(es, "sqn", [128, KC, TB], F32, 2)
        rsp = k.pool_of(es, "rsn", [128, TB], F32, 2)
        tp = k.pool_of(es, "tn", [128, TB], F32, 3)
        psr = Ring(k.psum[2:6])
        for blk in range(NT // TB):
            t0 = blk * TB
            v = 0 if t0 < NLAT else 1
            xt = xp.next()
            k.dma(k.sp, xt[:], src.t[:, t0:t0 + TB].rearrange("(k p) t -> p k t", p=128), xt,
                  reads=[src], writes=[xt])
            sq = sqp.next()
            k.op(k.act, lambda e: e.activation(out=sq[:], in_=xt[:], func=AF.Square), reads=[xt], writes=[sq])
            ps = psr.next()
            k.mm([(ps.t[:, 0:TB], C.ones_f[:], sq[:, kk, :], kk == 0, kk == KC - 1) for kk in range(KC)],
                 reads=[sq, C.ones_f], writes=[ps])
            rs = rsp.next()
            rstd_op(k, rs, rs[:], ps, ps.t[:, 0:TB], D)
            for kk in range(KC):
                tt = tp.next()
                k.op(k.dve, lambda e: e.tensor_tensor(out=tt[:], in0=xt[:, kk, :], in1=rs[:], op=ALU.mult),
                     reads=[xt, rs], writes=[tt])
                k.op(k.act, lambda e: e.activation(out=UT[:, kk, t0:t0 + TB], in_=tt[:], func=AF.Identity,
                                                   scale=C.A[:, l, v, kk:kk + 1], bias=C.mod[:, l, kk, v:v + 1]),
                     reads=[tt, C.A, C.mod], writes=[UT])
        if T.UTd is not None:
            k.dma(k.sp, T.UTd.t.rearrange("(k p) t -> p k t", p=128), UT[:], UT, reads=[UT], writes=[T.UTd])


def inproj_plan(l, lat_q, ctxq):
    allr = [(0, NT)]
    own = [(0, NT)] if ctxq else [(0, lat_q)]
    halo = own if ctxq else own + [(lat_q, lat_q + 2), (NLAT - 2, NLAT)]
    fvr = [(0, NT)] if ctxq else [(0, NLAT)]
    return [
        (0, C_NAV, "FM", "copy", allr, None),
        (C_NAV, C_CQ, "TM", "copy", allr, "VNA"),
        (C_CQ, C_CC, "FM", "copy", own, None),
        (C_CC, C_FV, "FM", "copy", halo, None),
        (C_FV, C_GCV, "TM", "copy", fvr, "VF"),
        (C_GCV, C_MG, "FM", "silu", own, None),
        (C_MG, N_IN, "FM", "sigm", own, None),
    ]


def stage_inproj(k, T, C, l, UT, lat_q, ctxq):
    plan = inproj_plan(l, lat_q, ctxq)
    with k.scope() as es:
        wp = k.pool_of(es, "win", [128, KC, 512], BF16, 3)
        otp = k.pool_of(es, "ot", [128, NT], BF16, 3, part=True)
        vtp = k.pool_of(es, "vt", [128, 512], BF16, 4)
        psr = Ring(k.psum)
        for wi in range(NWT):
            c0 = 0 if wi == 0 else 320 + 512 * (wi - 1)
            wd = 320 if wi == 0 else 512
            ent = [p for p in plan if p[0] <= c0 < p[1]][0]
            _, lo_, mode, func, ranges, dname = ent
            wt = wp.next()
            k.dma(k.pool, wt[:, :, 0:wd], T.w_in[l][:, c0:c0 + wd].rearrange("(k p) c -> p k c", p=128), wt,
                  writes=[wt])
            if mode == "FM":
                for s0 in range(0, wd, 128):
                    ncp = min(128, wd - s0)
                    ot = otp.next()
                    for (t0, n) in split_groups(ranges):
                        ps = psr.next()
                        k.mm([(ps.t[0:ncp, 0:n], wt[:, kk, s0:s0 + ncp], UT[:, kk, t0:t0 + n], kk == 0, kk == KC - 1)
                              for kk in range(KC)], reads=[wt, UT], writes=[ps])
                        if func == "copy":
                            k.op(k.dve, lambda e: e.tensor_copy(out=ot[0:ncp, t0:t0 + n], in_=ps.t[0:ncp, 0:n]),
                                 reads=[ps], writes=[ot])
                        else:
                            f = AF.Silu if func == "silu" else AF.Sigmoid
                            k.op(k.act, lambda e: e.activation(out=ot[0:ncp, t0:t0 + n], in_=ps.t[0:ncp, 0:n], func=f),
                                 reads=[ps], writes=[ot])
                    for (a, b) in ranges:
                        k.dma(k.sp, T.HT.t[c0 + s0:c0 + s0 + ncp, a:b], ot[0:ncp, a:b], ot, reads=[ot], writes=[T.HT])
            else:
                dest = T.VNA if dname == "VNA" else T.VF
                cc0 = c0 - ent[0]
                for (a, b) in ranges:
                    for tc in range(a, b, 128):
                        ps = psr.next()
                        k.mm([(ps.t[:, 0:512], UT[:, kk, tc:tc + 128], wt[:, kk, :], kk == 0, kk == KC - 1)
                              for kk in range(KC)], reads=[wt, UT], writes=[ps])
                        vt = vtp.next()
                        k.op(k.dve, lambda e: e.tensor_copy(out=vt[:], in_=ps.t[:]), reads=[ps], writes=[vt])
                        k.dma(k.sp, dest.t[tc:tc + 128, cc0:cc0 + 512], vt[:], vt, reads=[vt], writes=[dest])


_CONST_CACHE = {}


def host_consts(hf):
    if hf in _CONST_CACHE:
        return _CONST_CACHE[hf]
    bf = ml_dtypes.bfloat16
    c = {}
    c["ident"] = np.eye(128, dtype=np.float32)
    perm = np.zeros((64, 64), np.float32)
    for dp in range(64):
        sw = dp + 16 if (dp % 32) // 16 == 0 else dp - 16
        perm[sw, dp] = 1.0
    c["perm"] = perm
    i = np.arange(NLAT)
    t = (i + 1024 * hf) % NLAT
    pos = np.stack([t // GRID_W, t % GRID_W]).astype(np.float32)
    nf = 16
    inv = (np.float32(10000.0) ** (-np.arange(nf, dtype=np.float32) / np.float32(nf))).astype(np.float32)
    rope = np.zeros((64, 2, NLAT), np.float32)
    for d in range(64):
        half, ab, f = d // 32, (d % 32) // 16, d % 16
        ang = (pos[half] * inv[f]).astype(np.float32)
        rope[d, 0] = np.cos(ang)
        rope[d, 1] = -np.sin(ang) if ab == 0 else np.sin(ang)
    c["rope"] = rope
    m = np.full((128, NA_NM, 128), NEG, np.float32)
    kk = np.arange(128)
    for j in range(16):
        G = (j + 8 * hf) % 16
        for d in na_slots(j):
            Gk = G + d
            if not (0 <= Gk <= 15):
                continue
            kr = 2 * Gk + kk // 64
            kc = kk % 64
            r = 2 * G + kk // 64
            cc = kk % 64
            rs = np.clip(r - 4, 0, 24)
            cs = np.clip(cc - 8, 0, 48)
            valid = ((kr[:, None] >= rs[None, :]) & (kr[:, None] < rs[None, :] + 8) &
                     (kc[:, None] >= cs[None, :]) & (kc[:, None] < cs[None, :] + 16))
            m[:, NA_MIDX[(j, d)], :] = np.where(valid, 0.0, NEG)
    c["namask"] = m.astype(bf)
    cm = np.ones((128, 2, NLAT), np.float32)
    cm[:, 0, t == 0] = 0.0
    cm[:, 1, t == NLAT - 1] = 0.0
    c["cmask"] = cm
    prod = (t[:, None].astype(np.int64) * t[None, :].astype(np.int64)) % NLAT
    ang = prod.astype(np.float64) * (2.0 * np.pi / NLAT)
    c["dftN"] = np.stack([np.cos(ang), np.sin(ang)]).astype(np.float32) / np.float32(np.sqrt(NLAT))
    c["dftN"] = c["dftN"].astype(bf)
    n = np.arange(256)
    ang = ((n[:, None] * n[None, :]) % 256).astype(np.float64) * (2.0 * np.pi / 256)
    c["dftC"] = (np.stack([np.cos(ang), np.sin(ang)]) / 16.0).astype(np.float32).astype(bf)
    c["dftM"] = (np.stack([np.cos(ang), -np.sin(ang)]) / 16.0).astype(np.float32).astype(bf)
    _CONST_CACHE[hf] = c
    return c


def brel_table(na_rpb):
    kk = np.arange(128)
    out = np.zeros((2, 16, 128, 7, 128), np.float32)
    for si in range(7):
        d = si - 3
        dr = 2 * d + (kk // 64)[:, None] - (kk // 64)[None, :]
        dc = (kk % 64)[:, None] - (kk % 64)[None, :]
        ok = (np.abs(dr) <= 7) & (np.abs(dc) <= 15)
        ri = np.clip(dr + 7, 0, 14)
        ci = np.clip(dc + 15, 0, 30)
        g = na_rpb[:, :, ri, ci]
        out[:, :, :, si, :] = np.where(ok[None, None], g, np.float32(0.0))
    return out


def prep_inputs(inp, cores=range(8)):
    f = np.float32

    def pk(a, k):
        return np.ascontiguousarray(np.moveaxis(a.reshape(a.shape[:-1] + (k, 128)), -1, 0))

    shared = {
        "w_ada": np.ascontiguousarray(inp["w_ada"], f),
        "b_ada": np.ascontiguousarray(inp["b_ada"].reshape(2, 48, 128).transpose(0, 2, 1), f),
        "w_in": np.ascontiguousarray(inp["w_in"], f),
        "gpre": pk(inp["g_pre"], 16), "gpost": pk(inp["g_post"], 16),
        "gq": pk(inp["g_q"], 4), "gkv": pk(inp["g_kv"], 2),
        "w_uq": np.ascontiguousarray(inp["w_uq"], f), "w_ukv": np.ascontiguousarray(inp["w_ukv"], f),
        "convw": np.ascontiguousarray(inp["conv_w"].reshape(2, 3, 8, 128).transpose(3, 0, 2, 1), f),
        "brel": brel_table(np.asarray(inp["na_rpb"], f)),
        "w_p": np.ascontiguousarray(np.stack([inp["w_p_conv"], inp["w_p_mla"], inp["w_p_na"], inp["w_p_fnet"]],
                                             axis=1), f),
        "w_out": np.ascontiguousarray(inp["w_out"], f),
    }
    maps = []
    for cid in cores:
        b, hf = cid // 2, cid % 2
        m = dict(shared)
        m.update(host_consts(hf))
        xl = np.roll(np.asarray(inp["x"][b], f), -1024 * hf, axis=0)
        m["xT"] = np.ascontiguousarray(np.concatenate([xl, np.asarray(inp["ctx"][b], f)], axis=0).T)
        cv = np.stack([np.asarray(inp["c"][b], f), np.asarray(inp["c_ctx"], f)], axis=-1)
        m["cvec"] = np.ascontiguousarray(cv.reshape(16, 128, 2).transpose(1, 0, 2))
        maps.append(m)
    return maps


_PROG = {}


def kernel(**inputs):
    maps = prep_inputs(inputs)
    if "full" not in _PROG:
        _PROG["full"] = build_program()
    res = run_bass_kernel_spmd(_PROG["full"], maps, core_ids=list(range(8)))
    out = np.zeros((4, NLAT, D), np.float32)
    for cid in range(8):
        b, hf = cid // 2, cid % 2
        out[b, 1024 * hf:1024 * hf + 1024, :] = res.results[cid]["outT"].T
    return out


def stage_conv(k, T, C, l, lat_q, ctxq):
    segs = [(0, lat_q, NLAT - 1, lat_q % NLAT, True)]
    if ctxq:
        segs.append((NLAT, NCTX, None, None, False))
    with k.scope() as es:
        cmask = k.sb(es, "cmask", [128, 2, NLAT], F32)
        k.dma(k.sp, cmask[:], T.cmask[:], cmask, writes=[cmask])
        WM = lat_q
        ccp = k.pool_of(es, "cvc", [128, WM + 2], BF16, 2, part=True)
        cxp = k.pool_of(es, "cvx", [128, WM + 2], BF16, 2, part=True)
        cbp = k.pool_of(es, "cvb", [128, WM], BF16, 2)
        sgp = k.pool_of(es, "cvg", [128, WM], BF16, 2)
        zpp = k.pool_of(es, "cvz", [128, WM + 2], F32, 2, part=True)
        t1p = k.pool_of(es, "cvt", [128, WM], F32, 2)
        acp = k.pool_of(es, "cva", [128, WM], F32, 2)
        ogp = k.pool_of(es, "cvo", [128, WM], BF16, 2)
        for (t0, W, hl, hr, masked) in segs:
            for kc in range(8):
                r0 = kc * 128
                cc, cx, cb, sg = ccp.next(), cxp.next(), cbp.next(), sgp.next()
                for (tile, col) in ((cc, C_CC), (cx, C_CX)):
                    k.dma(k.sp, tile[:, 1:W + 1], T.HT.t[col + r0:col + r0 + 128, t0:t0 + W], tile,
                          reads=[T.HT], writes=[tile])
                    if masked:
                        k.dma(k.sp, tile[:, 0:1], T.HT.t[col + r0:col + r0 + 128, hl:hl + 1], tile,
                              reads=[T.HT], writes=[tile], slow=True)
                        k.dma(k.sp, tile[:, W + 1:W + 2], T.HT.t[col + r0:col + r0 + 128, hr:hr + 1], tile,
                              reads=[T.HT], writes=[tile], slow=True)
                k.dma(k.sp, cb[:, 0:W], T.HT.t[C_CB + r0:C_CB + r0 + 128, t0:t0 + W], cb, reads=[T.HT], writes=[cb])
                k.dma(k.sp, sg[:, 0:W], T.HT.t[C_GCV + r0:C_GCV + r0 + 128, t0:t0 + W], sg, reads=[T.HT], writes=[sg])
                zp = zpp.next()
                if masked:
                    k.op(k.dve, lambda e: e.tensor_tensor(out=zp[:, 0:W + 2], in0=cc[:, 0:W + 2], in1=cx[:, 0:W + 2],
                                                          op=ALU.mult), reads=[cc, cx], writes=[zp])
                else:
                    k.op(k.dve, lambda e: e.memset(zp[:, 0:1], 0.0), writes=[zp])
                    k.op(k.dve, lambda e: e.memset(zp[:, W + 1:W + 2], 0.0), writes=[zp])
                    k.op(k.dve, lambda e: e.tensor_tensor(out=zp[:, 1:W + 1], in0=cc[:, 1:W + 1], in1=cx[:, 1:W + 1],
                                                          op=ALU.mult), reads=[cc, cx], writes=[zp])
                acc = acp.next()
                k.op(k.dve, lambda e: e.tensor_scalar_mul(out=acc[:, 0:W], in0=zp[:, 1:W + 1],
                                                          scalar1=C.convw[:, l, kc, 1:2]),
                     reads=[zp, C.convw], writes=[acc])
                for (off, wi, mi) in ((0, 0, 0), (2, 2, 1)):
                    if masked:
                        t1 = t1p.next()
                        k.op(k.pool, lambda e: e.tensor_tensor(out=t1[:, 0:W], in0=zp[:, off:off + W],
                                                               in1=cmask[:, mi, 0:W], op=ALU.mult),
                             reads=[zp, cmask], writes=[t1])
                        srcb, srca = t1, t1[:, 0:W]
                    else:
                        srcb, srca = zp, zp[:, off:off + W]
                    k.op(k.dve, lambda e: e.scalar_tensor_tensor(out=acc[:, 0:W], in0=srca,
                                                                 scalar=C.convw[:, l, kc, wi:wi + 1],
                                                                 in1=acc[:, 0:W], op0=ALU.mult, op1=ALU.add),
                         reads=[srcb, acc, C.convw], writes=[acc])
                k.op(k.dve, lambda e: e.tensor_tensor(out=acc[:, 0:W], in0=acc[:, 0:W], in1=cb[:, 0:W], op=ALU.mult),
                     reads=[acc, cb], writes=[acc])
                og = ogp.next()
                k.op(k.pool, lambda e: e.tensor_tensor(out=og[:, 0:W], in0=acc[:, 0:W], in1=sg[:, 0:W], op=ALU.mult),
                     reads=[acc, sg], writes=[og])
                k.dma(k.sp, T.OG.t[0, r0:r0 + 128, t0:t0 + W], og[:, 0:W], og, reads=[og], writes=[T.OG])


def stage_mla(k, T, C, l, lat_q, ctxq):
    NQ = lat_q + (NCTX if ctxq else 0)
    g_all = split_groups([(0, NT)])
    g_q = split_groups([(0, NQ)])
    with k.scope() as es:
        rope = k.sb(es, "rope", [64, 2, NLAT], F32)
        k.dma(k.sp, rope[:], T.rope[:], rope, writes=[rope])
        ckv = k.sb(es, "ckv", [128, 2, NT], BF16)
        ckg = k.sb(es, "ckg", [128, 2, NT], BF16, part=True)
        kr = k.sb(es, "kr", [64, NT], BF16)
        krr = k.sb(es, "krr", [64, NT], BF16, part=True)
        rkv = k.sb(es, "rkv", [128, NT], F32, part=True)
        rkvT = k.sb(es, "rkvT", [128, NT // 128], F32, part=True)
        cq = k.sb(es, "cq", [128, 4, NQ], BF16)
        cqg = k.sb(es, "cqg", [128, 4, NQ], BF16, part=True)
        rq = k.sb(es, "rq", [128, NQ], F32, part=True)
        wukv = k.sb(es, "wukv", [128, 2, 2048], BF16)
        wuq = k.sb(es, "wuq", [128, 4, 1536], BF16)
        k.dma(k.pool, wukv[:], T.w_ukv[l].rearrange("(k p) c -> p k c", p=128), wukv, writes=[wukv])
        k.dma(k.pool, wuq[:], T.w_uq[l].rearrange("(k p) c -> p k c", p=128), wuq, writes=[wuq])
        k.dma(k.sp, ckv[:], T.HT.t[0:256, :].rearrange("(k p) t -> p k t", p=128), ckv, reads=[T.HT], writes=[ckv])
        k.dma(k.sp, kr[:], T.HT.t[256:320, :], kr, reads=[T.HT], writes=[kr])
        k.dma(k.sp, cq[:], T.HT.t[C_CQ:C_CQ + 512, 0:NQ].rearrange("(k p) t -> p k t", p=128), cq,
              reads=[T.HT], writes=[cq])
        sqp = k.pool_of(es, "sqm", [128, 4, 512], F32, 2)
        tmp = k.pool_of(es, "tmm", [128, 512], F32, 4)
        psr = Ring(k.psum[5:8])
        for (t0, n) in g_all:
            sq = sqp.next()
            k.op(k.act, lambda e: e.activation(out=sq[:, 0:2, 0:n], in_=ckv[:, :, t0:t0 + n], func=AF.Square),
                 reads=[ckv], writes=[sq])
            ps = psr.next()
            k.mm([(ps.t[:, 0:n], C.ones_f[:], sq[:, kk, 0:n], kk == 0, kk == 1) for kk in range(2)],
                 reads=[sq, C.ones_f], writes=[ps])
            rstd_op(k, rkv, rkv[:, t0:t0 + n], ps, ps.t[:, 0:n], 256)
            ps2 = psr.next()
            nch = n // 128
            mms = []
            for ci in range(nch):
                for kk in range(2):
                    mms.append((ps2.t[:, ci:ci + 1], sq[:, kk, ci * 128:(ci + 1) * 128], C.ones_f[:, 0:1],
                                kk == 0, kk == 1))
            k.mm(mms, reads=[sq, C.ones_f], writes=[ps2])
            rstd_op(k, rkvT, rkvT[:, t0 // 128:t0 // 128 + nch], ps2, ps2.t[:, 0:nch], 256)
        for kk in range(2):
            k.op(k.dve, lambda e: e.tensor_scalar_mul(out=ckg[:, kk, :], in0=ckv[:, kk, :],
                                                      scalar1=C.gkv[:, l, kk:kk + 1]), reads=[ckv, C.gkv], writes=[ckg])
        for (t0, n) in split_groups([(0, NLAT)]):
            ps = psr.next()
            k.mm([(ps.t[0:64, 0:n], C.perm_b[:], kr[:, t0:t0 + n], True, True)], reads=[C.perm_b, kr], writes=[ps])
            t1, t2 = tmp.next(), tmp.next()
            k.op(k.dve, lambda e: e.tensor_tensor(out=t1[0:64, 0:n], in0=kr[:, t0:t0 + n], in1=rope[:, 0, t0:t0 + n],
                                                  op=ALU.mult), reads=[kr, rope], writes=[t1])
            k.op(k.dve, lambda e: e.tensor_tensor(out=t2[0:64, 0:n], in0=ps.t[0:64, 0:n], in1=rope[:, 1, t0:t0 + n],
                                                  op=ALU.mult), reads=[ps, rope], writes=[t2])
            k.op(k.pool, lambda e: e.tensor_tensor(out=krr[:, t0:t0 + n], in0=t1[0:64, 0:n], in1=t2[0:64, 0:n],
                                                   op=ALU.add), reads=[t1, t2], writes=[krr])
        k.op(k.dve, lambda e: e.tensor_copy(out=krr[:, NLAT:NT], in_=kr[:, NLAT:NT]), reads=[kr], writes=[krr])
        for (t0, n) in g_q:
            sq = sqp.next()
            k.op(k.act, lambda e: e.activation(out=sq[:, :, 0:n], in_=cq[:, :, t0:t0 + n], func=AF.Square),
                 reads=[cq], writes=[sq])
            ps = psr.next()
            k.mm([(ps.t[:, 0:n], C.ones_f[:], sq[:, kk, 0:n], kk == 0, kk == 3) for kk in range(4)],
                 reads=[sq, C.ones_f], writes=[ps])
            rstd_op(k, rq, rq[:, t0:t0 + n], ps, ps.t[:, 0:n], 512)
        for kk in range(4):
            k.op(k.dve, lambda e: e.tensor_scalar_mul(out=cqg[:, kk, :], in0=cq[:, kk, :],
                                                      scalar1=C.gq[:, l, kk:kk + 1]), reads=[cq, C.gq], writes=[cqg])
        khp = k.pool_of(es, "kh", [128, NT], BF16, 2, part=True)
        vhp = k.pool_of(es, "vh", [128, NT // 128, 128], BF16, 2, part=True)
        qnp = k.pool_of(es, "qn", [128, NQ], BF16, 2, part=True)
        qrp = k.pool_of(es, "qr", [64, NQ], BF16, 2, part=True)
        ptp = k.pool_of(es, "pt", [128, 512], BF16, 3)
        sgp = k.pool_of(es, "sgm", [128, 512], BF16, 2)
        ogp = k.pool_of(es, "ogm", [128, 512], BF16, 2)
        sr = Ring(k.psum[0:3])
        po, pd = k.psum[3], k.psum[4]
        for h in range(8):
            kh, vh, qn, qr = khp.next(), vhp.next(), qnp.next(), qrp.next()
            for (t0, n) in g_all:
                ps = psr.next()
                k.mm([(ps.t[:, 0:n], wukv[:, kk, h * 256:h * 256 + 128], ckg[:, kk, t0:t0 + n], kk == 0, kk == 1)
                      for kk in range(2)], reads=[wukv, ckg], writes=[ps])
                k.op(k.dve, lambda e: e.tensor_tensor(out=kh[:, t0:t0 + n], in0=ps.t[:, 0:n], in1=rkv[:, t0:t0 + n],
                                                      op=ALU.mult), reads=[ps, rkv], writes=[kh])
            for c4 in range(0, NT // 128, 4):
                nch = min(4, NT // 128 - c4)
                ps = psr.next()
                mms = []
                for ci in range(nch):
                    for kk in range(2):
                        mms.append((ps.t[:, ci * 128:(ci + 1) * 128], ckg[:, kk, (c4 + ci) * 128:(c4 + ci + 1) * 128],
                                    wukv[:, kk, h * 256 + 128:h * 256 + 256], kk == 0, kk == 1))
                k.mm(mms, reads=[wukv, ckg], writes=[ps])
                for ci in range(nch):
                    k.op(k.act, lambda e: e.activation(out=vh[:, c4 + ci, :], in_=ps.t[:, ci * 128:(ci + 1) * 128],
                                                       func=AF.Copy, scale=rkvT[:, c4 + ci:c4 + ci + 1]),
                         reads=[ps, rkvT], writes=[vh])
            for (t0, n) in g_q:
                ps = psr.next()
                k.mm([(ps.t[:, 0:n], wuq[:, kk, h * 192:h * 192 + 128], cqg[:, kk, t0:t0 + n], kk == 0, kk == 3)
                      for kk in range(4)], reads=[wuq, cqg], writes=[ps])
                k.op(k.dve, lambda e: e.tensor_tensor(out=qn[:, t0:t0 + n], in0=ps.t[:, 0:n], in1=rq[:, t0:t0 + n],
                                                      op=ALU.mult), reads=[ps, rq], writes=[qn])
                ps2 = psr.next()
                k.mm([(ps2.t[0:64, 0:n], wuq[:, kk, h * 192 + 128:h * 192 + 192], cqg[:, kk, t0:t0 + n],
                       kk == 0, kk == 3) for kk in range(4)], reads=[wuq, cqg], writes=[ps2])
                if t0 < lat_q:
                    qf = tmp.next()
                    k.op(k.dve, lambda e: e.tensor_tensor(out=qf[0:64, 0:n], in0=ps2.t[0:64, 0:n],
                                                          in1=rq[0:64, t0:t0 + n], op=ALU.mult),
                         reads=[ps2, rq], writes=[qf])
                    ps3 = psr.next()
                    k.mm([(ps3.t[0:64, 0:n], C.perm_f[:], qf[0:64, 0:n], True, True)], reads=[C.perm_f, qf],
                         writes=[ps3])
                    t1, t2 = tmp.next(), tmp.next()
                    k.op(k.pool, lambda e: e.tensor_tensor(out=t1[0:64, 0:n], in0=qf[0:64, 0:n],
                                                           in1=rope[:, 0, t0:t0 + n], op=ALU.mult),
                         reads=[qf, rope], writes=[t1])
                    k.op(k.dve, lambda e: e.tensor_tensor(out=t2[0:64, 0:n], in0=ps3.t[0:64, 0:n],
                                                          in1=rope[:, 1, t0:t0 + n], op=ALU.mult),
                         reads=[ps3, rope], writes=[t2])
                    k.op(k.pool, lambda e: e.tensor_tensor(out=qr[:, t0:t0 + n], in0=t1[0:64, 0:n], in1=t2[0:64, 0:n],
                                                           op=ALU.add), reads=[t1, t2], writes=[qr])
                else:
                    k.op(k.dve, lambda e: e.tensor_tensor(out=qr[:, t0:t0 + n], in0=ps2.t[0:64, 0:n],
                                                          in1=rq[0:64, t0:t0 + n], op=ALU.mult),
                         reads=[ps2, rq], writes=[qr])
            qblocks = [(q0, n, list(range(NT // 128))) for (q0, n) in split_groups([(0, lat_q)])]
            if ctxq:
                qblocks.append((NLAT, NCTX, [16, 17]))
            for (q0, nq, chunks) in qblocks:
                sg = sgp.next()
                k.dma(k.sp, sg[:, 0:nq], T.HT.t[C_GML + h * 128:C_GML + (h + 1) * 128, q0:q0 + nq], sg,
                      reads=[T.HT], writes=[sg])

                def qk(c):
                    ps = sr.next()
                    k.mm([(ps.t[:, 0:nq], kh[:, c * 128:(c + 1) * 128], qn[:, q0:q0 + nq], True, False),
                          (ps.t[:, 0:nq], krr[:, c * 128:(c + 1) * 128], qr[:, q0:q0 + nq], False, True)],
                         reads=[kh, krr, qn, qr], writes=[ps])
                    return ps
                cur = qk(chunks[0])
                for i, c in enumerate(chunks):
                    nxt = qk(chunks[i + 1]) if i + 1 < len(chunks) else None
                    pt = ptp.next()
                    k.op(k.act, lambda e: e.activation(out=pt[:, 0:nq], in_=cur.t[:, 0:nq], func=AF.Exp,
                                                       scale=MLA_SCALE), reads=[cur], writes=[pt])
                    last = (i == len(chunks) - 1)
                    k.mm([(po.t[:, 0:nq], vh[:, c, :], pt[:, 0:nq], i == 0, last),
                          (pd.t[:, 0:nq], C.ones_b[:], pt[:, 0:nq], i == 0, last)],
                         reads=[vh, pt, C.ones_b], writes=[po, pd])
                    cur = nxt
                rd, o = tmp.next(), tmp.next()
                k.op(k.dve, lambda e: e.reciprocal(out=rd[:, 0:nq], in_=pd.t[:, 0:nq]), reads=[pd], writes=[rd])
                k.op(k.dve, lambda e: e.tensor_tensor(out=o[:, 0:nq], in0=po.t[:, 0:nq], in1=rd[:, 0:nq], op=ALU.mult),
                     reads=[po, rd], writes=[o])
                og = ogp.next()
                k.op(k.pool, lambda e: e.tensor_tensor(out=og[:, 0:nq], in0=o[:, 0:nq], in1=sg[:, 0:nq], op=ALU.mult),
                     reads=[o, sg], writes=[og])
                k.dma(k.sp, T.OG.t[1, h * 128:(h + 1) * 128, q0:q0 + nq], og[:, 0:nq], og, reads=[og], writes=[T.OG])


def stage_na(k, T, C, l, lat_q, ctxq):
    NQ = lat_q + (NCTX if ctxq else 0)
    with k.scope() as es:
        namask = k.sb(es, "namask", [128, NA_NM, 128], BF16)
        k.dma(k.sp, namask[:], T.namask[:], namask, writes=[namask])
        khp = k.pool_of(es, "nk", [64, NT], BF16, 2)
        qhp = k.pool_of(es, "nq", [64, NQ], BF16, 2)
        sgp = k.pool_of(es, "ng", [64, NQ], BF16, 2)
        vpp = k.pool_of(es, "nv", [128, NT // 128, 128], BF16, 2)
        brp = k.pool_of(es, "nb", [128, 7, 128], F32, 2)
        lgp = k.pool_of(es, "nl", [128, 6, 128], F32, 3)
        ptp = k.pool_of(es, "np", [128, 8, 128], BF16, 3, part=True)
        ogp = k.pool_of(es, "no", [64, NQ], BF16, 2, part=True)
        tmp = k.pool_of(es, "nt", [64, 128], F32, 4)
        sA = Ring([k.psum[0], k.psum[2]])
        sB = Ring([k.psum[1], k.psum[3]])
        por = Ring(k.psum[4:8])
        items = [(j, j * 128) for j in range(lat_q // 128)]
        if ctxq:
            items += [(None, NLAT), (None, NLAT + 128)]
        vp = None
        for h in range(16):
            hh = h % 2
            if hh == 0:
                vp = vpp.next()
                k.dma(k.sp, vp[:], T.VNA.t[:, h * 64:h * 64 + 128].rearrange("(c p) f -> p c f", p=128), vp,
                      reads=[T.VNA], writes=[vp])
            kh, qh, sg, br = khp.next(), qhp.next(), sgp.next(), brp.next()
            k.dma(k.sp, kh[:], T.HT.t[C_NAK + h * 64:C_NAK + (h + 1) * 64, :], kh, reads=[T.HT], writes=[kh])
            k.dma(k.sp, qh[:], T.HT.t[C_NAQ + h * 64:C_NAQ + (h + 1) * 64, 0:NQ], qh, reads=[T.HT], writes=[qh])
            k.dma(k.sp, sg[:], T.HT.t[C_GNA + h * 64:C_GNA + (h + 1) * 64, 0:NQ], sg, reads=[T.HT], writes=[sg])
            k.dma(k.sp, br[:], T.brel[l, h], br, writes=[br])
            og = ogp.next()
            for (j, q0) in items:
                if j is not None:
                    slots = [((j + d) % 16, d) for d in na_slots(j)] + [(16, None), (17, None)]
                else:
                    slots = [(16, None), (17, None)]
                nlat = len(slots) - 2
                pA, pB = sA.next(), sB.next()
                mms = []
                for p, (ch, d) in enumerate(slots):
                    bank = pA if p < 4 else pB
                    col = (p % 4) * 128
                    mms.append((bank.t[:, col:col + 128], kh[:, ch * 128:(ch + 1) * 128], qh[:, q0:q0 + 128],
                                True, d is None))
                    if d is not None:
                        mms.append((bank.t[:, col:col + 128], C.ident_b[:], namask[:, NA_MIDX[(j, d)], :], False, True))
                k.mm(mms, reads=[kh, qh, C.ident_b, namask], writes=[pA, pB])
                pt = ptp.next()
                if nlat:
                    lg = lgp.next()
                    k.op(k.dve, lambda e: e.scalar_tensor_tensor(
                        out=lg[:, 0:4, :], in0=pA.t[:, 0:512].rearrange("p (s q) -> p s q", q=128), scalar=NA_SCALE,
                        in1=br[:, 1:5, :], op0=ALU.mult, op1=ALU.add), reads=[pA, br], writes=[lg])
                    k.op(k.dve, lambda e: e.scalar_tensor_tensor(
                        out=lg[:, 4, :], in0=pB.t[:, 0:128], scalar=NA_SCALE, in1=br[:, 5, :],
                        op0=ALU.mult, op1=ALU.add), reads=[pB, br], writes=[lg])
                    if nlat == 6:
                        bi = slots[5][1] + 3
                        k.op(k.dve, lambda e: e.scalar_tensor_tensor(
                            out=lg[:, 5, :], in0=pB.t[:, 128:256], scalar=NA_SCALE, in1=br[:, bi, :],
                            op0=ALU.mult, op1=ALU.add), reads=[pB, br], writes=[lg])
                    k.op(k.act, lambda e: e.activation(out=pt[:, 0:nlat, :], in_=lg[:, 0:nlat, :], func=AF.Exp),
                         reads=[lg], writes=[pt])
                    c0 = (nlat - 4) * 128
                    k.op(k.act, lambda e: e.activation(
                        out=pt[:, nlat:nlat + 2, :], in_=pB.t[:, c0:c0 + 256].rearrange("p (s q) -> p s q", q=128),
                        func=AF.Exp, scale=NA_SCALE), reads=[pB], writes=[pt])
                else:
                    k.op(k.act, lambda e: e.activation(
                        out=pt[:, 0:2, :], in_=pA.t[:, 0:256].rearrange("p (s q) -> p s q", q=128),
                        func=AF.Exp, scale=NA_SCALE), reads=[pA], writes=[pt])
                pso = por.next()
                ns = len(slots)
                mms = [(pso.t[0:64, 0:128], vp[:, ch, hh * 64:(hh + 1) * 64], pt[:, p, :], p == 0, p == ns - 1)
                       for p, (ch, d) in enumerate(slots)]
                mms += [(pso.t[0:64, 128:256], C.ones_b[:, 0:64], pt[:, p, :], p == 0, p == ns - 1)
                        for p in range(ns)]
                k.mm(mms, reads=[vp, pt, C.ones_b], writes=[pso])
                rd, o = tmp.next(), tmp.next()
                k.op(k.dve, lambda e: e.reciprocal(out=rd[:], in_=pso.t[0:64, 128:256]), reads=[pso], writes=[rd])
                k.op(k.dve, lambda e: e.tensor_tensor(out=o[:], in0=pso.t[0:64, 0:128], in1=rd[:], op=ALU.mult),
                     reads=[pso, rd], writes=[o])
                k.op(k.pool, lambda e: e.tensor_tensor(out=og[:, q0:q0 + 128], in0=o[:], in1=sg[:, q0:q0 + 128],
                                                       op=ALU.mult), reads=[o, sg], writes=[og])
            k.dma(k.sp, T.OG.t[2, h * 64:(h + 1) * 64, 0:NQ], og[:], og, reads=[og], writes=[T.OG])


def stage_fourier(k, T, C, l, lat_q, ctxq):
    with k.scope() as es:
        vf = k.sb(es, "vf", [128, 16, 1024], BF16)
        k.dma(k.sp, vf[:], T.VF.t[0:NLAT, :].rearrange("(c p) f -> p c f", p=128), vf, reads=[T.VF], writes=[vf])
        jobs = [("lat", vf, 16, T.dftN, k0, 512, k0) for k0 in range(0, lat_q, 512)]
        if ctxq:
            vfc = k.sb(es, "vfc", [128, 2, 1024], BF16)
            k.dma(k.sp, vfc[:], T.VF.t[NLAT:NT, :].rearrange("(c p) f -> p c f", p=128), vfc,
                  reads=[T.VF], writes=[vfc])
            jobs.append(("ctx", vfc, 2, T.dftC, 0, 256, NLAT))
        cnp = k.pool_of(es, "fcn", [128, 16, 512], BF16, 2)
        snp = k.pool_of(es, "fsn", [128, 16, 512], BF16, 2)
        z1p = k.pool_of(es, "fz1", [128, 8, 512], BF16, 2, part=True)
        z2p = k.pool_of(es, "fz2", [128, 8, 512], BF16, 2, part=True)
        sgp = k.pool_of(es, "fsg", [128, 8, 512], BF16, 2)
        ogp = k.pool_of(es, "fog", [128, 8, 512], BF16, 2, part=True)
        zr = Ring(k.psum[0:4])
        orr = Ring(k.psum[4:8])
        for (kind, vt, nch, tab, k0, nk, tok0) in jobs:
            cn, sn, sg = cnp.next(), snp.next(), sgp.next()
            k.dma(k.sp, cn[:, 0:nch, 0:nk], tab.t[0][:, k0:k0 + nk].rearrange("(c p) k -> p c k", p=128), cn,
                  writes=[cn])
            k.dma(k.sp, sn[:, 0:nch, 0:nk], tab.t[1][:, k0:k0 + nk].rearrange("(c p) k -> p c k", p=128), sn,
                  writes=[sn])
            k.dma(k.sp, sg[:, :, 0:nk], T.HT.t[C_GFN:C_GFN + 1024, tok0:tok0 + nk].rearrange("(c p) t -> p c t", p=128),
                  sg, reads=[T.HT], writes=[sg])
            z1, z2 = z1p.next(), z2p.next()
            for cc in range(8):
                ps = zr.next()
                k.mm([(ps.t[:, 0:nk], vt[:, n, cc * 128:(cc + 1) * 128], cn[:, n, 0:nk], n == 0, n == nch - 1)
                      for n in range(nch)], reads=[vt, cn], writes=[ps])
                k.op(k.dve, lambda e: e.tensor_copy(out=z1[:, cc, 0:nk], in_=ps.t[:, 0:nk]), reads=[ps], writes=[z1])
                ps2 = zr.next()
                k.mm([(ps2.t[:, 0:nk], vt[:, n, cc * 128:(cc + 1) * 128], sn[:, n, 0:nk], n == 0, n == nch - 1)
                      for n in range(nch)], reads=[vt, sn], writes=[ps2])
                k.op(k.act, lambda e: e.activation(out=z2[:, cc, 0:nk], in_=ps2.t[:, 0:nk], func=AF.Copy),
                     reads=[ps2], writes=[z2])
            og = ogp.next()
            for g in range(4):
                for h2 in range(2):
                    oc = 2 * g + h2
                    ps = orr.next()
                    mms = [(ps.t[:, 0:nk], C.dftM[:, 0, ci, h2 * 128:(h2 + 1) * 128], z1[:, 2 * g + ci, 0:nk],
                            ci == 0, False) for ci in range(2)]
                    mms += [(ps.t[:, 0:nk], C.dftM[:, 1, ci, h2 * 128:(h2 + 1) * 128], z2[:, 2 * g + ci, 0:nk],
                             False, ci == 1) for ci in range(2)]
                    k.mm(mms, reads=[C.dftM, z1, z2], writes=[ps])
                    k.op(k.dve, lambda e: e.tensor_tensor(out=og[:, oc, 0:nk], in0=ps.t[:, 0:nk], in1=sg[:, oc, 0:nk],
                                                          op=ALU.mult), reads=[ps, sg], writes=[og])
            k.dma(k.sp, T.OG.t[3, :, tok0:tok0 + nk].rearrange("(c p) t -> p c t", p=128), og[:, :, 0:nk], og,
                  reads=[og], writes=[T.OG])


def stage_epilogue(k, T, C, l, lat_q, ctxq, src, dst):
    NQ = lat_q + (NCTX if ctxq else 0)
    NB = 1152 if NQ > 1024 else 1024
    for tb0 in range(0, NQ, NB):
        groups = split_groups([(tb0, tb0 + NB)])
        with k.scope() as es1:
            mT = k.sb(es1, "mT", [128, KC, NB], BF16, part=True)
            with k.scope() as es2:
                ogt = k.sb(es2, "ogt", [128, 32, NB], BF16)
                wpp = k.pool_of(es2, "ewp", [128, 8, 512], BF16, 6)
                gtp = k.pool_of(es2, "egt", [128, 4, 512], BF16, 2)
                tmp = k.pool_of(es2, "etm", [128, 512], F32, 8)
                for i in range(4):
                    k.dma(k.sp, ogt[:, i * 8:(i + 1) * 8, :],
                          T.OG.t[i, :, tb0:tb0 + NB].rearrange("(c p) t -> p c t", p=128), ogt,
                          reads=[T.OG], writes=[ogt])
                sets = [k.psum[0:4], k.psum[4:8]]
                si = 0
                for cg in range(4):
                    wps = []
                    for i in range(4):
                        wt = wpp.next()
                        k.dma(k.pool, wt[:], T.w_p[l, i][:, cg * 512:(cg + 1) * 512].rearrange(
                            "(k p) c -> p k c", p=128), wt, writes=[wt])
                        wps.append(wt)
                    for s in range(4):
                        j = cg * 4 + s
                        for (t0, n) in groups:
                            gt = gtp.next()
                            k.dma(k.sp, gt[:, :, 0:n],
                                  T.HT.t[C_MG:C_MG + 4 * D, t0:t0 + n].rearrange("(i c) t -> c i t", i=4)[
                                      j * 128:(j + 1) * 128], gt, reads=[T.HT], writes=[gt])
                            pss = sets[si % 2]
                            si += 1
                            for i in range(4):
                                k.mm([(pss[i].t[:, 0:n], wps[i][:, kc, s * 128:(s + 1) * 128],
                                       ogt[:, i * 8 + kc, t0 - tb0:t0 - tb0 + n], kc == 0, kc == 7)
                                      for kc in range(8)], reads=[wps[i], ogt], writes=[pss[i]])
                            a = [tmp.next() for _ in range(4)]
                            for i in range(4):
                                k.op(k.dve, lambda e: e.tensor_tensor(out=a[i][:, 0:n], in0=pss[i].t[:, 0:n],
                                                                      in1=gt[:, i, 0:n], op=ALU.mult),
                                     reads=[pss[i], gt], writes=[a[i]])
                            k.op(k.pool, lambda e: e.tensor_tensor(out=a[0][:, 0:n], in0=a[0][:, 0:n],
                                                                   in1=a[1][:, 0:n], op=ALU.add),
                                 reads=[a[0], a[1]], writes=[a[0]])
                            k.op(k.pool, lambda e: e.tensor_tensor(out=a[2][:, 0:n], in0=a[2][:, 0:n],
                                                                   in1=a[3][:, 0:n], op=ALU.add),
                                 reads=[a[2], a[3]], writes=[a[2]])
                            k.op(k.pool, lambda e: e.tensor_tensor(out=mT[:, j, t0 - tb0:t0 - tb0 + n],
                                                                   in0=a[0][:, 0:n], in1=a[2][:, 0:n], op=ALU.add),
                                 reads=[a[0], a[2]], writes=[mT])
            yT = k.sb(es1, "yT", [128, KC, NB], F32, part=True)
            psr = Ring(k.psum)
            with k.scope() as es3:
                wop = k.pool_of(es3, "ewo", [128, KC, 512], BF16, 2)
                for cg in range(4):
                    wt = wop.next()
                    k.dma(k.pool, wt[:], T.w_out[l][:, cg * 512:(cg + 1) * 512].rearrange("(k p) c -> p k c", p=128),
                          wt, writes=[wt])
                    for s in range(4):
                        j = cg * 4 + s
                        for (t0, n) in groups:
                            ps = psr.next()
                            k.mm([(ps.t[:, 0:n], wt[:, kc, s * 128:(s + 1) * 128], mT[:, kc, t0 - tb0:t0 - tb0 + n],
                                   kc == 0, kc == KC - 1) for kc in range(KC)], reads=[wt, mT], writes=[ps])
                            k.op(k.act, lambda e: e.activation(out=yT[:, j, t0 - tb0:t0 - tb0 + n], in_=ps.t[:, 0:n],
                                                               func=AF.Copy), reads=[ps], writes=[yT])
            with k.scope() as es4:
                sqp = k.pool_of(es4, "esq", [128, KC, 256], F32, 1)
                xtp = k.pool_of(es4, "ext", [128, KC, 256], F32, 2)
                otp = k.pool_of(es4, "eot", [128, KC, 256], F32, 2)
                rsp = k.pool_of(es4, "ers", [128, 256], F32, 2)
                tmp = k.pool_of(es4, "etn", [128, 256], F32, 4)
                for (t0, n) in split_groups([(tb0, tb0 + NB)], 256):
                    v = 0 if t0 < NLAT else 1
                    lo = t0 - tb0
                    xt = xtp.next()
                    k.dma(k.sp, xt[:, :, 0:n], src.t[:, t0:t0 + n].rearrange("(k p) t -> p k t", p=128), xt,
                          reads=[src], writes=[xt])
                    sq = sqp.next()
                    k.op(k.act, lambda e: e.activation(out=sq[:, :, 0:n], in_=yT[:, :, lo:lo + n], func=AF.Square),
                         reads=[yT], writes=[sq])
                    ps = psr.next()
                    k.mm([(ps.t[:, 0:n], C.ones_f[:], sq[:, kc, 0:n], kc == 0, kc == KC - 1) for kc in range(KC)],
                         reads=[sq, C.ones_f], writes=[ps])
                    rs = rsp.next()
                    rstd_op(k, rs, rs[:, 0:n], ps, ps.t[:, 0:n], D)
                    ot = otp.next()
                    for kc in range(KC):
                        tt = tmp.next()
                        k.op(k.pool, lambda e: e.tensor_tensor(out=tt[:, 0:n], in0=yT[:, kc, lo:lo + n],
                                                               in1=rs[:, 0:n], op=ALU.mult),
                             reads=[yT, rs], writes=[tt])
                        k.op(k.dve, lambda e: e.scalar_tensor_tensor(
                            out=ot[:, kc, 0:n], in0=tt[:, 0:n], scalar=C.G[:, l, v, kc:kc + 1], in1=xt[:, kc, 0:n],
                            op0=ALU.mult, op1=ALU.add), reads=[tt, xt, C.G], writes=[ot])
                    k.dma(k.sp, dst.t[:, t0:t0 + n].rearrange("(k p) t -> p k t", p=128), ot[:, :, 0:n], ot,
                          reads=[ot], writes=[dst])
```

```python
from contextlib import ExitStack
import numpy as np
import ml_dtypes
import concourse.bass as bass
import concourse.mybir as mybir
from concourse.bass_utils import run_bass_kernel_spmd

F32 = mybir.dt.float32
BF16 = mybir.dt.bfloat16
AF = mybir.ActivationFunctionType
ALU = mybir.AluOpType

D = 2048
NLAT = 2048
NCTX = 256
NT = NLAT + NCTX
KC = D // 128
N_IN = 20288
EPS = 1e-6
GRID_W = 64
MLA_SCALE = 192.0 ** -0.5
NA_SCALE = 0.125
NEG = -30000.0
NWT = 40

C_KV0, C_NAK, C_NAV, C_CQ, C_NAQ, C_CB, C_CC, C_CX, C_FV = 0, 320, 1344, 2368, 2880, 3904, 4928, 5952, 6976
C_GCV, C_GML, C_GNA, C_GFN, C_MG = 8000, 9024, 10048, 11072, 12096


class Eng:
    def __init__(self, nc, name, h, is_pe=False):
        self.nc, self.name, self.h, self.is_pe = nc, name, h, is_pe
        self.sem = nc.alloc_semaphore("pg_" + name)
        self.n = 0
        self.seen = {}

    def wait(self, ev):
        sem, val = ev
        if self.is_pe and sem is self.sem:
            return
        if self.seen.get(sem, 0) >= val:
            return
        self.h.wait_ge(sem, val)
        self.seen[sem] = val

    def mark(self, inst):
        self.n += 1
        inst.then_inc(self.sem, 1)
        return (self.sem, self.n)


class Buf:
    def __init__(self, t, name="", part=False):
        self.t = t
        self.name = name
        self.w = {}
        self.r = {}
        self.part = part
        self.dsem = None
        self.dcnt = 0

    def __getitem__(self, k):
        return self.t[k]


class K:
    def __init__(self, nc):
        self.nc = nc
        self.pe = Eng(nc, "pe", nc.tensor, True)
        self.act = Eng(nc, "act", nc.scalar)
        self.dve = Eng(nc, "dve", nc.vector)
        self.pool = Eng(nc, "pool", nc.gpsimd)
        self.sp = Eng(nc, "sp", nc.sync)
        self.nsem = 0
        self.psum = []
        self.psi = 0
        self.free_sems = []
        self.scopes = []

    def _pre(self, eng, reads, writes):
        for b in reads:
            for ev in b.w.values():
                eng.wait(ev)
        for b in writes:
            if not b.part:
                for ev in b.w.values():
                    eng.wait(ev)
            for ev in b.r.values():
                eng.wait(ev)

    def _post(self, ev, reads, writes):
        for b in reads:
            b.r[ev[0]] = ev
        for b in writes:
            if b.part:
                b.w[ev[0]] = ev
            else:
                b.w = {ev[0]: ev}
                b.r = {}

    def op(self, eng, fn, reads=(), writes=()):
        self._pre(eng, reads, writes)
        inst = fn(eng.h)
        ev = eng.mark(inst)
        self._post(ev, reads, writes)

    def mm(self, mms, reads=(), writes=()):
        eng = self.pe
        self._pre(eng, reads, writes)
        inst = None
        for mmv in mms:
            (o, l, r, st, sp) = mmv[:5]
            if len(mmv) > 5:
                inst = self.nc.tensor.matmul(o, l, r, start=st, stop=sp, skip_group_check=True)
            else:
                inst = self.nc.tensor.matmul(o, l, r, start=st, stop=sp)
        ev = eng.mark(inst)
        self._post(ev, reads, writes)

    def dma(self, q, out, in_, owner, reads=(), writes=(), slow=False):
        self._pre(q, reads, writes)
        if owner.dsem is None:
            if self.free_sems:
                owner.dsem, owner.dcnt = self.free_sems.pop()
                q.wait((owner.dsem, owner.dcnt))
            else:
                owner.dsem = self.nc.alloc_semaphore("d%d" % self.nsem)
                self.nsem += 1
        if slow:
            inst = q.h.dma_start(out=out, in_=in_, allow_slow_non_contiguous=True)
        else:
            inst = q.h.dma_start(out=out, in_=in_)
        owner.dcnt += 16
        inst.then_inc(owner.dsem, 16)
        ev = (owner.dsem, owner.dcnt)
        self._post(ev, reads, writes)

    def sb(self, es, name, shape, dt, part=False):
        self.nsb = getattr(self, "nsb", 0) + 1
        t = es.enter_context(self.nc.sbuf_tensor("s%d_%s" % (self.nsb, name), list(shape), dt))
        b = Buf(t, name, part)
        if self.scopes:
            self.scopes[-1].append(b)
        return b

    def scope(self):
        return _Scope(self)

    def pool_of(self, es, name, shape, dt, n, part=False):
        return Ring([self.sb(es, "%s%d" % (name, i), shape, dt, part) for i in range(n)])

    def ps(self):
        b = self.psum[self.psi % len(self.psum)]
        self.psi += 1
        return b


class _Scope:
    def __init__(self, k):
        self.k = k
        self.es = ExitStack()

    def __enter__(self):
        self.k.scopes.append([])
        self.es.__enter__()
        return self.es

    def __exit__(self, *a):
        k = self.k
        bufs = k.scopes.pop()
        evs = {}
        for b in bufs:
            for d in (b.w, b.r):
                for (sem, val) in d.values():
                    if evs.get(sem, (None, 0))[1] < val:
                        evs[sem] = (sem, val)
        for eng in (k.pe, k.act, k.dve, k.pool, k.sp):
            for ev in evs.values():
                if not (ev[0] is eng.sem):
                    eng.wait(ev)
                elif eng.is_pe:
                    pass
        for b in bufs:
            if b.dsem is not None:
                k.free_sems.append((b.dsem, b.dcnt))
        return self.es.__exit__(*a)


class Ring:
    def __init__(self, bufs):
        self.bufs = bufs
        self.i = 0

    def next(self):
        b = self.bufs[self.i % len(self.bufs)]
        self.i += 1
        return b


def split_groups(ranges, n=512):
    out = []
    for (a, b) in ranges:
        t = a
        while t < b:
            e = min(b, (t // n + 1) * n)
            out.append((t, e - t))
            t = e
    return out


from contextlib import ExitStack


class Cst:
    pass


def rstd_op(k, out_buf, out_ap, ps_buf, ps_ap, nfeat):
    k.op(k.dve, lambda e: e.tensor_scalar(out=out_ap, in0=ps_ap, scalar1=1.0 / nfeat, scalar2=EPS,
                                          op0=ALU.mult, op1=ALU.add), reads=[ps_buf], writes=[out_buf])
    k.op(k.act, lambda e: e.activation(out=out_ap, in_=out_ap, func=AF.Sqrt), reads=[out_buf], writes=[out_buf])
    k.op(k.dve, lambda e: e.reciprocal(out=out_ap, in_=out_ap), reads=[out_buf], writes=[out_buf])


def na_pairs(u):
    jlo = max(0, u - 2) - (1 if u == 3 else 0)
    jhi = min(7, u + 2) + (1 if u == 4 else 0)
    return list(range(jlo, jhi + 1))


def na_mask_index():
    idx = {}
    n = 0
    for qh in range(2):
        for u in range(-2, 10):
            for jr in na_pairs(u):
                idx[(qh, u, jr)] = n
                n += 1
    return idx, n


NA_MIDX, NA_NM = na_mask_index()


def build_program(layers=(0, 1), taps=(), stop_after=None, x1_in=False):
    nc = bass.Bass("TRN2", target_bir_lowering=False)
    k = K(nc)

    def din(name, shape, dt=F32):
        return Buf(nc.dram_tensor(name, list(shape), dt, kind="ExternalInput").ap(), name, part=True)

    def dscr(name, shape, dt):
        kind = "ExternalOutput" if name in taps else "Internal"
        return Buf(nc.dram_tensor(name, list(shape), dt, kind=kind).ap(), name, part=True)

    T = Cst()
    T.xT = din("xT", [D, NT])
    T.cvec = din("cvec", [128, 16, 2])
    T.w_ada = din("w_ada", [2, D, 3 * D])
    T.b_ada = din("b_ada", [2, 2, 3 * D])
    T.w_in = din("w_in", [2, D, N_IN])
    T.gpre = din("gpre", [128, 2, 16])
    T.gpost = din("gpost", [128, 2, 16])
    T.gq = din("gq", [128, 2, 4])
    T.gkv = din("gkv", [128, 2, 2])
    T.w_uq = din("w_uq", [2, 512, 1536])
    T.w_ukv = din("w_ukv", [2, 256, 2048])
    T.convw = din("convw", [128, 2, 8, 3])
    T.brel = din("brel", [2, 16, 128, 7, 128])
    T.w_p = din("w_p", [2, 4, 1024, D])
    T.w_out = din("w_out", [2, D, D])
    T.ident = din("ident", [128, 128])
    T.perm = din("perm", [64, 64])
    T.rope = din("rope", [64, 2, NLAT])
    T.namask = din("namask", [128, NA_NM, 128], BF16)
    T.cmask = din("cmask", [128, 2, NLAT])
    T.dftN = din("dftN", [2, NLAT, NLAT], BF16)
    T.dftC = din("dftC", [2, NCTX, NCTX], BF16)
    T.dftM = din("dftM", [2, 256, 256], BF16)
    T.outT = Buf(nc.dram_tensor("outT", [D, 1024], F32, kind="ExternalOutput").ap(), "outT", part=True)
    T.HT = dscr("HT", [N_IN, NT], BF16)
    T.VNA = dscr("VNA", [NT, 1024], BF16)
    T.VF = dscr("VF", [NT, 1024], BF16)
    T.OG = dscr("OG", [4, 1024, NT], BF16)
    T.X1 = dscr("X1", [D, NT], F32)
    T.UTd = dscr("UTd", [D, NT], BF16) if "UTd" in taps else None
    T.MODd = dscr("MODd", [128, 2, 48, 2], F32) if "MODd" in taps else None

    owners = []
    _dma = k.dma

    def dma_reg(q, out, in_, owner, reads=(), writes=(), slow=False):
        if owner not in owners:
            owners.append(owner)
        _dma(q, out, in_, owner, reads, writes, slow)
    k.dma = dma_reg

    with ExitStack() as top:
        for i in range(8):
            k.psum.append(Buf(top.enter_context(nc.psum_tensor("psb%d" % i, [128, 512], F32)), "ps%d" % i))
        C = Cst()
        C.ones_f = k.sb(top, "ones_f", [128, 128], F32)
        C.ones_b = k.sb(top, "ones_b", [128, 128], BF16)
        C.ident_b = k.sb(top, "ident_b", [128, 128], BF16)
        C.perm_b = k.sb(top, "perm_b", [64, 64], BF16)
        C.perm_f = k.sb(top, "perm_f", [64, 64], F32)
        C.dftM = k.sb(top, "dftM", [128, 2, 2, 256], BF16)
        C.mod = k.sb(top, "mod", [128, 2, 48, 2], F32, part=True)
        C.A = k.sb(top, "modA", [128, 2, 2, 16], F32, part=True)
        C.G = k.sb(top, "modG", [128, 2, 2, 16], F32, part=True)
        C.gpre = k.sb(top, "gpre", [128, 2, 16], F32)
        C.gpost = k.sb(top, "gpost", [128, 2, 16], F32)
        C.gq = k.sb(top, "gq", [128, 2, 4], F32)
        C.gkv = k.sb(top, "gkv", [128, 2, 2], F32)
        C.convw = k.sb(top, "convw", [128, 2, 8, 3], F32)
        C.scv = k.sb(top, "scv", [128, 16, 2], F32)
        C.i2 = k.sb(top, "i2", [2, 2], F32)

        k.op(k.dve, lambda e: e.memset(C.ones_f[:], 1.0), writes=[C.ones_f])
        k.op(k.dve, lambda e: e.memset(C.ones_b[:], 1.0), writes=[C.ones_b])
        k.dma(k.pool, C.ident_b[:], T.ident[:], C.ident_b, writes=[C.ident_b])
        k.dma(k.pool, C.perm_b[:], T.perm[:], C.perm_b, writes=[C.perm_b])
        k.dma(k.sp, C.perm_f[:], T.perm[:], C.perm_f, writes=[C.perm_f])
        k.dma(k.sp, C.i2[:], T.ident.t[0:2, 0:2], C.i2, writes=[C.i2])
        for cs in range(2):
            k.dma(k.sp, C.dftM[:, cs, :, :], T.dftM[cs].rearrange("(k p) c -> p k c", p=128), C.dftM,
                  writes=[C.dftM])
        for (dst, src) in ((C.gpre, T.gpre), (C.gpost, T.gpost), (C.gq, T.gq), (C.gkv, T.gkv), (C.convw, T.convw)):
            k.dma(k.sp, dst[:], src[:], dst, writes=[dst])

        stages = []

        def run(name, fn, *a):
            if stages and stages[-1] == "__stop__":
                return
            globals()[fn](k, T, C, *a)
            stages.append(name)
            if stop_after == name:
                stages.append("__stop__")

        run("ada", "stage_ada", 0)
        if T.MODd is not None and "__stop__" in stages[-1:]:
            pass
        for l in layers:
            lat_q = NLAT if l == 0 else 1024
            ctxq = (l == 0)
            src = T.xT if (l == 0 or x1_in) else T.X1
            with k.scope() as es_u:
                UT = k.sb(es_u, "UT", [128, KC, NT], BF16, part=True)
                run("prenorm%d" % l, "stage_prenorm", l, UT, src)
                run("inproj%d" % l, "stage_inproj", l, UT, lat_q, ctxq)
            if l == 0 and 1 in layers:
                run("ada1", "stage_ada", 1)
            run("conv%d" % l, "stage_conv", l, lat_q, ctxq)
            run("mla%d" % l, "stage_mla", l, lat_q, ctxq)
            run("na%d" % l, "stage_na", l, lat_q, ctxq)
            run("four%d" % l, "stage_fourier", l, lat_q, ctxq)
            dst = T.X1 if l == 0 else T.outT
            run("epi%d" % l, "stage_epilogue", l, lat_q, ctxq, src, dst)

        if T.MODd is not None:
            k.dma(k.sp, T.MODd[:], C.mod[:], C.mod, reads=[C.mod], writes=[T.MODd])
        for ob in owners:
            k.sp.wait((ob.dsem, ob.dcnt))
    return nc


def stage_ada(k, T, C, l):
    with k.scope() as es:
        if l == 0:
            cv = k.sb(es, "cv", [128, 16, 2], F32)
            k.dma(k.sp, cv[:], T.cvec[:], cv, writes=[cv])
            k.op(k.act, lambda e: e.activation(out=C.scv[:], in_=cv[:], func=AF.Silu), reads=[cv], writes=[C.scv])
        wpool = k.pool_of(es, "wada", [128, 16, 512], F32, 2)
        bad2 = k.sb(es, "bad2", [2, 3 * D], F32)
        mod2 = k.sb(es, "mod2", [2, 3 * D], F32, part=True)
        k.dma(k.sp, bad2[:], T.b_ada[l], bad2, writes=[bad2])
        psr = Ring(k.psum[0:4])
        for cg in range(12):
            wt = wpool.next()
            k.dma(k.sp, wt[:], T.w_ada[l][:, cg * 512:(cg + 1) * 512].rearrange("(k p) c -> p k c", p=128),
                  wt, writes=[wt])
            ps = psr.next()
            k.mm([(ps.t[0:2, 0:512], C.scv[:, kk, :], wt[:, kk, :], kk == 0, kk == 15) for kk in range(16)],
                 reads=[wt, C.scv], writes=[ps])
            k.op(k.dve, lambda e: e.tensor_tensor(out=mod2[:, cg * 512:(cg + 1) * 512], in0=ps.t[0:2, 0:512],
                                                  in1=bad2[:, cg * 512:(cg + 1) * 512], op=ALU.add),
                 reads=[ps, bad2], writes=[mod2])
        pm = k.psum[4]
        k.mm([(pm.t[:, 2 * m:2 * m + 2], mod2[0:2, m * 128:(m + 1) * 128], C.i2[:], True, True) for m in range(48)],
             reads=[mod2, C.i2], writes=[pm])
        k.op(k.dve, lambda e: e.tensor_copy(out=C.mod[:, l, :, :],
                                            in_=pm.t[:, 0:96].rearrange("p (m v) -> p m v", v=2)),
             reads=[pm], writes=[C.mod])
        for v in range(2):
            k.op(k.dve, lambda e: e.scalar_tensor_tensor(out=C.A[:, l, v, :], in0=C.mod[:, l, 16:32, v],
                                                         scalar=1.0, in1=C.gpre[:, l, :],
                                                         op0=ALU.add, op1=ALU.mult),
                 reads=[C.mod, C.gpre], writes=[C.A])
            k.op(k.dve, lambda e: e.tensor_tensor(out=C.G[:, l, v, :], in0=C.mod[:, l, 32:48, v],
                                                  in1=C.gpost[:, l, :], op=ALU.mult),
                 reads=[C.mod, C.gpost], writes=[C.G])


def stage_prenorm(k, T, C, l, UT, src):
    TB = 256
    with k.scope() as es:
        xp = k.pool_of(es, "xn", [128, KC, TB], F32, 2)
        sqp = k.pool_of(es, "sqn", [128, KC, TB], F32, 2)
        rsp = k.pool_of(es, "rsn", [128, TB], F32, 2)
        psr = Ring(k.psum[2:6])
        for blk in range(NT // TB):
            t0 = blk * TB
            v = 0 if t0 < NLAT else 1
            xt = xp.next()
            k.dma(k.sp, xt[:], src.t[:, t0:t0 + TB].rearrange("(k p) t -> p k t", p=128), xt,
                  reads=[src], writes=[xt])
            sq = sqp.next()
            k.op(k.act, lambda e: e.activation(out=sq[:], in_=xt[:], func=AF.Square), reads=[xt], writes=[sq])
            ps = psr.next()
            k.mm([(ps.t[:, 0:TB], C.ones_f[:], sq[:, kk, :], kk == 0, kk == KC - 1) for kk in range(KC)],
                 reads=[sq, C.ones_f], writes=[ps])
            rs = rsp.next()
            rstd_op(k, rs, rs[:], ps, ps.t[:, 0:TB], D)
            rb = rs[:].unsqueeze(1).to_broadcast([128, KC, TB])
            ab = C.A[:, l, v, :].unsqueeze(2).to_broadcast([128, KC, TB])
            bb = C.mod[:, l, 0:KC, v].unsqueeze(2).to_broadcast([128, KC, TB])
            k.op(k.dve, lambda e: e.tensor_tensor(out=xt[:], in0=xt[:], in1=rb, op=ALU.mult),
                 reads=[xt, rs], writes=[xt])
            k.op(k.pool, lambda e: e.tensor_tensor(out=xt[:], in0=xt[:], in1=ab, op=ALU.mult),
                 reads=[xt, C.A], writes=[xt])
            k.op(k.dve, lambda e: e.tensor_tensor(out=UT[:, :, t0:t0 + TB], in0=xt[:], in1=bb, op=ALU.add),
                 reads=[xt, C.mod], writes=[UT])
        if T.UTd is not None:
            k.dma(k.sp, T.UTd.t.rearrange("(k p) t -> p k t", p=128), UT[:], UT, reads=[UT], writes=[T.UTd])


def inproj_plan(l, lat_q, ctxq):
    allr = [(0, NT)]
    own = [(0, NT)] if ctxq else [(0, lat_q)]
    halo = own if ctxq else own + [(lat_q, lat_q + 2), (NLAT - 2, NLAT)]
    fvr = [(0, NT)] if ctxq else [(0, NLAT)]
    return [
        (0, C_NAV, "FM", "copy", allr, None),
        (C_NAV, C_CQ, "TM", "copy", allr, "VNA"),
        (C_CQ, C_CC, "FM", "copy", own, None),
        (C_CC, C_FV, "FM", "copy", halo, None),
        (C_FV, C_GCV, "TM", "copy", fvr, "VF"),
        (C_GCV, C_MG, "FM", "silu", own, None),
        (C_MG, N_IN, "FM", "sigm", own, None),
    ]


def stage_inproj(k, T, C, l, UT, lat_q, ctxq):
    plan = inproj_plan(l, lat_q, ctxq)
    with k.scope() as es:
        wp = k.pool_of(es, "win", [128, KC, 512], BF16, 3)
        otp = k.pool_of(es, "ot", [128, NT], BF16, 3, part=True)
        vtp = k.pool_of(es, "vt", [128, 512], BF16, 4)
        psr = Ring(k.psum)
        for wi in range(NWT):
            c0 = 0 if wi == 0 else 320 + 512 * (wi - 1)
            wd = 320 if wi == 0 else 512
            ent = [p for p in plan if p[0] <= c0 < p[1]][0]
            _, lo_, mode, func, ranges, dname = ent
            wt = wp.next()
            k.dma(k.pool, wt[:, :, 0:wd], T.w_in[l][:, c0:c0 + wd].rearrange("(k p) c -> p k c", p=128), wt,
                  writes=[wt])
            if mode == "FM":
                for s0 in range(0, wd, 128):
                    ncp = min(128, wd - s0)
                    ot = otp.next()
                    for (t0, n) in split_groups(ranges):
                        ps = psr.next()
                        k.mm([(ps.t[0:ncp, 0:n], wt[:, kk, s0:s0 + ncp], UT[:, kk, t0:t0 + n], kk == 0, kk == KC - 1)
                              for kk in range(KC)], reads=[wt, UT], writes=[ps])
                        if func == "copy":
                            k.op(k.dve, lambda e: e.tensor_copy(out=ot[0:ncp, t0:t0 + n], in_=ps.t[0:ncp, 0:n]),
                                 reads=[ps], writes=[ot])
                        else:
                            f = AF.Silu if func == "silu" else AF.Sigmoid
                            k.op(k.act, lambda e: e.activation(out=ot[0:ncp, t0:t0 + n], in_=ps.t[0:ncp, 0:n], func=f),
                                 reads=[ps], writes=[ot])
                    for (a, b) in ranges:
                        k.dma(k.sp, T.HT.t[c0 + s0:c0 + s0 + ncp, a:b], ot[0:ncp, a:b], ot, reads=[ot], writes=[T.HT])
            else:
                dest = T.VNA if dname == "VNA" else T.VF
                cc0 = c0 - ent[0]
                for (a, b) in ranges:
                    for tc in range(a, b, 128):
                        ps = psr.next()
                        k.mm([(ps.t[:, 0:512], UT[:, kk, tc:tc + 128], wt[:, kk, :], kk == 0, kk == KC - 1)
                              for kk in range(KC)], reads=[wt, UT], writes=[ps])
                        vt = vtp.next()
                        k.op(k.dve, lambda e: e.tensor_copy(out=vt[:], in_=ps.t[:]), reads=[ps], writes=[vt])
                        k.dma(k.sp, dest.t[tc:tc + 128, cc0:cc0 + 512], vt[:], vt, reads=[vt], writes=[dest])


_CONST_CACHE = {}


def host_consts(hf):
    if hf in _CONST_CACHE:
        return _CONST_CACHE[hf]
    bf = ml_dtypes.bfloat16
    c = {}
    c["ident"] = np.eye(128, dtype=np.float32)
    perm = np.zeros((64, 64), np.float32)
    for dp in range(64):
        sw = dp + 16 if (dp % 32) // 16 == 0 else dp - 16
        perm[sw, dp] = 1.0
    c["perm"] = perm
    i = np.arange(NLAT)
    t = (i + 1024 * hf) % NLAT
    pos = np.stack([t // GRID_W, t % GRID_W]).astype(np.float32)
    nf = 16
    inv = (np.float32(10000.0) ** (-np.arange(nf, dtype=np.float32) / np.float32(nf))).astype(np.float32)
    rope = np.zeros((64, 2, NLAT), np.float32)
    for d in range(64):
        half, ab, f = d // 32, (d % 32) // 16, d % 16
        ang = (pos[half] * inv[f]).astype(np.float32)
        rope[d, 0] = np.cos(ang)
        rope[d, 1] = -np.sin(ang) if ab == 0 else np.sin(ang)
    c["rope"] = rope
    m = np.full((128, NA_NM, 128), NEG, np.float32)
    kk = np.arange(128)
    for (qh, u, jr), mi in NA_MIDX.items():
        j = 8 * qh + jr
        d = u - jr
        G = (j + 8 * hf) % 16
        Gk = G + d
        if not (0 <= Gk <= 15):
            continue
        kr = 2 * Gk + kk // 64
        kc = kk % 64
        r = 2 * G + kk // 64
        cc = kk % 64
        rs = np.clip(r - 4, 0, 24)
        cs = np.clip(cc - 8, 0, 48)
        valid = ((kr[:, None] >= rs[None, :]) & (kr[:, None] < rs[None, :] + 8) &
                 (kc[:, None] >= cs[None, :]) & (kc[:, None] < cs[None, :] + 16))
        m[:, mi, :] = np.where(valid, 0.0, NEG)
    c["namask"] = m.astype(bf)
    cm = np.ones((128, 2, NLAT), np.float32)
    cm[:, 0, t == 0] = 0.0
    cm[:, 1, t == NLAT - 1] = 0.0
    c["cmask"] = cm
    prod = (t[:, None].astype(np.int64) * t[None, :].astype(np.int64)) % NLAT
    ang = prod.astype(np.float64) * (2.0 * np.pi / NLAT)
    c["dftN"] = np.stack([np.cos(ang), np.sin(ang)]).astype(np.float32) / np.float32(np.sqrt(NLAT))
    c["dftN"] = c["dftN"].astype(bf)
    n = np.arange(256)
    ang = ((n[:, None] * n[None, :]) % 256).astype(np.float64) * (2.0 * np.pi / 256)
    c["dftC"] = (np.stack([np.cos(ang), np.sin(ang)]) / 16.0).astype(np.float32).astype(bf)
    c["dftM"] = (np.stack([np.cos(ang), -np.sin(ang)]) / 16.0).astype(np.float32).astype(bf)
    _CONST_CACHE[hf] = c
    return c


def brel_table(na_rpb):
    kk = np.arange(128)
    out = np.zeros((2, 16, 128, 7, 128), np.float32)
    for si in range(7):
        d = 3 - si
        dr = 2 * d + (kk // 64)[:, None] - (kk // 64)[None, :]
        dc = (kk % 64)[:, None] - (kk % 64)[None, :]
        ok = (np.abs(dr) <= 7) & (np.abs(dc) <= 15)
        ri = np.clip(dr + 7, 0, 14)
        ci = np.clip(dc + 15, 0, 30)
        g = na_rpb[:, :, ri, ci]
        out[:, :, :, si, :] = np.where(ok[None, None], g, np.float32(0.0))
    return out


def prep_inputs(inp, cores=range(8)):
    f = np.float32

    def pk(a, k):
        return np.ascontiguousarray(np.moveaxis(a.reshape(a.shape[:-1] + (k, 128)), -1, 0))

    shared = {
        "w_ada": np.ascontiguousarray(inp["w_ada"], f),
        "b_ada": np.ascontiguousarray(np.stack([inp["b_ada"], inp["b_ada"]], axis=1), f),
        "w_in": np.ascontiguousarray(inp["w_in"], f),
        "gpre": pk(inp["g_pre"], 16), "gpost": pk(inp["g_post"], 16),
        "gq": pk(inp["g_q"], 4), "gkv": pk(inp["g_kv"], 2),
        "w_uq": np.ascontiguousarray(inp["w_uq"], f), "w_ukv": np.ascontiguousarray(inp["w_ukv"], f),
        "convw": np.ascontiguousarray(inp["conv_w"].reshape(2, 3, 8, 128).transpose(3, 0, 2, 1), f),
        "brel": brel_table(np.asarray(inp["na_rpb"], f)),
        "w_p": np.ascontiguousarray(np.stack([inp["w_p_conv"], inp["w_p_mla"], inp["w_p_na"], inp["w_p_fnet"]],
                                             axis=1), f),
        "w_out": np.ascontiguousarray(inp["w_out"], f),
    }
    maps = []
    for cid in cores:
        b, hf = cid // 2, cid % 2
        m = dict(shared)
        m.update(host_consts(hf))
        xl = np.roll(np.asarray(inp["x"][b], f), -1024 * hf, axis=0)
        m["xT"] = np.ascontiguousarray(np.concatenate([xl, np.asarray(inp["ctx"][b], f)], axis=0).T)
        cv = np.stack([np.asarray(inp["c"][b], f), np.asarray(inp["c_ctx"], f)], axis=-1)
        m["cvec"] = np.ascontiguousarray(cv.reshape(16, 128, 2).transpose(1, 0, 2))
        maps.append(m)
    return maps


_PROG = {}


def kernel(**inputs):
    maps = prep_inputs(inputs)
    if "full" not in _PROG:
        _PROG["full"] = build_program()
    res = run_bass_kernel_spmd(_PROG["full"], maps, core_ids=list(range(8)))
    out = np.zeros((4, NLAT, D), np.float32)
    for cid in range(8):
        b, hf = cid // 2, cid % 2
        out[b, 1024 * hf:1024 * hf + 1024, :] = res.results[cid]["outT"].T
    return out


def stage_conv(k, T, C, l, lat_q, ctxq):
    segs = [(0, lat_q, NLAT - 1, lat_q % NLAT, True)]
    if ctxq:
        segs.append((NLAT, NCTX, None, None, False))
    with k.scope() as es:
        cmask = k.sb(es, "cmask", [128, 2, NLAT], F32)
        k.dma(k.sp, cmask[:], T.cmask[:], cmask, writes=[cmask])
        WM = lat_q
        ccp = k.pool_of(es, "cvc", [128, WM + 2], BF16, 2, part=True)
        cxp = k.pool_of(es, "cvx", [128, WM + 2], BF16, 2, part=True)
        cbp = k.pool_of(es, "cvb", [128, WM], BF16, 2)
        sgp = k.pool_of(es, "cvg", [128, WM], BF16, 2)
        zpp = k.pool_of(es, "cvz", [128, WM + 2], F32, 2, part=True)
        t1p = k.pool_of(es, "cvt", [128, WM], F32, 2)
        acp = k.pool_of(es, "cva", [128, WM], F32, 2)
        ogp = k.pool_of(es, "cvo", [128, WM], BF16, 2)
        for (t0, W, hl, hr, masked) in segs:
            for kc in range(8):
                r0 = kc * 128
                cc, cx, cb, sg = ccp.next(), cxp.next(), cbp.next(), sgp.next()
                for (tile, col) in ((cc, C_CC), (cx, C_CX)):
                    k.dma(k.sp, tile[:, 1:W + 1], T.HT.t[col + r0:col + r0 + 128, t0:t0 + W], tile,
                          reads=[T.HT], writes=[tile])
                    if masked:
                        k.dma(k.sp, tile[:, 0:1], T.HT.t[col + r0:col + r0 + 128, hl:hl + 1], tile,
                              reads=[T.HT], writes=[tile], slow=True)
                        k.dma(k.sp, tile[:, W + 1:W + 2], T.HT.t[col + r0:col + r0 + 128, hr:hr + 1], tile,
                              reads=[T.HT], writes=[tile], slow=True)
                k.dma(k.sp, cb[:, 0:W], T.HT.t[C_CB + r0:C_CB + r0 + 128, t0:t0 + W], cb, reads=[T.HT], writes=[cb])
                k.dma(k.sp, sg[:, 0:W], T.HT.t[C_GCV + r0:C_GCV + r0 + 128, t0:t0 + W], sg, reads=[T.HT], writes=[sg])
                zp = zpp.next()
                if masked:
                    k.op(k.dve, lambda e: e.tensor_tensor(out=zp[:, 0:W + 2], in0=cc[:, 0:W + 2], in1=cx[:, 0:W + 2],
                                                          op=ALU.mult), reads=[cc, cx], writes=[zp])
                else:
                    k.op(k.dve, lambda e: e.memset(zp[:, 0:1], 0.0), writes=[zp])
                    k.op(k.dve, lambda e: e.memset(zp[:, W + 1:W + 2], 0.0), writes=[zp])
                    k.op(k.dve, lambda e: e.tensor_tensor(out=zp[:, 1:W + 1], in0=cc[:, 1:W + 1], in1=cx[:, 1:W + 1],
                                                          op=ALU.mult), reads=[cc, cx], writes=[zp])
                acc = acp.next()
                k.op(k.dve, lambda e: e.tensor_scalar_mul(out=acc[:, 0:W], in0=zp[:, 1:W + 1],
                                                          scalar1=C.convw[:, l, kc, 1:2]),
                     reads=[zp, C.convw], writes=[acc])
                for (off, wi, mi) in ((0, 0, 0), (2, 2, 1)):
                    if masked:
                        t1 = t1p.next()
                        k.op(k.pool, lambda e: e.tensor_tensor(out=t1[:, 0:W], in0=zp[:, off:off + W],
                                                               in1=cmask[:, mi, 0:W], op=ALU.mult),
                             reads=[zp, cmask], writes=[t1])
                        srcb, srca = t1, t1[:, 0:W]
                    else:
                        srcb, srca = zp, zp[:, off:off + W]
                    k.op(k.dve, lambda e: e.scalar_tensor_tensor(out=acc[:, 0:W], in0=srca,
                                                                 scalar=C.convw[:, l, kc, wi:wi + 1],
                                                                 in1=acc[:, 0:W], op0=ALU.mult, op1=ALU.add),
                         reads=[srcb, acc, C.convw], writes=[acc])
                k.op(k.dve, lambda e: e.tensor_tensor(out=acc[:, 0:W], in0=acc[:, 0:W], in1=cb[:, 0:W], op=ALU.mult),
                     reads=[acc, cb], writes=[acc])
                og = ogp.next()
                k.op(k.pool, lambda e: e.tensor_tensor(out=og[:, 0:W], in0=acc[:, 0:W], in1=sg[:, 0:W], op=ALU.mult),
                     reads=[acc, sg], writes=[og])
                k.dma(k.sp, T.OG.t[0, r0:r0 + 128, t0:t0 + W], og[:, 0:W], og, reads=[og], writes=[T.OG])


def stage_mla(k, T, C, l, lat_q, ctxq):
    NQ = lat_q + (NCTX if ctxq else 0)
    g_all = split_groups([(0, NT)])
    g_q = split_groups([(0, NQ)])
    with k.scope() as es:
        rope = k.sb(es, "rope", [64, 2, NLAT], F32)
        k.dma(k.sp, rope[:], T.rope[:], rope, writes=[rope])
        ckv = k.sb(es, "ckv", [128, 2, NT], BF16)
        ckg = k.sb(es, "ckg", [128, 2, NT], BF16, part=True)
        kr = k.sb(es, "kr", [64, NT], BF16)
        krr = k.sb(es, "krr", [64, NT], BF16, part=True)
        rkv = k.sb(es, "rkv", [128, NT], F32, part=True)
        rkvT = k.sb(es, "rkvT", [128, NT // 128], F32, part=True)
        cq = k.sb(es, "cq", [128, 4, NQ], BF16)
        cqg = k.sb(es, "cqg", [128, 4, NQ], BF16, part=True)
        rq = k.sb(es, "rq", [128, NQ], F32, part=True)
        wukv = k.sb(es, "wukv", [128, 2, 2048], BF16)
        wuq = k.sb(es, "wuq", [128, 4, 1536], BF16)
        k.dma(k.pool, wukv[:], T.w_ukv[l].rearrange("(k p) c -> p k c", p=128), wukv, writes=[wukv])
        k.dma(k.pool, wuq[:], T.w_uq[l].rearrange("(k p) c -> p k c", p=128), wuq, writes=[wuq])
        k.dma(k.sp, ckv[:], T.HT.t[0:256, :].rearrange("(k p) t -> p k t", p=128), ckv, reads=[T.HT], writes=[ckv])
        k.dma(k.sp, kr[:], T.HT.t[256:320, :], kr, reads=[T.HT], writes=[kr])
        k.dma(k.sp, cq[:], T.HT.t[C_CQ:C_CQ + 512, 0:NQ].rearrange("(k p) t -> p k t", p=128), cq,
              reads=[T.HT], writes=[cq])
        sqp = k.pool_of(es, "sqm", [128, 4, 512], F32, 2)
        tmp = k.pool_of(es, "tmm", [128, 512], F32, 4)
        psr = Ring(k.psum[5:8])
        for (t0, n) in g_all:
            sq = sqp.next()
            k.op(k.act, lambda e: e.activation(out=sq[:, 0:2, 0:n], in_=ckv[:, :, t0:t0 + n], func=AF.Square),
                 reads=[ckv], writes=[sq])
            ps = psr.next()
            k.mm([(ps.t[:, 0:n], C.ones_f[:], sq[:, kk, 0:n], kk == 0, kk == 1) for kk in range(2)],
                 reads=[sq, C.ones_f], writes=[ps])
            rstd_op(k, rkv, rkv[:, t0:t0 + n], ps, ps.t[:, 0:n], 256)
            ps2 = psr.next()
            nch = n // 128
            mms = []
            for ci in range(nch):
                for kk in range(2):
                    mms.append((ps2.t[:, ci:ci + 1], sq[:, kk, ci * 128:(ci + 1) * 128], C.ones_f[:, 0:1],
                                kk == 0, kk == 1))
            k.mm(mms, reads=[sq, C.ones_f], writes=[ps2])
            rstd_op(k, rkvT, rkvT[:, t0 // 128:t0 // 128 + nch], ps2, ps2.t[:, 0:nch], 256)
        for kk in range(2):
            k.op(k.dve, lambda e: e.tensor_scalar_mul(out=ckg[:, kk, :], in0=ckv[:, kk, :],
                                                      scalar1=C.gkv[:, l, kk:kk + 1]), reads=[ckv, C.gkv], writes=[ckg])
        for (t0, n) in split_groups([(0, NLAT)]):
            ps = psr.next()
            k.mm([(ps.t[0:64, 0:n], C.perm_b[:], kr[:, t0:t0 + n], True, True)], reads=[C.perm_b, kr], writes=[ps])
            t1, t2 = tmp.next(), tmp.next()
            k.op(k.dve, lambda e: e.tensor_tensor(out=t1[0:64, 0:n], in0=kr[:, t0:t0 + n], in1=rope[:, 0, t0:t0 + n],
                                                  op=ALU.mult), reads=[kr, rope], writes=[t1])
            k.op(k.dve, lambda e: e.tensor_tensor(out=t2[0:64, 0:n], in0=ps.t[0:64, 0:n], in1=rope[:, 1, t0:t0 + n],
                                                  op=ALU.mult), reads=[ps, rope], writes=[t2])
            k.op(k.pool, lambda e: e.tensor_tensor(out=krr[:, t0:t0 + n], in0=t1[0:64, 0:n], in1=t2[0:64, 0:n],
                                                   op=ALU.add), reads=[t1, t2], writes=[krr])
        k.op(k.dve, lambda e: e.tensor_copy(out=krr[:, NLAT:NT], in_=kr[:, NLAT:NT]), reads=[kr], writes=[krr])
        for (t0, n) in g_q:
            sq = sqp.next()
            k.op(k.act, lambda e: e.activation(out=sq[:, :, 0:n], in_=cq[:, :, t0:t0 + n], func=AF.Square),
                 reads=[cq], writes=[sq])
            ps = psr.next()
            k.mm([(ps.t[:, 0:n], C.ones_f[:], sq[:, kk, 0:n], kk == 0, kk == 3) for kk in range(4)],
                 reads=[sq, C.ones_f], writes=[ps])
            rstd_op(k, rq, rq[:, t0:t0 + n], ps, ps.t[:, 0:n], 512)
        for kk in range(4):
            k.op(k.dve, lambda e: e.tensor_scalar_mul(out=cqg[:, kk, :], in0=cq[:, kk, :],
                                                      scalar1=C.gq[:, l, kk:kk + 1]), reads=[cq, C.gq], writes=[cqg])
        khp = k.pool_of(es, "kh", [128, NT], BF16, 2, part=True)
        vhp = k.pool_of(es, "vh", [128, NT // 128, 128], BF16, 2, part=True)
        qnp = k.pool_of(es, "qn", [128, NQ], BF16, 2, part=True)
        qrp = k.pool_of(es, "qr", [64, NQ], BF16, 2, part=True)
        ptp = k.pool_of(es, "pt", [128, 512], BF16, 3)
        sgp = k.pool_of(es, "sgm", [128, 512], BF16, 2)
        ogp = k.pool_of(es, "ogm", [128, 512], BF16, 2)
        sr = Ring(k.psum[0:3])
        po, pd = k.psum[3], k.psum[4]
        for h in range(8):
            kh, vh, qn, qr = khp.next(), vhp.next(), qnp.next(), qrp.next()
            for (t0, n) in g_all:
                ps = psr.next()
                k.mm([(ps.t[:, 0:n], wukv[:, kk, h * 256:h * 256 + 128], ckg[:, kk, t0:t0 + n], kk == 0, kk == 1)
                      for kk in range(2)], reads=[wukv, ckg], writes=[ps])
                k.op(k.dve, lambda e: e.tensor_tensor(out=kh[:, t0:t0 + n], in0=ps.t[:, 0:n], in1=rkv[:, t0:t0 + n],
                                                      op=ALU.mult), reads=[ps, rkv], writes=[kh])
            for c4 in range(0, NT // 128, 4):
                nch = min(4, NT // 128 - c4)
                ps = psr.next()
                mms = []
                for ci in range(nch):
                    for kk in range(2):
                        mms.append((ps.t[:, ci * 128:(ci + 1) * 128], ckg[:, kk, (c4 + ci) * 128:(c4 + ci + 1) * 128],
                                    wukv[:, kk, h * 256 + 128:h * 256 + 256], kk == 0, kk == 1))
                k.mm(mms, reads=[wukv, ckg], writes=[ps])
                for ci in range(nch):
                    k.op(k.act, lambda e: e.activation(out=vh[:, c4 + ci, :], in_=ps.t[:, ci * 128:(ci + 1) * 128],
                                                       func=AF.Copy, scale=rkvT[:, c4 + ci:c4 + ci + 1]),
                         reads=[ps, rkvT], writes=[vh])
            for (t0, n) in g_q:
                ps = psr.next()
                k.mm([(ps.t[:, 0:n], wuq[:, kk, h * 192:h * 192 + 128], cqg[:, kk, t0:t0 + n], kk == 0, kk == 3)
                      for kk in range(4)], reads=[wuq, cqg], writes=[ps])
                k.op(k.dve, lambda e: e.tensor_tensor(out=qn[:, t0:t0 + n], in0=ps.t[:, 0:n], in1=rq[:, t0:t0 + n],
                                                      op=ALU.mult), reads=[ps, rq], writes=[qn])
                ps2 = psr.next()
                k.mm([(ps2.t[0:64, 0:n], wuq[:, kk, h * 192 + 128:h * 192 + 192], cqg[:, kk, t0:t0 + n],
                       kk == 0, kk == 3) for kk in range(4)], reads=[wuq, cqg], writes=[ps2])
                if t0 < lat_q:
                    qf = tmp.next()
                    k.op(k.dve, lambda e: e.tensor_tensor(out=qf[0:64, 0:n], in0=ps2.t[0:64, 0:n],
                                                          in1=rq[0:64, t0:t0 + n], op=ALU.mult),
                         reads=[ps2, rq], writes=[qf])
                    ps3 = psr.next()
                    k.mm([(ps3.t[0:64, 0:n], C.perm_f[:], qf[0:64, 0:n], True, True)], reads=[C.perm_f, qf],
                         writes=[ps3])
                    t1, t2 = tmp.next(), tmp.next()
                    k.op(k.pool, lambda e: e.tensor_tensor(out=t1[0:64, 0:n], in0=qf[0:64, 0:n],
                                                           in1=rope[:, 0, t0:t0 + n], op=ALU.mult),
                         reads=[qf, rope], writes=[t1])
                    k.op(k.dve, lambda e: e.tensor_tensor(out=t2[0:64, 0:n], in0=ps3.t[0:64, 0:n],
                                                          in1=rope[:, 1, t0:t0 + n], op=ALU.mult),
                         reads=[ps3, rope], writes=[t2])
                    k.op(k.pool, lambda e: e.tensor_tensor(out=qr[:, t0:t0 + n], in0=t1[0:64, 0:n], in1=t2[0:64, 0:n],
                                                           op=ALU.add), reads=[t1, t2], writes=[qr])
                else:
                    k.op(k.dve, lambda e: e.tensor_tensor(out=qr[:, t0:t0 + n], in0=ps2.t[0:64, 0:n],
                                                          in1=rq[0:64, t0:t0 + n], op=ALU.mult),
                         reads=[ps2, rq], writes=[qr])
            qblocks = [(q0, n, list(range(NT // 128))) for (q0, n) in split_groups([(0, lat_q)])]
            if ctxq:
                qblocks.append((NLAT, NCTX, [16, 17]))
            for (q0, nq, chunks) in qblocks:
                sg = sgp.next()
                k.dma(k.sp, sg[:, 0:nq], T.HT.t[C_GML + h * 128:C_GML + (h + 1) * 128, q0:q0 + nq], sg,
                      reads=[T.HT], writes=[sg])

                def qk(c):
                    ps = sr.next()
                    k.mm([(ps.t[:, 0:nq], kh[:, c * 128:(c + 1) * 128], qn[:, q0:q0 + nq], True, False),
                          (ps.t[:, 0:nq], krr[:, c * 128:(c + 1) * 128], qr[:, q0:q0 + nq], False, True)],
                         reads=[kh, krr, qn, qr], writes=[ps])
                    return ps
                cur = qk(chunks[0])
                for i, c in enumerate(chunks):
                    nxt = qk(chunks[i + 1]) if i + 1 < len(chunks) else None
                    pt = ptp.next()
                    k.op(k.act, lambda e: e.activation(out=pt[:, 0:nq], in_=cur.t[:, 0:nq], func=AF.Exp,
                                                       scale=MLA_SCALE), reads=[cur], writes=[pt])
                    last = (i == len(chunks) - 1)
                    k.mm([(po.t[:, 0:nq], vh[:, c, :], pt[:, 0:nq], i == 0, last),
                          (pd.t[:, 0:nq], C.ones_b[:], pt[:, 0:nq], i == 0, last)],
                         reads=[vh, pt, C.ones_b], writes=[po, pd])
                    cur = nxt
                rd, o = tmp.next(), tmp.next()
                k.op(k.dve, lambda e: e.reciprocal(out=rd[:, 0:nq], in_=pd.t[:, 0:nq]), reads=[pd], writes=[rd])
                k.op(k.dve, lambda e: e.tensor_tensor(out=o[:, 0:nq], in0=po.t[:, 0:nq], in1=rd[:, 0:nq], op=ALU.mult),
                     reads=[po, rd], writes=[o])
                og = ogp.next()
                k.op(k.pool, lambda e: e.tensor_tensor(out=og[:, 0:nq], in0=o[:, 0:nq], in1=sg[:, 0:nq], op=ALU.mult),
                     reads=[o, sg], writes=[og])
                k.dma(k.sp, T.OG.t[1, h * 128:(h + 1) * 128, q0:q0 + nq], og[:, 0:nq], og, reads=[og], writes=[T.OG])


def stage_na(k, T, C, l, lat_q, ctxq):
    NQ = lat_q + (NCTX if ctxq else 0)
    nhalf = lat_q // 1024
    with k.scope() as es:
        namask = k.sb(es, "namask", [128, NA_NM, 128], BF16)
        k.dma(k.sp, namask[:], T.namask[:], namask, writes=[namask])
        khp = k.pool_of(es, "nk", [64, NT], BF16, 2)
        qhp = k.pool_of(es, "nq", [64, NQ], BF16, 2)
        sgp = k.pool_of(es, "ng", [64, NQ], BF16, 2)
        vpp = k.pool_of(es, "nv", [128, NT // 128, 128], BF16, 2)
        brp = k.pool_of(es, "nb", [128, 7, 128], F32, 2)
        lgp = k.pool_of(es, "nl", [128, 6, 128], F32, 3)
        ptp = k.pool_of(es, "np", [128, 6, 128], BF16, 3)
        pcp = k.pool_of(es, "npc", [128, 512], BF16, 3)
        ogp = k.pool_of(es, "no", [64, NQ], BF16, 2, part=True)
        rdp = k.pool_of(es, "nr", [64, 512], F32, 2)
        otp = k.pool_of(es, "nt", [64, 512], F32, 2)
        ssets = [(k.psum[0], k.psum[1]), (k.psum[2], k.psum[3])]
        si = 0
        ob = (k.psum[4], k.psum[5])
        db = (k.psum[6], k.psum[7])
        vp = None
        for h in range(16):
            hh = h % 2
            if hh == 0:
                vp = vpp.next()
                k.dma(k.sp, vp[:], T.VNA.t[:, h * 64:h * 64 + 128].rearrange("(c p) f -> p c f", p=128), vp,
                      reads=[T.VNA], writes=[vp])
            kh, qh, sg, br = khp.next(), qhp.next(), sgp.next(), brp.next()
            k.dma(k.sp, kh[:], T.HT.t[C_NAK + h * 64:C_NAK + (h + 1) * 64, :], kh, reads=[T.HT], writes=[kh])
            k.dma(k.sp, qh[:], T.HT.t[C_NAQ + h * 64:C_NAQ + (h + 1) * 64, 0:NQ], qh, reads=[T.HT], writes=[qh])
            k.dma(k.sp, sg[:], T.HT.t[C_GNA + h * 64:C_GNA + (h + 1) * 64, 0:NQ], sg, reads=[T.HT], writes=[sg])
            k.dma(k.sp, br[:], T.brel[l, h], br, writes=[br])
            og = ogp.next()
            vcols = slice(hh * 64, (hh + 1) * 64)
            passes = [(hf_, 1024 * hf_, 1024) for hf_ in range(nhalf)]
            if ctxq:
                passes.append((None, NLAT, NCTX))
            for (qhalf, QB, nqp) in passes:
                qgroups = [(g0, min(512, nqp - g0)) for g0 in range(0, nqp, 512)]
                steps = []
                for cc in (16, 17):
                    for gi, (g0, gn) in enumerate(qgroups):
                        def mk_ctx(cc=cc, gi=gi, g0=g0, gn=gn):
                            st = {}

                            def qk():
                                nonlocal si
                                st["sA"] = ssets[si % 2][0]
                                si += 1
                                k.mm([(st["sA"].t[:, 0:gn], kh[:, cc * 128:(cc + 1) * 128],
                                       qh[:, QB + g0:QB + g0 + gn], True, True)], reads=[kh, qh], writes=[st["sA"]])

                            def sm():
                                st["pc"] = pcp.next()
                                k.op(k.act, lambda e: e.activation(out=st["pc"][:, 0:gn], in_=st["sA"].t[:, 0:gn],
                                                                   func=AF.Exp, scale=NA_SCALE),
                                     reads=[st["sA"]], writes=[st["pc"]])

                            def pv():
                                last = (cc == 17 and qhalf is None)
                                k.mm([(ob[gi].t[0:64, 0:gn], vp[:, cc, vcols], st["pc"][:, 0:gn], cc == 16, last, 1),
                                      (db[gi].t[0:64, 0:gn], C.ones_b[:, 0:64], st["pc"][:, 0:gn], cc == 16, last, 1)],
                                     reads=[vp, st["pc"], C.ones_b], writes=[ob[gi], db[gi]])
                            return (qk, sm, pv)
                        steps.append(mk_ctx())
                if qhalf is not None:
                    for u in range(-2, 10):
                        def mk_lat(u=u):
                            prs = na_pairs(u)
                            npr = len(prs)
                            jlo = prs[0]
                            c = (8 * qhalf + u) % 16
                            m0 = NA_MIDX[(qhalf, u, jlo)]
                            s0 = 3 - u + jlo
                            nA = min(npr, 4)
                            nB = npr - nA
                            q0 = QB + 128 * jlo
                            st = {}

                            def qk():
                                nonlocal si
                                sA, sB = ssets[si % 2]
                                si += 1
                                st["sA"], st["sB"] = sA, sB
                                mms = [(sA.t[:, 0:128 * nA], kh[:, c * 128:(c + 1) * 128], qh[:, q0:q0 + 128 * nA],
                                        True, False),
                                       (sA.t[:, 0:128 * nA], C.ident_b[:],
                                        namask[:, m0:m0 + nA, :].rearrange("p s q -> p (s q)"), False, True)]
                                wr = [sA]
                                if nB:
                                    mms += [(sB.t[:, 0:128 * nB], kh[:, c * 128:(c + 1) * 128],
                                             qh[:, q0 + 512:q0 + 512 + 128 * nB], True, False),
                                            (sB.t[:, 0:128 * nB], C.ident_b[:],
                                             namask[:, m0 + 4:m0 + 4 + nB, :].rearrange("p s q -> p (s q)"), False, True)]
                                    wr.append(sB)
                                k.mm(mms, reads=[kh, qh, C.ident_b, namask], writes=wr)

                            def sm():
                                sA, sB = st["sA"], st["sB"]
                                lg = lgp.next()
                                k.op(k.dve, lambda e: e.scalar_tensor_tensor(
                                    out=lg[:, 0:nA, :], in0=sA.t[:, 0:128 * nA].rearrange("p (s q) -> p s q", q=128),
                                    scalar=NA_SCALE, in1=br[:, s0:s0 + nA, :], op0=ALU.mult, op1=ALU.add),
                                    reads=[sA, br], writes=[lg])
                                if nB:
                                    k.op(k.dve, lambda e: e.scalar_tensor_tensor(
                                        out=lg[:, 4:4 + nB, :],
                                        in0=sB.t[:, 0:128 * nB].rearrange("p (s q) -> p s q", q=128),
                                        scalar=NA_SCALE, in1=br[:, s0 + 4:s0 + 4 + nB, :], op0=ALU.mult, op1=ALU.add),
                                        reads=[sB, br], writes=[lg])
                                st["pt"] = ptp.next()
                                k.op(k.act, lambda e: e.activation(out=st["pt"][:, 0:npr, :], in_=lg[:, 0:npr, :],
                                                                   func=AF.Exp), reads=[lg], writes=[st["pt"]])

                            def pv():
                                pt = st["pt"]
                                mms = []
                                wr = []
                                for bi in range(2):
                                    a_ = max(jlo, 4 * bi)
                                    b_ = min(prs[-1], 4 * bi + 3)
                                    if a_ > b_:
                                        continue
                                    cols = slice(128 * (a_ - 4 * bi), 128 * (b_ + 1 - 4 * bi))
                                    rhs = pt[:, a_ - jlo:b_ + 1 - jlo, :].rearrange("p s q -> p (s q)")
                                    fin = (u == 9)
                                    mms.append((ob[bi].t[0:64, cols], vp[:, c, vcols], rhs, False, fin, 1))
                                    mms.append((db[bi].t[0:64, cols], C.ones_b[:, 0:64], rhs, False, fin, 1))
                                    wr += [ob[bi], db[bi]]
                                k.mm(mms, reads=[vp, pt, C.ones_b], writes=wr)
                            return (qk, sm, pv)
                        steps.append(mk_lat())
                steps[0][0]()
                for i_, (qk_, sm_, pv_) in enumerate(steps):
                    if i_ + 1 < len(steps):
                        steps[i_ + 1][0]()
                    sm_()
                    pv_()
                for gi, (g0, gn) in enumerate(qgroups):
                    rd, o = rdp.next(), otp.next()
                    k.op(k.dve, lambda e: e.reciprocal(out=rd[:, 0:gn], in_=db[gi].t[0:64, 0:gn]),
                         reads=[db[gi]], writes=[rd])
                    k.op(k.dve, lambda e: e.tensor_tensor(out=o[:, 0:gn], in0=ob[gi].t[0:64, 0:gn], in1=rd[:, 0:gn],
                                                          op=ALU.mult), reads=[ob[gi], rd], writes=[o])
                    k.op(k.pool, lambda e: e.tensor_tensor(out=og[:, QB + g0:QB + g0 + gn], in0=o[:, 0:gn],
                                                           in1=sg[:, QB + g0:QB + g0 + gn], op=ALU.mult),
                         reads=[o, sg], writes=[og])
            k.dma(k.sp, T.OG.t[2, h * 64:(h + 1) * 64, 0:NQ], og[:], og, reads=[og], writes=[T.OG])


def stage_fourier(k, T, C, l, lat_q, ctxq):
    with k.scope() as es:
        vf = k.sb(es, "vf", [128, 16, 1024], BF16)
        k.dma(k.sp, vf[:], T.VF.t[0:NLAT, :].rearrange("(c p) f -> p c f", p=128), vf, reads=[T.VF], writes=[vf])
        jobs = [("lat", vf, 16, T.dftN, k0, 512, k0) for k0 in range(0, lat_q, 512)]
        if ctxq:
            vfc = k.sb(es, "vfc", [128, 2, 1024], BF16)
            k.dma(k.sp, vfc[:], T.VF.t[NLAT:NT, :].rearrange("(c p) f -> p c f", p=128), vfc,
                  reads=[T.VF], writes=[vfc])
            jobs.append(("ctx", vfc, 2, T.dftC, 0, 256, NLAT))
        cnp = k.pool_of(es, "fcn", [128, 16, 512], BF16, 2)
        snp = k.pool_of(es, "fsn", [128, 16, 512], BF16, 2)
        z1p = k.pool_of(es, "fz1", [128, 8, 512], BF16, 2, part=True)
        z2p = k.pool_of(es, "fz2", [128, 8, 512], BF16, 2, part=True)
        sgp = k.pool_of(es, "fsg", [128, 8, 512], BF16, 2)
        ogp = k.pool_of(es, "fog", [128, 8, 512], BF16, 2, part=True)
        zr = Ring(k.psum[0:4])
        orr = Ring(k.psum[4:8])
        for (kind, vt, nch, tab, k0, nk, tok0) in jobs:
            cn, sn, sg = cnp.next(), snp.next(), sgp.next()
            k.dma(k.sp, cn[:, 0:nch, 0:nk], tab.t[0][:, k0:k0 + nk].rearrange("(c p) k -> p c k", p=128), cn,
                  writes=[cn])
            k.dma(k.sp, sn[:, 0:nch, 0:nk], tab.t[1][:, k0:k0 + nk].rearrange("(c p) k -> p c k", p=128), sn,
                  writes=[sn])
            k.dma(k.sp, sg[:, :, 0:nk], T.HT.t[C_GFN:C_GFN + 1024, tok0:tok0 + nk].rearrange("(c p) t -> p c t", p=128),
                  sg, reads=[T.HT], writes=[sg])
            z1, z2 = z1p.next(), z2p.next()
            for cc in range(8):
                ps = zr.next()
                k.mm([(ps.t[:, 0:nk], vt[:, n, cc * 128:(cc + 1) * 128], cn[:, n, 0:nk], n == 0, n == nch - 1)
                      for n in range(nch)], reads=[vt, cn], writes=[ps])
                k.op(k.dve, lambda e: e.tensor_copy(out=z1[:, cc, 0:nk], in_=ps.t[:, 0:nk]), reads=[ps], writes=[z1])
                ps2 = zr.next()
                k.mm([(ps2.t[:, 0:nk], vt[:, n, cc * 128:(cc + 1) * 128], sn[:, n, 0:nk], n == 0, n == nch - 1)
                      for n in range(nch)], reads=[vt, sn], writes=[ps2])
                k.op(k.act, lambda e: e.activation(out=z2[:, cc, 0:nk], in_=ps2.t[:, 0:nk], func=AF.Copy),
                     reads=[ps2], writes=[z2])
            og = ogp.next()
            for g in range(4):
                for h2 in range(2):
                    oc = 2 * g + h2
                    ps = orr.next()
                    mms = [(ps.t[:, 0:nk], C.dftM[:, 0, ci, h2 * 128:(h2 + 1) * 128], z1[:, 2 * g + ci, 0:nk],
                            ci == 0, False) for ci in range(2)]
                    mms += [(ps.t[:, 0:nk], C.dftM[:, 1, ci, h2 * 128:(h2 + 1) * 128], z2[:, 2 * g + ci, 0:nk],
                             False, ci == 1) for ci in range(2)]
                    k.mm(mms, reads=[C.dftM, z1, z2], writes=[ps])
                    k.op(k.dve, lambda e: e.tensor_tensor(out=og[:, oc, 0:nk], in0=ps.t[:, 0:nk], in1=sg[:, oc, 0:nk],
                                                          op=ALU.mult), reads=[ps, sg], writes=[og])
            k.dma(k.sp, T.OG.t[3, :, tok0:tok0 + nk].rearrange("(c p) t -> p c t", p=128), og[:, :, 0:nk], og,
                  reads=[og], writes=[T.OG])


def stage_epilogue(k, T, C, l, lat_q, ctxq, src, dst):
    NQ = lat_q + (NCTX if ctxq else 0)
    NB = 1152 if NQ > 1024 else 1024
    for tb0 in range(0, NQ, NB):
        groups = split_groups([(tb0, tb0 + NB)])
        with k.scope() as es1:
            mT = k.sb(es1, "mT", [128, KC, NB], BF16, part=True)
            with k.scope() as es2:
                ogt = k.sb(es2, "ogt", [128, 32, NB], BF16)
                wpp = k.pool_of(es2, "ewp", [128, 8, 512], BF16, 6)
                gtp = k.pool_of(es2, "egt", [128, 4, 512], BF16, 2)
                tmp = k.pool_of(es2, "etm", [128, 512], F32, 8)
                for i in range(4):
                    k.dma(k.sp, ogt[:, i * 8:(i + 1) * 8, :],
                          T.OG.t[i, :, tb0:tb0 + NB].rearrange("(c p) t -> p c t", p=128), ogt,
                          reads=[T.OG], writes=[ogt])
                sets = [k.psum[0:4], k.psum[4:8]]
                si = 0
                for cg in range(4):
                    wps = []
                    for i in range(4):
                        wt = wpp.next()
                        k.dma(k.pool, wt[:], T.w_p[l, i][:, cg * 512:(cg + 1) * 512].rearrange(
                            "(k p) c -> p k c", p=128), wt, writes=[wt])
                        wps.append(wt)
                    for s in range(4):
                        j = cg * 4 + s
                        for (t0, n) in groups:
                            gt = gtp.next()
                            k.dma(k.sp, gt[:, :, 0:n],
                                  T.HT.t[C_MG:C_MG + 4 * D, t0:t0 + n].rearrange("(i c) t -> c i t", i=4)[
                                      j * 128:(j + 1) * 128], gt, reads=[T.HT], writes=[gt])
                            pss = sets[si % 2]
                            si += 1
                            for i in range(4):
                                k.mm([(pss[i].t[:, 0:n], wps[i][:, kc, s * 128:(s + 1) * 128],
                                       ogt[:, i * 8 + kc, t0 - tb0:t0 - tb0 + n], kc == 0, kc == 7)
                                      for kc in range(8)], reads=[wps[i], ogt], writes=[pss[i]])
                            a = [tmp.next() for _ in range(4)]
                            for i in range(4):
                                k.op(k.dve, lambda e: e.tensor_tensor(out=a[i][:, 0:n], in0=pss[i].t[:, 0:n],
                                                                      in1=gt[:, i, 0:n], op=ALU.mult),
                                     reads=[pss[i], gt], writes=[a[i]])
                            k.op(k.pool, lambda e: e.tensor_tensor(out=a[0][:, 0:n], in0=a[0][:, 0:n],
                                                                   in1=a[1][:, 0:n], op=ALU.add),
                                 reads=[a[0], a[1]], writes=[a[0]])
                            k.op(k.pool, lambda e: e.tensor_tensor(out=a[2][:, 0:n], in0=a[2][:, 0:n],
                                                                   in1=a[3][:, 0:n], op=ALU.add),
                                 reads=[a[2], a[3]], writes=[a[2]])
                            k.op(k.pool, lambda e: e.tensor_tensor(out=mT[:, j, t0 - tb0:t0 - tb0 + n],
                                                                   in0=a[0][:, 0:n], in1=a[2][:, 0:n], op=ALU.add),
                                 reads=[a[0], a[2]], writes=[mT])
            yT = k.sb(es1, "yT", [128, KC, NB], F32, part=True)
            psr = Ring(k.psum)
            with k.scope() as es3:
                wop = k.pool_of(es3, "ewo", [128, KC, 512], BF16, 2)
                for cg in range(4):
                    wt = wop.next()
                    k.dma(k.pool, wt[:], T.w_out[l][:, cg * 512:(cg + 1) * 512].rearrange("(k p) c -> p k c", p=128),
                          wt, writes=[wt])
                    for s in range(4):
                        j = cg * 4 + s
                        for (t0, n) in groups:
                            ps = psr.next()
                            k.mm([(ps.t[:, 0:n], wt[:, kc, s * 128:(s + 1) * 128], mT[:, kc, t0 - tb0:t0 - tb0 + n],
                                   kc == 0, kc == KC - 1) for kc in range(KC)], reads=[wt, mT], writes=[ps])
                            k.op(k.act, lambda e: e.activation(out=yT[:, j, t0 - tb0:t0 - tb0 + n], in_=ps.t[:, 0:n],
                                                               func=AF.Copy), reads=[ps], writes=[yT])
            with k.scope() as es4:
                sqp = k.pool_of(es4, "esq", [128, KC, 256], F32, 1)
                xtp = k.pool_of(es4, "ext", [128, KC, 256], F32, 2)
                otp = k.pool_of(es4, "eot", [128, KC, 256], F32, 2)
                rsp = k.pool_of(es4, "ers", [128, 256], F32, 2)
                for (t0, n) in split_groups([(tb0, tb0 + NB)], 256):
                    v = 0 if t0 < NLAT else 1
                    lo = t0 - tb0
                    xt = xtp.next()
                    k.dma(k.sp, xt[:, :, 0:n], src.t[:, t0:t0 + n].rearrange("(k p) t -> p k t", p=128), xt,
                          reads=[src], writes=[xt])
                    sq = sqp.next()
                    k.op(k.act, lambda e: e.activation(out=sq[:, :, 0:n], in_=yT[:, :, lo:lo + n], func=AF.Square),
                         reads=[yT], writes=[sq])
                    ps = psr.next()
                    k.mm([(ps.t[:, 0:n], C.ones_f[:], sq[:, kc, 0:n], kc == 0, kc == KC - 1) for kc in range(KC)],
                         reads=[sq, C.ones_f], writes=[ps])
                    rs = rsp.next()
                    rstd_op(k, rs, rs[:, 0:n], ps, ps.t[:, 0:n], D)
                    ot = otp.next()
                    rb = rs[:, 0:n].unsqueeze(1).to_broadcast([128, KC, n])
                    gb = C.G[:, l, v, :].unsqueeze(2).to_broadcast([128, KC, n])
                    k.op(k.dve, lambda e: e.tensor_tensor(out=ot[:, :, 0:n], in0=yT[:, :, lo:lo + n], in1=rb,
                                                          op=ALU.mult), reads=[yT, rs], writes=[ot])
                    k.op(k.pool, lambda e: e.tensor_tensor(out=ot[:, :, 0:n], in0=ot[:, :, 0:n], in1=gb,
                                                           op=ALU.mult), reads=[ot, C.G], writes=[ot])
                    k.op(k.dve, lambda e: e.tensor_tensor(out=ot[:, :, 0:n], in0=ot[:, :, 0:n], in1=xt[:, :, 0:n],
                                                          op=ALU.add), reads=[ot, xt], writes=[ot])
                    k.dma(k.sp, dst.t[:, t0:t0 + n].rearrange("(k p) t -> p k t", p=128), ot[:, :, 0:n], ot,
                          reads=[ot], writes=[dst])
```

```python
from contextlib import ExitStack
import numpy as np
import ml_dtypes
import concourse.bass as bass
import concourse.mybir as mybir
from concourse.bass_utils import run_bass_kernel_spmd

F32 = mybir.dt.float32
BF16 = mybir.dt.bfloat16
AF = mybir.ActivationFunctionType
ALU = mybir.AluOpType

D = 2048
NLAT = 2048
NCTX = 256
NT = NLAT + NCTX
KC = D // 128
N_IN = 20288
EPS = 1e-6
GRID_W = 64
MLA_SCALE = 192.0 ** -0.5
NA_SCALE = 0.125
NEG = -30000.0
NWT = 40

C_KV0, C_NAK, C_NAV, C_CQ, C_NAQ, C_CB, C_CC, C_CX, C_FV = 0, 320, 1344, 2368, 2880, 3904, 4928, 5952, 6976
C_GCV, C_GML, C_GNA, C_GFN, C_MG = 8000, 9024, 10048, 11072, 12096


class Eng:
    def __init__(self, nc, name, h, is_pe=False):
        self.nc, self.name, self.h, self.is_pe = nc, name, h, is_pe
        self.sem = nc.alloc_semaphore("pg_" + name)
        self.n = 0
        self.seen = {}

    def wait(self, ev):
        sem, val = ev
        if self.is_pe and sem is self.sem:
            return
        if self.seen.get(sem, 0) >= val:
            return
        self.h.wait_ge(sem, val)
        self.seen[sem] = val

    def mark(self, inst):
        self.n += 1
        inst.then_inc(self.sem, 1)
        return (self.sem, self.n)


class Buf:
    def __init__(self, t, name="", part=False):
        self.t = t
        self.name = name
        self.w = {}
        self.r = {}
        self.part = part
        self.dsem = None
        self.dcnt = 0

    def __getitem__(self, k):
        return self.t[k]


class K:
    def __init__(self, nc):
        self.nc = nc
        self.pe = Eng(nc, "pe", nc.tensor, True)
        self.act = Eng(nc, "act", nc.scalar)
        self.dve = Eng(nc, "dve", nc.vector)
        self.pool = Eng(nc, "pool", nc.gpsimd)
        self.sp = Eng(nc, "sp", nc.sync)
        self.nsem = 0
        self.psum = []
        self.psi = 0
        self.free_sems = []
        self.scopes = []

    def _pre(self, eng, reads, writes):
        for b in reads:
            for ev in b.w.values():
                eng.wait(ev)
        for b in writes:
            if not b.part:
                for ev in b.w.values():
                    eng.wait(ev)
            for ev in b.r.values():
                eng.wait(ev)

    def _post(self, ev, reads, writes):
        for b in reads:
            b.r[ev[0]] = ev
        for b in writes:
            if b.part:
                b.w[ev[0]] = ev
            else:
                b.w = {ev[0]: ev}
                b.r = {}

    def op(self, eng, fn, reads=(), writes=()):
        self._pre(eng, reads, writes)
        inst = fn(eng.h)
        ev = eng.mark(inst)
        self._post(ev, reads, writes)

    def mm(self, mms, reads=(), writes=()):
        eng = self.pe
        self._pre(eng, reads, writes)
        inst = None
        for mmv in mms:
            (o, l, r, st, sp) = mmv[:5]
            if len(mmv) > 5:
                inst = self.nc.tensor.matmul(o, l, r, start=st, stop=sp, skip_group_check=True)
            else:
                inst = self.nc.tensor.matmul(o, l, r, start=st, stop=sp)
        ev = eng.mark(inst)
        self._post(ev, reads, writes)

    def dma(self, q, out, in_, owner, reads=(), writes=(), slow=False):
        self._pre(q, reads, writes)
        if owner.dsem is None:
            if self.free_sems:
                owner.dsem, owner.dcnt = self.free_sems.pop()
                q.wait((owner.dsem, owner.dcnt))
            else:
                owner.dsem = self.nc.alloc_semaphore("d%d" % self.nsem)
                self.nsem += 1
        if slow:
            inst = q.h.dma_start(out=out, in_=in_, allow_slow_non_contiguous=True)
        else:
            inst = q.h.dma_start(out=out, in_=in_)
        owner.dcnt += 16
        inst.then_inc(owner.dsem, 16)
        ev = (owner.dsem, owner.dcnt)
        self._post(ev, reads, writes)

    def sb(self, es, name, shape, dt, part=False):
        self.nsb = getattr(self, "nsb", 0) + 1
        t = es.enter_context(self.nc.sbuf_tensor("s%d_%s" % (self.nsb, name), list(shape), dt))
        b = Buf(t, name, part)
        if self.scopes:
            self.scopes[-1].append(b)
        return b

    def scope(self):
        return _Scope(self)

    def pool_of(self, es, name, shape, dt, n, part=False):
        return Ring([self.sb(es, "%s%d" % (name, i), shape, dt, part) for i in range(n)])

    def ps(self):
        b = self.psum[self.psi % len(self.psum)]
        self.psi += 1
        return b


class _Scope:
    def __init__(self, k):
        self.k = k
        self.es = ExitStack()

    def __enter__(self):
        self.k.scopes.append([])
        self.es.__enter__()
        return self.es

    def __exit__(self, *a):
        k = self.k
        bufs = k.scopes.pop()
        evs = {}
        for b in bufs:
            for d in (b.w, b.r):
                for (sem, val) in d.values():
                    if evs.get(sem, (None, 0))[1] < val:
                        evs[sem] = (sem, val)
        for eng in (k.pe, k.act, k.dve, k.pool, k.sp):
            for ev in evs.values():
                if not (ev[0] is eng.sem):
                    eng.wait(ev)
                elif eng.is_pe:
                    pass
        for b in bufs:
            if b.dsem is not None:
                k.free_sems.append((b.dsem, b.dcnt))
        return self.es.__exit__(*a)


class Ring:
    def __init__(self, bufs):
        self.bufs = bufs
        self.i = 0

    def next(self):
        b = self.bufs[self.i % len(self.bufs)]
        self.i += 1
        return b


def split_groups(ranges, n=512):
    out = []
    for (a, b) in ranges:
        t = a
        while t < b:
            e = min(b, (t // n + 1) * n)
            out.append((t, e - t))
            t = e
    return out


from contextlib import ExitStack


class Cst:
    pass


def rstd_op(k, out_buf, out_ap, ps_buf, ps_ap, nfeat):
    k.op(k.dve, lambda e: e.tensor_scalar(out=out_ap, in0=ps_ap, scalar1=1.0 / nfeat, scalar2=EPS,
                                          op0=ALU.mult, op1=ALU.add), reads=[ps_buf], writes=[out_buf])
    k.op(k.act, lambda e: e.activation(out=out_ap, in_=out_ap, func=AF.Sqrt), reads=[out_buf], writes=[out_buf])
    k.op(k.dve, lambda e: e.reciprocal(out=out_ap, in_=out_ap), reads=[out_buf], writes=[out_buf])


def na_pairs(u):
    jlo = max(0, u - 2) - (1 if u == 3 else 0)
    jhi = min(7, u + 2) + (1 if u == 4 else 0)
    return list(range(jlo, jhi + 1))


def na_mask_index():
    idx = {}
    n = 0
    for qh in range(2):
        for u in range(-2, 10):
            for jr in na_pairs(u):
                idx[(qh, u, jr)] = n
                n += 1
    return idx, n


NA_MIDX, NA_NM = na_mask_index()


def build_program(layers=(0, 1), taps=(), stop_after=None, x1_in=False):
    nc = bass.Bass("TRN2", target_bir_lowering=False)
    k = K(nc)

    def din(name, shape, dt=F32):
        return Buf(nc.dram_tensor(name, list(shape), dt, kind="ExternalInput").ap(), name, part=True)

    def dscr(name, shape, dt):
        kind = "ExternalOutput" if name in taps else "Internal"
        return Buf(nc.dram_tensor(name, list(shape), dt, kind=kind).ap(), name, part=True)

    T = Cst()
    T.xT = din("xT", [D, NT])
    T.cvec = din("cvec", [128, 16, 2])
    T.w_ada = din("w_ada", [2, D, 3 * D])
    T.b_ada = din("b_ada", [2, 2, 3 * D])
    T.w_in = din("w_in", [2, D, N_IN])
    T.gpre = din("gpre", [128, 2, 16])
    T.gpost = din("gpost", [128, 2, 16])
    T.gq = din("gq", [128, 2, 4])
    T.gkv = din("gkv", [128, 2, 2])
    T.w_uq = din("w_uq", [2, 512, 1536])
    T.w_ukv = din("w_ukv", [2, 256, 2048])
    T.convw = din("convw", [128, 2, 8, 3])
    T.brel = din("brel", [2, 16, 128, 7, 128])
    T.w_p = din("w_p", [2, 4, 1024, D])
    T.w_out = din("w_out", [2, D, D])
    T.ident = din("ident", [128, 128])
    T.perm = din("perm", [64, 64])
    T.rope = din("rope", [64, 2, NLAT])
    T.namask = din("namask", [128, NA_NM, 128], BF16)
    T.cmask = din("cmask", [128, 2, NLAT])
    T.dftN = din("dftN", [2, NLAT, NLAT], BF16)
    T.dftC = din("dftC", [2, NCTX, NCTX], BF16)
    T.dftM = din("dftM", [2, 256, 256], BF16)
    T.outT = Buf(nc.dram_tensor("outT", [D, 1024], F32, kind="ExternalOutput").ap(), "outT", part=True)
    T.HT = dscr("HT", [N_IN, NT], BF16)
    T.VNA = dscr("VNA", [NT, 1024], BF16)
    T.VF = dscr("VF", [NT, 1024], BF16)
    T.OG = dscr("OG", [4, 1024, NT], BF16)
    T.X1 = dscr("X1", [D, NT], F32)
    T.UTd = dscr("UTd", [D, NT], BF16) if "UTd" in taps else None
    T.MODd = dscr("MODd", [128, 2, 48, 2], F32) if "MODd" in taps else None

    owners = []
    _dma = k.dma

    def dma_reg(q, out, in_, owner, reads=(), writes=(), slow=False):
        if owner not in owners:
            owners.append(owner)
        _dma(q, out, in_, owner, reads, writes, slow)
    k.dma = dma_reg

    with ExitStack() as top:
        for i in range(8):
            k.psum.append(Buf(top.enter_context(nc.psum_tensor("psb%d" % i, [128, 512], F32)), "ps%d" % i))
        C = Cst()
        C.ones_f = k.sb(top, "ones_f", [128, 128], F32)
        C.ones_b = k.sb(top, "ones_b", [128, 128], BF16)
        C.ident_b = k.sb(top, "ident_b", [128, 128], BF16)
        C.perm_b = k.sb(top, "perm_b", [64, 64], BF16)
        C.perm_f = k.sb(top, "perm_f", [64, 64], F32)
        C.dftM = k.sb(top, "dftM", [128, 2, 2, 256], BF16)
        C.mod = k.sb(top, "mod", [128, 2, 48, 2], F32, part=True)
        C.A = k.sb(top, "modA", [128, 2, 2, 16], F32, part=True)
        C.G = k.sb(top, "modG", [128, 2, 2, 16], F32, part=True)
        C.gpre = k.sb(top, "gpre", [128, 2, 16], F32)
        C.gpost = k.sb(top, "gpost", [128, 2, 16], F32)
        C.gq = k.sb(top, "gq", [128, 2, 4], F32)
        C.gkv = k.sb(top, "gkv", [128, 2, 2], F32)
        C.convw = k.sb(top, "convw", [128, 2, 8, 3], F32)
        C.scv = k.sb(top, "scv", [128, 16, 2], F32)
        C.i2 = k.sb(top, "i2", [2, 2], F32)

        k.op(k.dve, lambda e: e.memset(C.ones_f[:], 1.0), writes=[C.ones_f])
        k.op(k.dve, lambda e: e.memset(C.ones_b[:], 1.0), writes=[C.ones_b])
        k.dma(k.pool, C.ident_b[:], T.ident[:], C.ident_b, writes=[C.ident_b])
        k.dma(k.pool, C.perm_b[:], T.perm[:], C.perm_b, writes=[C.perm_b])
        k.dma(k.sp, C.perm_f[:], T.perm[:], C.perm_f, writes=[C.perm_f])
        k.dma(k.sp, C.i2[:], T.ident.t[0:2, 0:2], C.i2, writes=[C.i2])
        for cs in range(2):
            k.dma(k.sp, C.dftM[:, cs, :, :], T.dftM[cs].rearrange("(k p) c -> p k c", p=128), C.dftM,
                  writes=[C.dftM])
        for (dst, src) in ((C.gpre, T.gpre), (C.gpost, T.gpost), (C.gq, T.gq), (C.gkv, T.gkv), (C.convw, T.convw)):
            k.dma(k.sp, dst[:], src[:], dst, writes=[dst])

        stages = []

        def run(name, fn, *a):
            if stages and stages[-1] == "__stop__":
                return
            globals()[fn](k, T, C, *a)
            stages.append(name)
            if stop_after == name:
                stages.append("__stop__")

        run("ada", "stage_ada", 0)
        if T.MODd is not None and "__stop__" in stages[-1:]:
            pass
        for l in layers:
            lat_q = NLAT if l == 0 else 1024
            ctxq = (l == 0)
            src = T.xT if (l == 0 or x1_in) else T.X1
            with k.scope() as es_u:
                UT = k.sb(es_u, "UT", [128, KC, NT], BF16, part=True)
                run("prenorm%d" % l, "stage_prenorm", l, UT, src)
                run("inproj%d" % l, "stage_inproj", l, UT, lat_q, ctxq)
            if l == 0 and 1 in layers:
                run("ada1", "stage_ada", 1)
            run("conv%d" % l, "stage_conv", l, lat_q, ctxq)
            run("mla%d" % l, "stage_mla", l, lat_q, ctxq)
            run("na%d" % l, "stage_na", l, lat_q, ctxq)
            run("four%d" % l, "stage_fourier", l, lat_q, ctxq)
            dst = T.X1 if l == 0 else T.outT
            run("epi%d" % l, "stage_epilogue", l, lat_q, ctxq, src, dst)

        if T.MODd is not None:
            k.dma(k.sp, T.MODd[:], C.mod[:], C.mod, reads=[C.mod], writes=[T.MODd])
        for ob in owners:
            k.sp.wait((ob.dsem, ob.dcnt))
    return nc


def stage_ada(k, T, C, l):
    with k.scope() as es:
        if l == 0:
            cv = k.sb(es, "cv", [128, 16, 2], F32)
            k.dma(k.sp, cv[:], T.cvec[:], cv, writes=[cv])
            k.op(k.act, lambda e: e.activation(out=C.scv[:], in_=cv[:], func=AF.Silu), reads=[cv], writes=[C.scv])
        wpool = k.pool_of(es, "wada", [128, 16, 512], F32, 2)
        bad2 = k.sb(es, "bad2", [2, 3 * D], F32)
        mod2 = k.sb(es, "mod2", [2, 3 * D], F32, part=True)
        k.dma(k.sp, bad2[:], T.b_ada[l], bad2, writes=[bad2])
        psr = Ring(k.psum[0:4])
        for cg in range(12):
            wt = wpool.next()
            k.dma(k.sp, wt[:], T.w_ada[l][:, cg * 512:(cg + 1) * 512].rearrange("(k p) c -> p k c", p=128),
                  wt, writes=[wt])
            ps = psr.next()
            k.mm([(ps.t[0:2, 0:512], C.scv[:, kk, :], wt[:, kk, :], kk == 0, kk == 15) for kk in range(16)],
                 reads=[wt, C.scv], writes=[ps])
            k.op(k.dve, lambda e: e.tensor_tensor(out=mod2[:, cg * 512:(cg + 1) * 512], in0=ps.t[0:2, 0:512],
                                                  in1=bad2[:, cg * 512:(cg + 1) * 512], op=ALU.add),
                 reads=[ps, bad2], writes=[mod2])
        pm = k.psum[4]
        k.mm([(pm.t[:, 2 * m:2 * m + 2], mod2[0:2, m * 128:(m + 1) * 128], C.i2[:], True, True) for m in range(48)],
             reads=[mod2, C.i2], writes=[pm])
        k.op(k.dve, lambda e: e.tensor_copy(out=C.mod[:, l, :, :],
                                            in_=pm.t[:, 0:96].rearrange("p (m v) -> p m v", v=2)),
             reads=[pm], writes=[C.mod])
        for v in range(2):
            k.op(k.dve, lambda e: e.scalar_tensor_tensor(out=C.A[:, l, v, :], in0=C.mod[:, l, 16:32, v],
                                                         scalar=1.0, in1=C.gpre[:, l, :],
                                                         op0=ALU.add, op1=ALU.mult),
                 reads=[C.mod, C.gpre], writes=[C.A])
            k.op(k.dve, lambda e: e.tensor_tensor(out=C.G[:, l, v, :], in0=C.mod[:, l, 32:48, v],
                                                  in1=C.gpost[:, l, :], op=ALU.mult),
                 reads=[C.mod, C.gpost], writes=[C.G])


def stage_prenorm(k, T, C, l, UT, src):
    TB = 256
    with k.scope() as es:
        xp = k.pool_of(es, "xn", [128, KC, TB], F32, 2)
        sqp = k.pool_of(es, "sqn", [128, KC, TB], F32, 2)
        rsp = k.pool_of(es, "rsn", [128, TB], F32, 2)
        psr = Ring(k.psum[2:6])
        for blk in range(NT // TB):
            t0 = blk * TB
            v = 0 if t0 < NLAT else 1
            xt = xp.next()
            k.dma(k.sp, xt[:], src.t[:, t0:t0 + TB].rearrange("(k p) t -> p k t", p=128), xt,
                  reads=[src], writes=[xt])
            sq = sqp.next()
            k.op(k.act, lambda e: e.activation(out=sq[:], in_=xt[:], func=AF.Square), reads=[xt], writes=[sq])
            ps = psr.next()
            k.mm([(ps.t[:, 0:TB], C.ones_f[:], sq[:, kk, :], kk == 0, kk == KC - 1) for kk in range(KC)],
                 reads=[sq, C.ones_f], writes=[ps])
            rs = rsp.next()
            rstd_op(k, rs, rs[:], ps, ps.t[:, 0:TB], D)
            rb = rs[:].unsqueeze(1).to_broadcast([128, KC, TB])
            ab = C.A[:, l, v, :].unsqueeze(2).to_broadcast([128, KC, TB])
            bb = C.mod[:, l, 0:KC, v].unsqueeze(2).to_broadcast([128, KC, TB])
            k.op(k.dve, lambda e: e.tensor_tensor(out=xt[:], in0=xt[:], in1=rb, op=ALU.mult),
                 reads=[xt, rs], writes=[xt])
            k.op(k.pool, lambda e: e.tensor_tensor(out=xt[:], in0=xt[:], in1=ab, op=ALU.mult),
                 reads=[xt, C.A], writes=[xt])
            k.op(k.dve, lambda e: e.tensor_tensor(out=UT[:, :, t0:t0 + TB], in0=xt[:], in1=bb, op=ALU.add),
                 reads=[xt, C.mod], writes=[UT])
        if T.UTd is not None:
            k.dma(k.sp, T.UTd.t.rearrange("(k p) t -> p k t", p=128), UT[:], UT, reads=[UT], writes=[T.UTd])


def inproj_plan(l, lat_q, ctxq):
    allr = [(0, NT)]
    own = [(0, NT)] if ctxq else [(0, lat_q)]
    halo = own if ctxq else own + [(lat_q, lat_q + 2), (NLAT - 2, NLAT)]
    fvr = [(0, NT)] if ctxq else [(0, NLAT)]
    return [
        (0, C_NAV, "FM", "copy", allr, None),
        (C_NAV, C_CQ, "TM", "copy", allr, "VNA"),
        (C_CQ, C_CC, "FM", "copy", own, None),
        (C_CC, C_FV, "FM", "copy", halo, None),
        (C_FV, C_GCV, "TM", "copy", fvr, "VF"),
        (C_GCV, C_MG, "FM", "silu", own, None),
        (C_MG, N_IN, "FM", "sigm", own, None),
    ]


def stage_inproj(k, T, C, l, UT, lat_q, ctxq):
    plan = inproj_plan(l, lat_q, ctxq)
    with k.scope() as es:
        wp = k.pool_of(es, "win", [128, KC, 512], BF16, 3)
        otp = k.pool_of(es, "ot", [128, NT], BF16, 3, part=True)
        vtp = k.pool_of(es, "vt", [128, 512], BF16, 4)
        psr = Ring(k.psum)
        for wi in range(NWT):
            c0 = 0 if wi == 0 else 320 + 512 * (wi - 1)
            wd = 320 if wi == 0 else 512
            ent = [p for p in plan if p[0] <= c0 < p[1]][0]
            _, lo_, mode, func, ranges, dname = ent
            wt = wp.next()
            k.dma(k.pool, wt[:, :, 0:wd], T.w_in[l][:, c0:c0 + wd].rearrange("(k p) c -> p k c", p=128), wt,
                  writes=[wt])
            if mode == "FM":
                for s0 in range(0, wd, 128):
                    ncp = min(128, wd - s0)
                    ot = otp.next()
                    for (t0, n) in split_groups(ranges):
                        ps = psr.next()
                        k.mm([(ps.t[0:ncp, 0:n], wt[:, kk, s0:s0 + ncp], UT[:, kk, t0:t0 + n], kk == 0, kk == KC - 1)
                              for kk in range(KC)], reads=[wt, UT], writes=[ps])
                        if func == "copy":
                            k.op(k.dve, lambda e: e.tensor_copy(out=ot[0:ncp, t0:t0 + n], in_=ps.t[0:ncp, 0:n]),
                                 reads=[ps], writes=[ot])
                        else:
                            f = AF.Silu if func == "silu" else AF.Sigmoid
                            k.op(k.act, lambda e: e.activation(out=ot[0:ncp, t0:t0 + n], in_=ps.t[0:ncp, 0:n], func=f),
                                 reads=[ps], writes=[ot])
                    for (a, b) in ranges:
                        k.dma(k.sp, T.HT.t[c0 + s0:c0 + s0 + ncp, a:b], ot[0:ncp, a:b], ot, reads=[ot], writes=[T.HT])
            else:
                dest = T.VNA if dname == "VNA" else T.VF
                cc0 = c0 - ent[0]
                for (a, b) in ranges:
                    for tc in range(a, b, 128):
                        ps = psr.next()
                        k.mm([(ps.t[:, 0:512], UT[:, kk, tc:tc + 128], wt[:, kk, :], kk == 0, kk == KC - 1)
                              for kk in range(KC)], reads=[wt, UT], writes=[ps])
                        vt = vtp.next()
                        k.op(k.dve, lambda e: e.tensor_copy(out=vt[:], in_=ps.t[:]), reads=[ps], writes=[vt])
                        k.dma(k.sp, dest.t[tc:tc + 128, cc0:cc0 + 512], vt[:], vt, reads=[vt], writes=[dest])


_CONST_CACHE = {}


def host_consts(hf):
    if hf in _CONST_CACHE:
        return _CONST_CACHE[hf]
    bf = ml_dtypes.bfloat16
    c = {}
    c["ident"] = np.eye(128, dtype=np.float32)
    perm = np.zeros((64, 64), np.float32)
    for dp in range(64):
        sw = dp + 16 if (dp % 32) // 16 == 0 else dp - 16
        perm[sw, dp] = 1.0
    c["perm"] = perm
    i = np.arange(NLAT)
    t = (i + 1024 * hf) % NLAT
    pos = np.stack([t // GRID_W, t % GRID_W]).astype(np.float32)
    nf = 16
    inv = (np.float32(10000.0) ** (-np.arange(nf, dtype=np.float32) / np.float32(nf))).astype(np.float32)
    rope = np.zeros((64, 2, NLAT), np.float32)
    for d in range(64):
        half, ab, f = d // 32, (d % 32) // 16, d % 16
        ang = (pos[half] * inv[f]).astype(np.float32)
        rope[d, 0] = np.cos(ang)
        rope[d, 1] = -np.sin(ang) if ab == 0 else np.sin(ang)
    c["rope"] = rope
    m = np.full((128, NA_NM, 128), NEG / NA_SCALE, np.float32)
    kk = np.arange(128)
    for (qh, u, jr), mi in NA_MIDX.items():
        j = 8 * qh + jr
        d = u - jr
        G = (j + 8 * hf) % 16
        Gk = G + d
        if not (0 <= Gk <= 15):
            continue
        kr = 2 * Gk + kk // 64
        kc = kk % 64
        r = 2 * G + kk // 64
        cc = kk % 64
        rs = np.clip(r - 4, 0, 24)
        cs = np.clip(cc - 8, 0, 48)
        valid = ((kr[:, None] >= rs[None, :]) & (kr[:, None] < rs[None, :] + 8) &
                 (kc[:, None] >= cs[None, :]) & (kc[:, None] < cs[None, :] + 16))
        m[:, mi, :] = np.where(valid, 0.0, NEG / NA_SCALE)
    c["namask"] = m.astype(bf)
    cm = np.ones((128, 2, NLAT), np.float32)
    cm[:, 0, t == 0] = 0.0
    cm[:, 1, t == NLAT - 1] = 0.0
    c["cmask"] = cm
    prod = (t[:, None].astype(np.int64) * t[None, :].astype(np.int64)) % NLAT
    ang = prod.astype(np.float64) * (2.0 * np.pi / NLAT)
    c["dftN"] = np.stack([np.cos(ang), np.sin(ang)]).astype(np.float32) / np.float32(np.sqrt(NLAT))
    c["dftN"] = c["dftN"].astype(bf)
    n = np.arange(256)
    ang = ((n[:, None] * n[None, :]) % 256).astype(np.float64) * (2.0 * np.pi / 256)
    c["dftC"] = (np.stack([np.cos(ang), np.sin(ang)]) / 16.0).astype(np.float32).astype(bf)
    c["dftM"] = (np.stack([np.cos(ang), -np.sin(ang)]) / 16.0).astype(np.float32).astype(bf)
    _CONST_CACHE[hf] = c
    return c


def brel_table(na_rpb):
    kk = np.arange(128)
    out = np.zeros((2, 16, 128, 7, 128), np.float32)
    for si in range(7):
        d = 3 - si
        dr = 2 * d + (kk // 64)[:, None] - (kk // 64)[None, :]
        dc = (kk % 64)[:, None] - (kk % 64)[None, :]
        ok = (np.abs(dr) <= 7) & (np.abs(dc) <= 15)
        ri = np.clip(dr + 7, 0, 14)
        ci = np.clip(dc + 15, 0, 30)
        g = na_rpb[:, :, ri, ci]
        out[:, :, :, si, :] = np.where(ok[None, None], g, np.float32(0.0))
    return out


def prep_inputs(inp, cores=range(8)):
    f = np.float32

    def pk(a, k):
        return np.ascontiguousarray(np.moveaxis(a.reshape(a.shape[:-1] + (k, 128)), -1, 0))

    shared = {
        "w_ada": np.ascontiguousarray(inp["w_ada"], f),
        "b_ada": np.ascontiguousarray(np.stack([inp["b_ada"], inp["b_ada"]], axis=1), f),
        "w_in": np.ascontiguousarray(inp["w_in"], f),
        "gpre": pk(inp["g_pre"], 16), "gpost": pk(inp["g_post"], 16),
        "gq": pk(inp["g_q"], 4), "gkv": pk(inp["g_kv"], 2),
        "w_uq": np.ascontiguousarray(inp["w_uq"], f), "w_ukv": np.ascontiguousarray(inp["w_ukv"], f),
        "convw": np.ascontiguousarray(inp["conv_w"].reshape(2, 3, 8, 128).transpose(3, 0, 2, 1), f),
        "brel": brel_table(np.asarray(inp["na_rpb"], f)),
        "w_p": np.ascontiguousarray(np.stack([inp["w_p_conv"], inp["w_p_mla"], inp["w_p_na"], inp["w_p_fnet"]],
                                             axis=1), f),
        "w_out": np.ascontiguousarray(inp["w_out"], f),
    }
    maps = []
    for cid in cores:
        b, hf = cid // 2, cid % 2
        m = dict(shared)
        m.update(host_consts(hf))
        xl = np.roll(np.asarray(inp["x"][b], f), -1024 * hf, axis=0)
        m["xT"] = np.ascontiguousarray(np.concatenate([xl, np.asarray(inp["ctx"][b], f)], axis=0).T)
        cv = np.stack([np.asarray(inp["c"][b], f), np.asarray(inp["c_ctx"], f)], axis=-1)
        m["cvec"] = np.ascontiguousarray(cv.reshape(16, 128, 2).transpose(1, 0, 2))
        maps.append(m)
    return maps


_PROG = {}


def kernel(**inputs):
    maps = prep_inputs(inputs)
    if "full" not in _PROG:
        _PROG["full"] = build_program()
    res = run_bass_kernel_spmd(_PROG["full"], maps, core_ids=list(range(8)))
    out = np.zeros((4, NLAT, D), np.float32)
    for cid in range(8):
        b, hf = cid // 2, cid % 2
        out[b, 1024 * hf:1024 * hf + 1024, :] = res.results[cid]["outT"].T
    return out


def stage_conv(k, T, C, l, lat_q, ctxq):
    segs = [(0, lat_q, NLAT - 1, lat_q % NLAT, True)]
    if ctxq:
        segs.append((NLAT, NCTX, None, None, False))
    with k.scope() as es:
        cmask = k.sb(es, "cmask", [128, 2, NLAT], F32)
        k.dma(k.sp, cmask[:], T.cmask[:], cmask, writes=[cmask])
        WM = lat_q
        ccp = k.pool_of(es, "cvc", [128, WM + 2], BF16, 2, part=True)
        cxp = k.pool_of(es, "cvx", [128, WM + 2], BF16, 2, part=True)
        cbp = k.pool_of(es, "cvb", [128, WM], BF16, 2)
        sgp = k.pool_of(es, "cvg", [128, WM], BF16, 2)
        zpp = k.pool_of(es, "cvz", [128, WM + 2], F32, 2, part=True)
        t1p = k.pool_of(es, "cvt", [128, WM], F32, 2)
        acp = k.pool_of(es, "cva", [128, WM], F32, 2)
        ogp = k.pool_of(es, "cvo", [128, WM], BF16, 2)
        for (t0, W, hl, hr, masked) in segs:
            for kc in range(8):
                r0 = kc * 128
                cc, cx, cb, sg = ccp.next(), cxp.next(), cbp.next(), sgp.next()
                for (tile, col) in ((cc, C_CC), (cx, C_CX)):
                    k.dma(k.sp, tile[:, 1:W + 1], T.HT.t[col + r0:col + r0 + 128, t0:t0 + W], tile,
                          reads=[T.HT], writes=[tile])
                    if masked:
                        k.dma(k.sp, tile[:, 0:1], T.HT.t[col + r0:col + r0 + 128, hl:hl + 1], tile,
                              reads=[T.HT], writes=[tile], slow=True)
                        k.dma(k.sp, tile[:, W + 1:W + 2], T.HT.t[col + r0:col + r0 + 128, hr:hr + 1], tile,
                              reads=[T.HT], writes=[tile], slow=True)
                k.dma(k.sp, cb[:, 0:W], T.HT.t[C_CB + r0:C_CB + r0 + 128, t0:t0 + W], cb, reads=[T.HT], writes=[cb])
                k.dma(k.sp, sg[:, 0:W], T.HT.t[C_GCV + r0:C_GCV + r0 + 128, t0:t0 + W], sg, reads=[T.HT], writes=[sg])
                zp = zpp.next()
                if masked:
                    k.op(k.dve, lambda e: e.tensor_tensor(out=zp[:, 0:W + 2], in0=cc[:, 0:W + 2], in1=cx[:, 0:W + 2],
                                                          op=ALU.mult), reads=[cc, cx], writes=[zp])
                else:
                    k.op(k.dve, lambda e: e.memset(zp[:, 0:1], 0.0), writes=[zp])
                    k.op(k.dve, lambda e: e.memset(zp[:, W + 1:W + 2], 0.0), writes=[zp])
                    k.op(k.dve, lambda e: e.tensor_tensor(out=zp[:, 1:W + 1], in0=cc[:, 1:W + 1], in1=cx[:, 1:W + 1],
                                                          op=ALU.mult), reads=[cc, cx], writes=[zp])
                acc = acp.next()
                k.op(k.dve, lambda e: e.tensor_scalar_mul(out=acc[:, 0:W], in0=zp[:, 1:W + 1],
                                                          scalar1=C.convw[:, l, kc, 1:2]),
                     reads=[zp, C.convw], writes=[acc])
                for (off, wi, mi) in ((0, 0, 0), (2, 2, 1)):
                    if masked:
                        t1 = t1p.next()
                        k.op(k.pool, lambda e: e.tensor_tensor(out=t1[:, 0:W], in0=zp[:, off:off + W],
                                                               in1=cmask[:, mi, 0:W], op=ALU.mult),
                             reads=[zp, cmask], writes=[t1])
                        srcb, srca = t1, t1[:, 0:W]
                    else:
                        srcb, srca = zp, zp[:, off:off + W]
                    k.op(k.dve, lambda e: e.scalar_tensor_tensor(out=acc[:, 0:W], in0=srca,
                                                                 scalar=C.convw[:, l, kc, wi:wi + 1],
                                                                 in1=acc[:, 0:W], op0=ALU.mult, op1=ALU.add),
                         reads=[srcb, acc, C.convw], writes=[acc])
                k.op(k.dve, lambda e: e.tensor_tensor(out=acc[:, 0:W], in0=acc[:, 0:W], in1=cb[:, 0:W], op=ALU.mult),
                     reads=[acc, cb], writes=[acc])
                og = ogp.next()
                k.op(k.pool, lambda e: e.tensor_tensor(out=og[:, 0:W], in0=acc[:, 0:W], in1=sg[:, 0:W], op=ALU.mult),
                     reads=[acc, sg], writes=[og])
                k.dma(k.sp, T.OG.t[0, r0:r0 + 128, t0:t0 + W], og[:, 0:W], og, reads=[og], writes=[T.OG])


def stage_mla(k, T, C, l, lat_q, ctxq):
    NQ = lat_q + (NCTX if ctxq else 0)
    g_all = split_groups([(0, NT)])
    g_q = split_groups([(0, NQ)])
    with k.scope() as es:
        rope = k.sb(es, "rope", [64, 2, NLAT], F32)
        k.dma(k.sp, rope[:], T.rope[:], rope, writes=[rope])
        ckv = k.sb(es, "ckv", [128, 2, NT], BF16)
        ckg = k.sb(es, "ckg", [128, 2, NT], BF16, part=True)
        kr = k.sb(es, "kr", [64, NT], BF16)
        krr = k.sb(es, "krr", [64, NT], BF16, part=True)
        rkv = k.sb(es, "rkv", [128, NT], F32, part=True)
        rkvT = k.sb(es, "rkvT", [128, NT // 128], F32, part=True)
        cq = k.sb(es, "cq", [128, 4, NQ], BF16)
        cqg = k.sb(es, "cqg", [128, 4, NQ], BF16, part=True)
        rq = k.sb(es, "rq", [128, NQ], F32, part=True)
        wukv = k.sb(es, "wukv", [128, 2, 2048], BF16)
        wuq = k.sb(es, "wuq", [128, 4, 1536], BF16)
        k.dma(k.pool, wukv[:], T.w_ukv[l].rearrange("(k p) c -> p k c", p=128), wukv, writes=[wukv])
        k.dma(k.pool, wuq[:], T.w_uq[l].rearrange("(k p) c -> p k c", p=128), wuq, writes=[wuq])
        k.dma(k.sp, ckv[:], T.HT.t[0:256, :].rearrange("(k p) t -> p k t", p=128), ckv, reads=[T.HT], writes=[ckv])
        k.dma(k.sp, kr[:], T.HT.t[256:320, :], kr, reads=[T.HT], writes=[kr])
        k.dma(k.sp, cq[:], T.HT.t[C_CQ:C_CQ + 512, 0:NQ].rearrange("(k p) t -> p k t", p=128), cq,
              reads=[T.HT], writes=[cq])
        sqp = k.pool_of(es, "sqm", [128, 4, 512], F32, 2)
        tmp = k.pool_of(es, "tmm", [128, 512], F32, 6)
        psr = Ring(k.psum[5:8])
        for (t0, n) in g_all:
            sq = sqp.next()
            k.op(k.act, lambda e: e.activation(out=sq[:, 0:2, 0:n], in_=ckv[:, :, t0:t0 + n], func=AF.Square),
                 reads=[ckv], writes=[sq])
            ps = psr.next()
            k.mm([(ps.t[:, 0:n], C.ones_f[:], sq[:, kk, 0:n], kk == 0, kk == 1) for kk in range(2)],
                 reads=[sq, C.ones_f], writes=[ps])
            rstd_op(k, rkv, rkv[:, t0:t0 + n], ps, ps.t[:, 0:n], 256)
            ps2 = psr.next()
            nch = n // 128
            mms = []
            for ci in range(nch):
                for kk in range(2):
                    mms.append((ps2.t[:, ci:ci + 1], sq[:, kk, ci * 128:(ci + 1) * 128], C.ones_f[:, 0:1],
                                kk == 0, kk == 1))
            k.mm(mms, reads=[sq, C.ones_f], writes=[ps2])
            rstd_op(k, rkvT, rkvT[:, t0 // 128:t0 // 128 + nch], ps2, ps2.t[:, 0:nch], 256)
        for kk in range(2):
            k.op(k.dve, lambda e: e.tensor_scalar_mul(out=ckg[:, kk, :], in0=ckv[:, kk, :],
                                                      scalar1=C.gkv[:, l, kk:kk + 1]), reads=[ckv, C.gkv], writes=[ckg])
        for (t0, n) in split_groups([(0, NLAT)]):
            ps = psr.next()
            k.mm([(ps.t[0:64, 0:n], C.perm_b[:], kr[:, t0:t0 + n], True, True)], reads=[C.perm_b, kr], writes=[ps])
            t1, t2 = tmp.next(), tmp.next()
            k.op(k.dve, lambda e: e.tensor_tensor(out=t1[0:64, 0:n], in0=kr[:, t0:t0 + n], in1=rope[:, 0, t0:t0 + n],
                                                  op=ALU.mult), reads=[kr, rope], writes=[t1])
            k.op(k.dve, lambda e: e.tensor_tensor(out=t2[0:64, 0:n], in0=ps.t[0:64, 0:n], in1=rope[:, 1, t0:t0 + n],
                                                  op=ALU.mult), reads=[ps, rope], writes=[t2])
            k.op(k.pool, lambda e: e.tensor_tensor(out=krr[:, t0:t0 + n], in0=t1[0:64, 0:n], in1=t2[0:64, 0:n],
                                                   op=ALU.add), reads=[t1, t2], writes=[krr])
        k.op(k.dve, lambda e: e.tensor_copy(out=krr[:, NLAT:NT], in_=kr[:, NLAT:NT]), reads=[kr], writes=[krr])
        for (t0, n) in g_q:
            sq = sqp.next()
            k.op(k.act, lambda e: e.activation(out=sq[:, :, 0:n], in_=cq[:, :, t0:t0 + n], func=AF.Square),
                 reads=[cq], writes=[sq])
            ps = psr.next()
            k.mm([(ps.t[:, 0:n], C.ones_f[:], sq[:, kk, 0:n], kk == 0, kk == 3) for kk in range(4)],
                 reads=[sq, C.ones_f], writes=[ps])
            rstd_op(k, rq, rq[:, t0:t0 + n], ps, ps.t[:, 0:n], 512)
        for kk in range(4):
            k.op(k.dve, lambda e: e.tensor_scalar_mul(out=cqg[:, kk, :], in0=cq[:, kk, :],
                                                      scalar1=C.gq[:, l, kk:kk + 1]), reads=[cq, C.gq], writes=[cqg])
        khp = k.pool_of(es, "kh", [128, NT], BF16, 2, part=True)
        vhp = k.pool_of(es, "vh", [128, NT // 128, 128], BF16, 2, part=True)
        qnp = k.pool_of(es, "qn", [128, NQ], BF16, 2, part=True)
        qrp = k.pool_of(es, "qr", [64, NQ], BF16, 2, part=True)
        ptp = k.pool_of(es, "pt", [128, 512], BF16, 3)
        sgp = k.pool_of(es, "sgm", [128, 512], BF16, 2)
        ogp = k.pool_of(es, "ogm", [128, 512], BF16, 2)
        sr = Ring(k.psum[0:3])
        po, pd = k.psum[3], k.psum[4]
        for h in range(8):
            kh, vh, qn, qr = khp.next(), vhp.next(), qnp.next(), qrp.next()
            for (t0, n) in g_all:
                ps = psr.next()
                k.mm([(ps.t[:, 0:n], wukv[:, kk, h * 256:h * 256 + 128], ckg[:, kk, t0:t0 + n], kk == 0, kk == 1)
                      for kk in range(2)], reads=[wukv, ckg], writes=[ps])
                k.op(k.dve, lambda e: e.tensor_tensor(out=kh[:, t0:t0 + n], in0=ps.t[:, 0:n], in1=rkv[:, t0:t0 + n],
                                                      op=ALU.mult), reads=[ps, rkv], writes=[kh])
            for c4 in range(0, NT // 128, 4):
                nch = min(4, NT // 128 - c4)
                ps = psr.next()
                mms = []
                for ci in range(nch):
                    for kk in range(2):
                        mms.append((ps.t[:, ci * 128:(ci + 1) * 128], ckg[:, kk, (c4 + ci) * 128:(c4 + ci + 1) * 128],
                                    wukv[:, kk, h * 256 + 128:h * 256 + 256], kk == 0, kk == 1))
                k.mm(mms, reads=[wukv, ckg], writes=[ps])
                for ci in range(nch):
                    k.op(k.act, lambda e: e.activation(out=vh[:, c4 + ci, :], in_=ps.t[:, ci * 128:(ci + 1) * 128],
                                                       func=AF.Copy, scale=rkvT[:, c4 + ci:c4 + ci + 1]),
                         reads=[ps, rkvT], writes=[vh])
            for (t0, n) in g_q:
                ps = psr.next()
                k.mm([(ps.t[:, 0:n], wuq[:, kk, h * 192:h * 192 + 128], cqg[:, kk, t0:t0 + n], kk == 0, kk == 3)
                      for kk in range(4)], reads=[wuq, cqg], writes=[ps])
                k.op(k.dve, lambda e: e.tensor_tensor(out=qn[:, t0:t0 + n], in0=ps.t[:, 0:n], in1=rq[:, t0:t0 + n],
                                                      op=ALU.mult), reads=[ps, rq], writes=[qn])
                ps2 = psr.next()
                k.mm([(ps2.t[0:64, 0:n], wuq[:, kk, h * 192 + 128:h * 192 + 192], cqg[:, kk, t0:t0 + n],
                       kk == 0, kk == 3) for kk in range(4)], reads=[wuq, cqg], writes=[ps2])
                if t0 < lat_q:
                    qf = tmp.next()
                    k.op(k.dve, lambda e: e.tensor_tensor(out=qf[0:64, 0:n], in0=ps2.t[0:64, 0:n],
                                                          in1=rq[0:64, t0:t0 + n], op=ALU.mult),
                         reads=[ps2, rq], writes=[qf])
                    ps3 = psr.next()
                    k.mm([(ps3.t[0:64, 0:n], C.perm_f[:], qf[0:64, 0:n], True, True)], reads=[C.perm_f, qf],
                         writes=[ps3])
                    t1, t2 = tmp.next(), tmp.next()
                    k.op(k.pool, lambda e: e.tensor_tensor(out=t1[0:64, 0:n], in0=qf[0:64, 0:n],
                                                           in1=rope[:, 0, t0:t0 + n], op=ALU.mult),
                         reads=[qf, rope], writes=[t1])
                    k.op(k.dve, lambda e: e.tensor_tensor(out=t2[0:64, 0:n], in0=ps3.t[0:64, 0:n],
                                                          in1=rope[:, 1, t0:t0 + n], op=ALU.mult),
                         reads=[ps3, rope], writes=[t2])
                    k.op(k.pool, lambda e: e.tensor_tensor(out=qr[:, t0:t0 + n], in0=t1[0:64, 0:n], in1=t2[0:64, 0:n],
                                                           op=ALU.add), reads=[t1, t2], writes=[qr])
                else:
                    k.op(k.dve, lambda e: e.tensor_tensor(out=qr[:, t0:t0 + n], in0=ps2.t[0:64, 0:n],
                                                          in1=rq[0:64, t0:t0 + n], op=ALU.mult),
                         reads=[ps2, rq], writes=[qr])
            qblocks = [(q0, n, list(range(NT // 128))) for (q0, n) in split_groups([(0, lat_q)])]
            if ctxq:
                qblocks.append((NLAT, NCTX, [16, 17]))
            for (q0, nq, chunks) in qblocks:
                sg = sgp.next()
                k.dma(k.sp, sg[:, 0:nq], T.HT.t[C_GML + h * 128:C_GML + (h + 1) * 128, q0:q0 + nq], sg,
                      reads=[T.HT], writes=[sg])

                def qk(c):
                    ps = sr.next()
                    k.mm([(ps.t[:, 0:nq], kh[:, c * 128:(c + 1) * 128], qn[:, q0:q0 + nq], True, False),
                          (ps.t[:, 0:nq], krr[:, c * 128:(c + 1) * 128], qr[:, q0:q0 + nq], False, True)],
                         reads=[kh, krr, qn, qr], writes=[ps])
                    return ps
                cur = qk(chunks[0])
                for i, c in enumerate(chunks):
                    nxt = qk(chunks[i + 1]) if i + 1 < len(chunks) else None
                    pt = ptp.next()
                    k.op(k.act, lambda e: e.activation(out=pt[:, 0:nq], in_=cur.t[:, 0:nq], func=AF.Exp,
                                                       scale=MLA_SCALE), reads=[cur], writes=[pt])
                    last = (i == len(chunks) - 1)
                    k.mm([(po.t[:, 0:nq], vh[:, c, :], pt[:, 0:nq], i == 0, last),
                          (pd.t[:, 0:nq], C.ones_b[:], pt[:, 0:nq], i == 0, last)],
                         reads=[vh, pt, C.ones_b], writes=[po, pd])
                    cur = nxt
                rd, o = tmp.next(), tmp.next()
                k.op(k.act, lambda e: e.activation(out=rd[:, 0:nq], in_=pd.t[:, 0:nq], func=AF.Copy),
                     reads=[pd], writes=[rd])
                k.op(k.dve, lambda e: e.tensor_copy(out=o[:, 0:nq], in_=po.t[:, 0:nq]), reads=[po], writes=[o])
                k.op(k.dve, lambda e: e.reciprocal(out=rd[:, 0:nq], in_=rd[:, 0:nq]), reads=[rd], writes=[rd])
                k.op(k.dve, lambda e: e.tensor_tensor(out=o[:, 0:nq], in0=o[:, 0:nq], in1=rd[:, 0:nq], op=ALU.mult),
                     reads=[o, rd], writes=[o])
                og = ogp.next()
                k.op(k.pool, lambda e: e.tensor_tensor(out=og[:, 0:nq], in0=o[:, 0:nq], in1=sg[:, 0:nq], op=ALU.mult),
                     reads=[o, sg], writes=[og])
                k.dma(k.sp, T.OG.t[1, h * 128:(h + 1) * 128, q0:q0 + nq], og[:, 0:nq], og, reads=[og], writes=[T.OG])


def stage_na(k, T, C, l, lat_q, ctxq):
    NQ = lat_q + (NCTX if ctxq else 0)
    nhalf = lat_q // 1024
    with k.scope() as es:
        namask = k.sb(es, "namask", [128, NA_NM, 128], BF16)
        k.dma(k.sp, namask[:], T.namask[:], namask, writes=[namask])
        khp = k.pool_of(es, "nk", [64, NT], BF16, 2)
        qhp = k.pool_of(es, "nq", [64, NQ], BF16, 2)
        sgp = k.pool_of(es, "ng", [64, NQ], BF16, 2)
        vpp = k.pool_of(es, "nv", [128, NT // 128, 128], BF16, 2)
        brp = k.pool_of(es, "nb", [128, 7, 128], F32, 2)
        tbp = k.pool_of(es, "ntb", [128, NA_NM, 128], BF16, 2, part=True)
        ptp = k.pool_of(es, "np", [128, 6, 128], BF16, 3, part=True)
        pcp = k.pool_of(es, "npc", [128, 512], BF16, 3)
        ogp = k.pool_of(es, "no", [64, NQ], BF16, 2, part=True)
        rdp = k.pool_of(es, "nr", [64, 512], F32, 4)
        otp = k.pool_of(es, "nt", [64, 512], F32, 4)
        ssets = [(k.psum[0], k.psum[1]), (k.psum[2], k.psum[3])]
        si = 0
        ob = (k.psum[4], k.psum[5])
        db = (k.psum[6], k.psum[7])
        vp = None
        for h in range(16):
            hh = h % 2
            if hh == 0:
                vp = vpp.next()
                k.dma(k.sp, vp[:], T.VNA.t[:, h * 64:h * 64 + 128].rearrange("(c p) f -> p c f", p=128), vp,
                      reads=[T.VNA], writes=[vp])
            kh, qh, sg, br = khp.next(), qhp.next(), sgp.next(), brp.next()
            k.dma(k.sp, kh[:], T.HT.t[C_NAK + h * 64:C_NAK + (h + 1) * 64, :], kh, reads=[T.HT], writes=[kh])
            k.dma(k.sp, qh[:], T.HT.t[C_NAQ + h * 64:C_NAQ + (h + 1) * 64, 0:NQ], qh, reads=[T.HT], writes=[qh])
            k.dma(k.sp, sg[:], T.HT.t[C_GNA + h * 64:C_GNA + (h + 1) * 64, 0:NQ], sg, reads=[T.HT], writes=[sg])
            k.dma(k.sp, br[:], T.brel[l, h], br, writes=[br])
            og = ogp.next()
            vcols = slice(hh * 64, (hh + 1) * 64)
            tb = tbp.next()
            for qh_ in range(nhalf):
                for u in range(-2, 10):
                    prs = na_pairs(u)
                    m0 = NA_MIDX[(qh_, u, prs[0])]
                    s0 = 3 - u + prs[0]
                    k.op(k.dve, lambda e: e.scalar_tensor_tensor(
                        out=tb[:, m0:m0 + len(prs), :], in0=br[:, s0:s0 + len(prs), :], scalar=1.0 / NA_SCALE,
                        in1=namask[:, m0:m0 + len(prs), :], op0=ALU.mult, op1=ALU.add),
                        reads=[br, namask], writes=[tb])
            passes = [(hf_, 1024 * hf_, 1024) for hf_ in range(nhalf)]
            if ctxq:
                passes.append((None, NLAT, NCTX))
            for (qhalf, QB, nqp) in passes:
                qgroups = [(g0, min(512, nqp - g0)) for g0 in range(0, nqp, 512)]
                steps = []
                for cc in (16, 17):
                    for gi, (g0, gn) in enumerate(qgroups):
                        def mk_ctx(cc=cc, gi=gi, g0=g0, gn=gn):
                            st = {}

                            def qk():
                                nonlocal si
                                st["sA"] = ssets[si % 2][0]
                                si += 1
                                k.mm([(st["sA"].t[:, 0:gn], kh[:, cc * 128:(cc + 1) * 128],
                                       qh[:, QB + g0:QB + g0 + gn], True, True)], reads=[kh, qh], writes=[st["sA"]])

                            def sm():
                                st["pc"] = pcp.next()
                                k.op(k.act, lambda e: e.activation(out=st["pc"][:, 0:gn], in_=st["sA"].t[:, 0:gn],
                                                                   func=AF.Exp, scale=NA_SCALE),
                                     reads=[st["sA"]], writes=[st["pc"]])

                            def pv():
                                last = (cc == 17 and qhalf is None)
                                k.mm([(ob[gi].t[0:64, 0:gn], vp[:, cc, vcols], st["pc"][:, 0:gn], cc == 16, last, 1),
                                      (db[gi].t[0:64, 0:gn], C.ones_b[:, 0:64], st["pc"][:, 0:gn], cc == 16, last, 1)],
                                     reads=[vp, st["pc"], C.ones_b], writes=[ob[gi], db[gi]])
                            return (qk, sm, pv)
                        steps.append(mk_ctx())
                if qhalf is not None:
                    for u in range(-2, 10):
                        def mk_lat(u=u):
                            prs = na_pairs(u)
                            npr = len(prs)
                            jlo = prs[0]
                            c = (8 * qhalf + u) % 16
                            m0 = NA_MIDX[(qhalf, u, jlo)]
                            s0 = 3 - u + jlo
                            nA = min(npr, 4)
                            nB = npr - nA
                            q0 = QB + 128 * jlo
                            st = {}

                            def qk():
                                nonlocal si
                                sA, sB = ssets[si % 2]
                                si += 1
                                st["sA"], st["sB"] = sA, sB
                                mms = [(sA.t[:, 0:128 * nA], kh[:, c * 128:(c + 1) * 128], qh[:, q0:q0 + 128 * nA],
                                        True, False),
                                       (sA.t[:, 0:128 * nA], C.ident_b[:],
                                        tb[:, m0:m0 + nA, :].rearrange("p s q -> p (s q)"), False, True)]
                                wr = [sA]
                                if nB:
                                    mms += [(sB.t[:, 0:128 * nB], kh[:, c * 128:(c + 1) * 128],
                                             qh[:, q0 + 512:q0 + 512 + 128 * nB], True, False),
                                            (sB.t[:, 0:128 * nB], C.ident_b[:],
                                             tb[:, m0 + 4:m0 + 4 + nB, :].rearrange("p s q -> p (s q)"), False, True)]
                                    wr.append(sB)
                                k.mm(mms, reads=[kh, qh, C.ident_b, tb], writes=wr)

                            def sm():
                                sA, sB = st["sA"], st["sB"]
                                st["pt"] = ptp.next()
                                k.op(k.act, lambda e: e.activation(
                                    out=st["pt"][:, 0:nA, :], in_=sA.t[:, 0:128 * nA].rearrange("p (s q) -> p s q", q=128),
                                    func=AF.Exp, scale=NA_SCALE), reads=[sA], writes=[st["pt"]])
                                if nB:
                                    k.op(k.act, lambda e: e.activation(
                                        out=st["pt"][:, 4:4 + nB, :],
                                        in_=sB.t[:, 0:128 * nB].rearrange("p (s q) -> p s q", q=128),
                                        func=AF.Exp, scale=NA_SCALE), reads=[sB], writes=[st["pt"]])

                            def pv():
                                pt = st["pt"]
                                mms = []
                                wr = []
                                for bi in range(2):
                                    a_ = max(jlo, 4 * bi)
                                    b_ = min(prs[-1], 4 * bi + 3)
                                    if a_ > b_:
                                        continue
                                    cols = slice(128 * (a_ - 4 * bi), 128 * (b_ + 1 - 4 * bi))
                                    rhs = pt[:, a_ - jlo:b_ + 1 - jlo, :].rearrange("p s q -> p (s q)")
                                    fin = (u == 9)
                                    mms.append((ob[bi].t[0:64, cols], vp[:, c, vcols], rhs, False, fin, 1))
                                    mms.append((db[bi].t[0:64, cols], C.ones_b[:, 0:64], rhs, False, fin, 1))
                                    wr += [ob[bi], db[bi]]
                                k.mm(mms, reads=[vp, pt, C.ones_b], writes=wr)
                            return (qk, sm, pv)
                        steps.append(mk_lat())
                steps[0][0]()
                for i_, (qk_, sm_, pv_) in enumerate(steps):
                    if i_ + 1 < len(steps):
                        steps[i_ + 1][0]()
                    sm_()
                    pv_()
                for gi, (g0, gn) in enumerate(qgroups):
                    rd, o = rdp.next(), otp.next()
                    k.op(k.dve, lambda e: e.tensor_copy(out=rd[:, 0:gn], in_=db[gi].t[0:64, 0:gn]),
                         reads=[db[gi]], writes=[rd])
                    k.op(k.dve, lambda e: e.tensor_copy(out=o[:, 0:gn], in_=ob[gi].t[0:64, 0:gn]),
                         reads=[ob[gi]], writes=[o])
                    k.op(k.dve, lambda e: e.reciprocal(out=rd[:, 0:gn], in_=rd[:, 0:gn]), reads=[rd], writes=[rd])
                    k.op(k.dve, lambda e: e.tensor_tensor(out=o[:, 0:gn], in0=o[:, 0:gn], in1=rd[:, 0:gn],
                                                          op=ALU.mult), reads=[o, rd], writes=[o])
                    k.op(k.pool, lambda e: e.tensor_tensor(out=og[:, QB + g0:QB + g0 + gn], in0=o[:, 0:gn],
                                                           in1=sg[:, QB + g0:QB + g0 + gn], op=ALU.mult),
                         reads=[o, sg], writes=[og])
            k.dma(k.sp, T.OG.t[2, h * 64:(h + 1) * 64, 0:NQ], og[:], og, reads=[og], writes=[T.OG])


def stage_fourier(k, T, C, l, lat_q, ctxq):
    with k.scope() as es:
        vf = k.sb(es, "vf", [128, 16, 1024], BF16)
        k.dma(k.sp, vf[:], T.VF.t[0:NLAT, :].rearrange("(c p) f -> p c f", p=128), vf, reads=[T.VF], writes=[vf])
        jobs = [("lat", vf, 16, T.dftN, k0, 512, k0) for k0 in range(0, lat_q, 512)]
        if ctxq:
            vfc = k.sb(es, "vfc", [128, 2, 1024], BF16)
            k.dma(k.sp, vfc[:], T.VF.t[NLAT:NT, :].rearrange("(c p) f -> p c f", p=128), vfc,
                  reads=[T.VF], writes=[vfc])
            jobs.append(("ctx", vfc, 2, T.dftC, 0, 256, NLAT))
        cnp = k.pool_of(es, "fcn", [128, 16, 512], BF16, 2)
        snp = k.pool_of(es, "fsn", [128, 16, 512], BF16, 2)
        z1p = k.pool_of(es, "fz1", [128, 8, 512], BF16, 2, part=True)
        z2p = k.pool_of(es, "fz2", [128, 8, 512], BF16, 2, part=True)
        sgp = k.pool_of(es, "fsg", [128, 8, 512], BF16, 2)
        ogp = k.pool_of(es, "fog", [128, 8, 512], BF16, 2, part=True)
        zr = Ring(k.psum[0:4])
        orr = Ring(k.psum[4:8])
        for (kind, vt, nch, tab, k0, nk, tok0) in jobs:
            cn, sn, sg = cnp.next(), snp.next(), sgp.next()
            k.dma(k.sp, cn[:, 0:nch, 0:nk], tab.t[0][:, k0:k0 + nk].rearrange("(c p) k -> p c k", p=128), cn,
                  writes=[cn])
            k.dma(k.sp, sn[:, 0:nch, 0:nk], tab.t[1][:, k0:k0 + nk].rearrange("(c p) k -> p c k", p=128), sn,
                  writes=[sn])
            k.dma(k.sp, sg[:, :, 0:nk], T.HT.t[C_GFN:C_GFN + 1024, tok0:tok0 + nk].rearrange("(c p) t -> p c t", p=128),
                  sg, reads=[T.HT], writes=[sg])
            z1, z2 = z1p.next(), z2p.next()
            for cc in range(8):
                ps = zr.next()
                k.mm([(ps.t[:, 0:nk], vt[:, n, cc * 128:(cc + 1) * 128], cn[:, n, 0:nk], n == 0, n == nch - 1)
                      for n in range(nch)], reads=[vt, cn], writes=[ps])
                k.op(k.dve, lambda e: e.tensor_copy(out=z1[:, cc, 0:nk], in_=ps.t[:, 0:nk]), reads=[ps], writes=[z1])
                ps2 = zr.next()
                k.mm([(ps2.t[:, 0:nk], vt[:, n, cc * 128:(cc + 1) * 128], sn[:, n, 0:nk], n == 0, n == nch - 1)
                      for n in range(nch)], reads=[vt, sn], writes=[ps2])
                k.op(k.act, lambda e: e.activation(out=z2[:, cc, 0:nk], in_=ps2.t[:, 0:nk], func=AF.Copy),
                     reads=[ps2], writes=[z2])
            og = ogp.next()
            for g in range(4):
                for h2 in range(2):
                    oc = 2 * g + h2
                    ps = orr.next()
                    mms = [(ps.t[:, 0:nk], C.dftM[:, 0, ci, h2 * 128:(h2 + 1) * 128], z1[:, 2 * g + ci, 0:nk],
                            ci == 0, False) for ci in range(2)]
                    mms += [(ps.t[:, 0:nk], C.dftM[:, 1, ci, h2 * 128:(h2 + 1) * 128], z2[:, 2 * g + ci, 0:nk],
                             False, ci == 1) for ci in range(2)]
                    k.mm(mms, reads=[C.dftM, z1, z2], writes=[ps])
                    k.op(k.dve, lambda e: e.tensor_tensor(out=og[:, oc, 0:nk], in0=ps.t[:, 0:nk], in1=sg[:, oc, 0:nk],
                                                          op=ALU.mult), reads=[ps, sg], writes=[og])
            k.dma(k.sp, T.OG.t[3, :, tok0:tok0 + nk].rearrange("(c p) t -> p c t", p=128), og[:, :, 0:nk], og,
                  reads=[og], writes=[T.OG])


def stage_epilogue(k, T, C, l, lat_q, ctxq, src, dst):
    NQ = lat_q + (NCTX if ctxq else 0)
    NB = 1152 if NQ > 1024 else 1024
    for tb0 in range(0, NQ, NB):
        groups = split_groups([(tb0, tb0 + NB)])
        with k.scope() as es1:
            mT = k.sb(es1, "mT", [128, KC, NB], BF16, part=True)
            with k.scope() as es2:
                ogt = k.sb(es2, "ogt", [128, 32, NB], BF16)
                wpp = k.pool_of(es2, "ewp", [128, 8, 512], BF16, 6)
                gtp = k.pool_of(es2, "egt", [128, 4, 512], BF16, 2)
                tmp = k.pool_of(es2, "etm", [128, 512], F32, 8)
                for i in range(4):
                    k.dma(k.sp, ogt[:, i * 8:(i + 1) * 8, :],
                          T.OG.t[i, :, tb0:tb0 + NB].rearrange("(c p) t -> p c t", p=128), ogt,
                          reads=[T.OG], writes=[ogt])
                sets = [k.psum[0:4], k.psum[4:8]]
                si = 0
                for cg in range(4):
                    wps = []
                    for i in range(4):
                        wt = wpp.next()
                        k.dma(k.pool, wt[:], T.w_p[l, i][:, cg * 512:(cg + 1) * 512].rearrange(
                            "(k p) c -> p k c", p=128), wt, writes=[wt])
                        wps.append(wt)
                    for s in range(4):
                        j = cg * 4 + s
                        for (t0, n) in groups:
                            gt = gtp.next()
                            k.dma(k.sp, gt[:, :, 0:n],
                                  T.HT.t[C_MG:C_MG + 4 * D, t0:t0 + n].rearrange("(i c) t -> c i t", i=4)[
                                      j * 128:(j + 1) * 128], gt, reads=[T.HT], writes=[gt])
                            pss = sets[si % 2]
                            si += 1
                            for i in range(4):
                                k.mm([(pss[i].t[:, 0:n], wps[i][:, kc, s * 128:(s + 1) * 128],
                                       ogt[:, i * 8 + kc, t0 - tb0:t0 - tb0 + n], kc == 0, kc == 7)
                                      for kc in range(8)], reads=[wps[i], ogt], writes=[pss[i]])
                            a = [tmp.next() for _ in range(4)]
                            for i in range(4):
                                k.op(k.dve, lambda e: e.tensor_tensor(out=a[i][:, 0:n], in0=pss[i].t[:, 0:n],
                                                                      in1=gt[:, i, 0:n], op=ALU.mult),
                                     reads=[pss[i], gt], writes=[a[i]])
                            k.op(k.pool, lambda e: e.tensor_tensor(out=a[0][:, 0:n], in0=a[0][:, 0:n],
                                                                   in1=a[1][:, 0:n], op=ALU.add),
                                 reads=[a[0], a[1]], writes=[a[0]])
                            k.op(k.pool, lambda e: e.tensor_tensor(out=a[2][:, 0:n], in0=a[2][:, 0:n],
                                                                   in1=a[3][:, 0:n], op=ALU.add),
                                 reads=[a[2], a[3]], writes=[a[2]])
                            k.op(k.pool, lambda e: e.tensor_tensor(out=mT[:, j, t0 - tb0:t0 - tb0 + n],
                                                                   in0=a[0][:, 0:n], in1=a[2][:, 0:n], op=ALU.add),
                                 reads=[a[0], a[2]], writes=[mT])
            yT = k.sb(es1, "yT", [128, KC, NB], F32, part=True)
            psr = Ring(k.psum)
            with k.scope() as es3:
                wop = k.pool_of(es3, "ewo", [128, KC, 512], BF16, 2)
                for cg in range(4):
                    wt = wop.next()
                    k.dma(k.pool, wt[:], T.w_out[l][:, cg * 512:(cg + 1) * 512].rearrange("(k p) c -> p k c", p=128),
                          wt, writes=[wt])
                    for s in range(4):
                        j = cg * 4 + s
                        for (t0, n) in groups:
                            ps = psr.next()
                            k.mm([(ps.t[:, 0:n], wt[:, kc, s * 128:(s + 1) * 128], mT[:, kc, t0 - tb0:t0 - tb0 + n],
                                   kc == 0, kc == KC - 1) for kc in range(KC)], reads=[wt, mT], writes=[ps])
                            k.op(k.act, lambda e: e.activation(out=yT[:, j, t0 - tb0:t0 - tb0 + n], in_=ps.t[:, 0:n],
                                                               func=AF.Copy), reads=[ps], writes=[yT])
            with k.scope() as es4:
                sqp = k.pool_of(es4, "esq", [128, KC, 256], F32, 1)
                xtp = k.pool_of(es4, "ext", [128, KC, 256], F32, 2)
                otp = k.pool_of(es4, "eot", [128, KC, 256], F32, 2)
                rsp = k.pool_of(es4, "ers", [128, 256], F32, 2)
                for (t0, n) in split_groups([(tb0, tb0 + NB)], 256):
                    v = 0 if t0 < NLAT else 1
                    lo = t0 - tb0
                    xt = xtp.next()
                    k.dma(k.sp, xt[:, :, 0:n], src.t[:, t0:t0 + n].rearrange("(k p) t -> p k t", p=128), xt,
                          reads=[src], writes=[xt])
                    sq = sqp.next()
                    k.op(k.act, lambda e: e.activation(out=sq[:, :, 0:n], in_=yT[:, :, lo:lo + n], func=AF.Square),
                         reads=[yT], writes=[sq])
                    ps = psr.next()
                    k.mm([(ps.t[:, 0:n], C.ones_f[:], sq[:, kc, 0:n], kc == 0, kc == KC - 1) for kc in range(KC)],
                         reads=[sq, C.ones_f], writes=[ps])
                    rs = rsp.next()
                    rstd_op(k, rs, rs[:, 0:n], ps, ps.t[:, 0:n], D)
                    ot = otp.next()
                    rb = rs[:, 0:n].unsqueeze(1).to_broadcast([128, KC, n])
                    gb = C.G[:, l, v, :].unsqueeze(2).to_broadcast([128, KC, n])
                    k.op(k.dve, lambda e: e.tensor_tensor(out=ot[:, :, 0:n], in0=yT[:, :, lo:lo + n], in1=rb,
                                                          op=ALU.mult), reads=[yT, rs], writes=[ot])
                    k.op(k.pool, lambda e: e.tensor_tensor(out=ot[:, :, 0:n], in0=ot[:, :, 0:n], in1=gb,
                                                           op=ALU.mult), reads=[ot, C.G], writes=[ot])
                    k.op(k.dve, lambda e: e.tensor_tensor(out=ot[:, :, 0:n], in0=ot[:, :, 0:n], in1=xt[:, :, 0:n],
                                                          op=ALU.add), reads=[ot, xt], writes=[ot])
                    k.dma(k.sp, dst.t[:, t0:t0 + n].rearrange("(k p) t -> p k t", p=128), ot[:, :, 0:n], ot,
                          reads=[ot], writes=[dst])
```
